# Optimizing a Trainium2 kernel written in Bass

```python
import jax, jax.numpy as jnp
from jax import lax
import numpy as np

D_MODEL = 2048
BATCH = 2
SEQ = 4096
DEPTH = 1

MIX_WIDTH = D_MODEL
HGRN_HEADS = 8
HGRN_KDIM = 128
HGRN_VDIM = MIX_WIDTH // 2 // HGRN_HEADS
HGRN_CHUNK = 64
ATTN_HEADS = 8
ATTN_HEAD_DIM = MIX_WIDTH // 2 // ATTN_HEADS
IDX_HEADS = 16
IDX_DIM = 64
DSA_TOPK = 256
DSA_QBLOCK = 64
N_MEM = 256
CROSS_HEADS = 4
CROSS_HEAD_DIM = 128
CROSS_WIDTH = CROSS_HEADS * CROSS_HEAD_DIM
D_FF = 4 * D_MODEL
EPS = 1e-6

HG_K = HGRN_HEADS * HGRN_KDIM
HG_V = HGRN_HEADS * HGRN_VDIM
ATT_W = ATTN_HEADS * ATTN_HEAD_DIM
SPLIT_SIZES = (HG_K, HG_K, HG_V, HG_V, ATT_W, ATT_W, ATT_W, IDX_HEADS * IDX_DIM, IDX_DIM, IDX_HEADS)
SPLIT_POINTS = tuple(int(p) for p in np.cumsum(SPLIT_SIZES)[:-1])
IN_WIDTH = int(sum(SPLIT_SIZES))
OUT_WIDTH = HG_V + ATT_W

kernel_name = "hymba_hgrn2_dsa_hybrid_layer"


def rms_norm(x, gain):
    xf = x.astype(jnp.float32)
    y = xf * lax.rsqrt(jnp.mean(xf * xf, axis=-1, keepdims=True) + EPS)
    return (y * gain.astype(jnp.float32)).astype(x.dtype)


def chunk_gated_recurrence(q, k, v, log_f):
    B, S, H, K = q.shape
    V = v.shape[-1]
    C = HGRN_CHUNK
    n = S // C

    def to_chunks(t):
        return t.reshape(B, n, C, H, t.shape[-1]).transpose(1, 0, 3, 2, 4)

    causal = jnp.tril(jnp.ones((C, C), dtype=bool))[:, :, None]

    def step(state, inp):
        qb, kb, vb, gb = inp
        A = jnp.cumsum(gb, axis=2)
        o_inter = jnp.einsum('bhtk,bhkv->bhtv', qb * jnp.exp(A), state)
        diff = A[:, :, :, None, :] - A[:, :, None, :, :]
        decay = jnp.exp(jnp.where(causal, diff, -jnp.inf))
        scores = jnp.einsum('bhtk,bhtsk,bhsk->bhts', qb, decay, kb)
        o_intra = jnp.einsum('bhts,bhsv->bhtv', scores, vb)
        A_last = A[:, :, -1:, :]
        new_state = state * jnp.exp(A_last[:, :, 0, :])[..., None] + jnp.einsum(
            'bhsk,bhsv->bhkv', kb * jnp.exp(A_last - A), vb)
        return new_state, o_inter + o_intra

    state0 = jnp.zeros((B, H, K, V), jnp.float32)
    _, ys = lax.scan(step, state0, (to_chunks(q), to_chunks(k), to_chunks(v), to_chunks(log_f)))
    return ys.transpose(1, 0, 3, 2, 4).reshape(B, S, H, V)


def hgrn2_group(q, f_pre, i, g, lb, onorm):
    B, S, _ = q.shape
    f32 = jnp.float32
    q = jax.nn.silu(q.reshape(B, S, HGRN_HEADS, HGRN_KDIM).astype(f32)) * (HGRN_KDIM ** -0.5)
    lb = lb.reshape(HGRN_HEADS, HGRN_KDIM).astype(f32)
    f = lb + (1.0 - lb) * jax.nn.sigmoid(f_pre.reshape(B, S, HGRN_HEADS, HGRN_KDIM).astype(f32))
    v = i.reshape(B, S, HGRN_HEADS, HGRN_VDIM).astype(f32)
    o = chunk_gated_recurrence(q, 1.0 - f, v, jnp.log(f))
    o = rms_norm(o, onorm) * jax.nn.silu(g.reshape(B, S, HGRN_HEADS, HGRN_VDIM).astype(f32))
    return o.reshape(B, S, HG_V).astype(i.dtype)


def dsa_group(q, k, v, q_idx, k_idx, w_idx, qnorm, knorm):
    B, S, _ = q.shape
    f32 = jnp.float32
    H, Dh = ATTN_HEADS, ATTN_HEAD_DIM
    q = rms_norm(q.reshape(B, S, H, Dh), qnorm)
    k = rms_norm(k.reshape(B, S, H, Dh), knorm)
    v = v.reshape(B, S, H, Dh)
    q_idx = q_idx.reshape(B, S, IDX_HEADS, IDX_DIM).astype(f32)
    k_idx = k_idx.astype(f32)
    w_idx = w_idx.astype(f32) * (IDX_HEADS ** -0.5 * IDX_DIM ** -0.5)
    top = min(DSA_TOPK, S // 4)
    n_blocks = S // DSA_QBLOCK
    s_pos = jnp.arange(S)
    scale = Dh ** -0.5
    gather = jax.vmap(lambda kv, ix: kv[ix])

    def block(bi):
        start = bi * DSA_QBLOCK
        qb = lax.dynamic_slice_in_dim(q, start, DSA_QBLOCK, axis=1)
        qib = lax.dynamic_slice_in_dim(q_idx, start, DSA_QBLOCK, axis=1)
        wb = lax.dynamic_slice_in_dim(w_idx, start, DSA_QBLOCK, axis=1)
        t_pos = start + jnp.arange(DSA_QBLOCK)
        logits = jnp.einsum('bthd,bsd->bths', qib, k_idx)
        idx_score = jnp.einsum('bth,bths->bts', wb, jax.nn.relu(logits))
        visible = s_pos[None, :] <= t_pos[:, None]
        idx_score = jnp.where(visible[None], idx_score, -jnp.inf)
        _, sel = lax.top_k(idx_score, top)
        valid = sel <= t_pos[None, :, None]
        kg = gather(k, sel)
        vg = gather(v, sel)
        s = jnp.einsum('bthd,btkhd->bthk', qb, kg).astype(f32) * scale
        s = jnp.where(valid[:, :, None, :], s, -jnp.inf)
        p = jax.nn.softmax(s, axis=-1).astype(vg.dtype)
        return jnp.einsum('bthk,btkhd->bthd', p, vg)

    out = lax.map(block, jnp.arange(n_blocks))
    return out.transpose(1, 0, 2, 3, 4).reshape(B, S, ATT_W)


def hybrid_mixer(xn, w_in, lb, hgrn_onorm, attn_qnorm, attn_knorm, w_out):
    proj = xn @ w_in
    (hq, hf, hi, hg, aq, ak, av, iq, ik, iw) = jnp.split(proj, SPLIT_POINTS, axis=-1)
    y_h = hgrn2_group(hq, hf, hi, hg, lb, hgrn_onorm)
    y_a = dsa_group(aq, ak, av, iq, ik, iw, attn_qnorm, attn_knorm)
    return jnp.concatenate([y_h, y_a], axis=-1) @ w_out


def memory_cross_attention(hn, memn, wq, wk, wv, wo, qnorm, knorm):
    B, S, _ = hn.shape
    M = memn.shape[1]
    q = rms_norm((hn @ wq).reshape(B, S, CROSS_HEADS, CROSS_HEAD_DIM), qnorm)
    k = rms_norm((memn @ wk).reshape(B, M, CROSS_HEADS, CROSS_HEAD_DIM), knorm)
    v = (memn @ wv).reshape(B, M, CROSS_HEADS, CROSS_HEAD_DIM)
    s = jnp.einsum('bthd,bmhd->bhtm', q, k).astype(jnp.float32) * (CROSS_HEAD_DIM ** -0.5)
    p = jax.nn.softmax(s, axis=-1).astype(v.dtype)
    o = jnp.einsum('bhtm,bmhd->bthd', p, v).reshape(B, S, CROSS_WIDTH)
    return o @ wo


def sqrelu_mlp(hn, w_up, w_down):
    return jnp.square(jax.nn.relu(hn @ w_up)) @ w_down


def setup_inputs(seed: int = 0) -> dict:
    key = jax.random.key(seed)
    ks = jax.random.split(key, 20)
    f32 = jnp.float32
    nrm = lambda k, shape, fan_in: jax.random.normal(k, shape, f32) * (fan_in ** -0.5)
    gain = lambda k, shape: 1.0 + 0.02 * jax.random.normal(k, shape, f32)
    L = DEPTH
    return {
        "x": jax.random.normal(ks[0], (BATCH, SEQ, D_MODEL), f32),
        "mem": jax.random.normal(ks[1], (BATCH, N_MEM, D_MODEL), f32),
        "norm_mix": gain(ks[2], (L, D_MODEL)),
        "w_in": nrm(ks[3], (L, D_MODEL, IN_WIDTH), D_MODEL),
        "hgrn_lb_logits": 0.1 * jax.random.normal(ks[4], (L + 1, HG_K), f32),
        "hgrn_onorm": gain(ks[5], (L, HGRN_VDIM)),
        "attn_qnorm": gain(ks[6], (L, ATTN_HEAD_DIM)),
        "attn_knorm": gain(ks[7], (L, ATTN_HEAD_DIM)),
        "w_out": nrm(ks[8], (L, OUT_WIDTH, D_MODEL), OUT_WIDTH),
        "norm_cross": gain(ks[9], (L, D_MODEL)),
        "mem_norm": gain(ks[10], (L, D_MODEL)),
        "wq_x": nrm(ks[11], (L, D_MODEL, CROSS_WIDTH), D_MODEL),
        "wk_x": nrm(ks[12], (L, D_MODEL, CROSS_WIDTH), D_MODEL),
        "wv_x": nrm(ks[13], (L, D_MODEL, CROSS_WIDTH), D_MODEL),
        "wo_x": nrm(ks[14], (L, CROSS_WIDTH, D_MODEL), CROSS_WIDTH),
        "xq_norm": gain(ks[15], (L, CROSS_HEAD_DIM)),
        "xk_norm": gain(ks[16], (L, CROSS_HEAD_DIM)),
        "norm_mlp": gain(ks[17], (L, D_MODEL)),
        "w_up": nrm(ks[18], (L, D_MODEL, D_FF), D_MODEL),
        "w_down": nrm(ks[19], (L, D_FF, D_MODEL), D_FF),
    }


def reference(x, mem, norm_mix, w_in, hgrn_lb_logits, hgrn_onorm, attn_qnorm, attn_knorm, w_out,
              norm_cross, mem_norm, wq_x, wk_x, wv_x, wo_x, xq_norm, xk_norm,
              norm_mlp, w_up, w_down):
    lb_all = jnp.cumsum(jax.nn.softmax(hgrn_lb_logits.astype(jnp.float32), axis=0), axis=0)
    h = x
    for l in range(DEPTH):
        h = h + hybrid_mixer(rms_norm(h, norm_mix[l]), w_in[l], lb_all[l], hgrn_onorm[l],
                             attn_qnorm[l], attn_knorm[l], w_out[l])
        h = h + memory_cross_attention(rms_norm(h, norm_cross[l]), rms_norm(mem, mem_norm[l]),
                                       wq_x[l], wk_x[l], wv_x[l], wo_x[l], xq_norm[l], xk_norm[l])
        h = h + sqrelu_mlp(rms_norm(h, norm_mlp[l]), w_up[l], w_down[l])
    return h
```

```python
import contextlib
import numpy as np
import concourse.bass as bass
import concourse.mybir as mybir
from concourse.bass_utils import run_bass_kernel_spmd

F32 = mybir.dt.float32
BF16 = mybir.dt.bfloat16
I32 = mybir.dt.int32
AF = mybir.ActivationFunctionType
ALU = mybir.AluOpType
AX = mybir.AxisListType

D = 2048
T = 4096
OWN = 1024
NT = T // 128
KC = D // 128
EPS = 1e-6
NEG = -1.0e30
C_HQ, C_HF, C_HI, C_HG, C_AQ, C_AK, C_AV, C_IQ, C_IK, C_IW = 0, 1024, 2048, 3072, 4096, 5120, 6144, 7168, 8192, 8256
NBIS = 14


class Sem:
    _n = 0

    def __init__(self, h):
        self.h = h
        Sem._n += 1
        self.uid = Sem._n


class Res:
    def __init__(self, name):
        self.name = name
        self.lw = None
        self.rd = {}
        self.dsem = None
        self.dcount = 0


class Buf:
    def __init__(self, t, name):
        self.t = t
        self.r = Res(name)
        self.name = name
        self._subs = {}

    def sub(self, key):
        if key not in self._subs:
            self._subs[key] = Res(f"{self.name}.{key}")
        return self._subs[key]

    def __getitem__(self, idx):
        return self.t[idx]


class Eng:
    def __init__(self, kb, name, eng, pe=False):
        self.kb = kb
        self.name = name
        self.eng = eng
        self.pe = pe
        self.sem = kb.newsem("e_" + name)
        self.count = 0
        self.seen = {}


class KB:
    def __init__(self):
        self.nc = bass.Bass("TRN2", target_bir_lowering=False)
        self.es = contextlib.ExitStack()
        self.nsem = 0
        nc = self.nc
        self.pe = Eng(self, "pe", nc.tensor, pe=True)
        self.act = Eng(self, "act", nc.scalar)
        self.dve = Eng(self, "dve", nc.vector)
        self.pool = Eng(self, "pool", nc.gpsimd)
        self.sp = Eng(self, "sp", nc.sync)
        self.engs = [self.pe, self.act, self.dve, self.pool, self.sp]
        self.dma_owners = []
        self.ninst = 0

    def newsem(self, name):
        self.nsem += 1
        return Sem(self.es.enter_context(self.nc.semaphore(f"{name}_{self.nsem}")))

    def dram(self, name, shape, dt, kind="Internal"):
        return self.nc.dram_tensor(name, list(shape), dt, kind=kind).ap()

    def sb(self, name, shape, dt, es=None):
        es = es or self.es
        self.nsb = getattr(self, "nsb", 0) + 1
        name = f"{name}_{self.nsb}"
        return Buf(es.enter_context(self.nc.sbuf_tensor(name, list(shape), dt)), name)

    def ps(self, name, shape, dt, es=None):
        es = es or self.es
        return Buf(es.enter_context(self.nc.psum_tensor(name, list(shape), dt)), name)

    @staticmethod
    def _res(x):
        return x.r if isinstance(x, Buf) else x

    def _waits(self, E, reads, writes):
        deps = {}

        def add(d):
            if d is None:
                return
            s, v = d
            if s.uid not in deps or deps[s.uid][1] < v:
                deps[s.uid] = (s, v)

        for r in reads:
            add(self._res(r).lw)
        for w in writes:
            w = self._res(w)
            add(w.lw)
            for d in w.rd.values():
                add(d)
        for uid, (s, v) in deps.items():
            if E.pe and s is E.sem:
                continue
            if E.seen.get(uid, 0) >= v:
                continue
            E.eng.wait_ge(s.h, v)
            E.seen[uid] = v

    def _commit(self, dep, reads, writes):
        s, v = dep
        for r in reads:
            r = self._res(r)
            if s.uid not in r.rd or r.rd[s.uid][1] < v:
                r.rd[s.uid] = dep
        for w in writes:
            w = self._res(w)
            w.lw = dep
            w.rd = {}

    def op(self, E, reads, writes, fn, inc=True):
        self._waits(E, reads, writes)
        inst = fn()
        self.ninst += 1
        if inc:
            E.count += 1
            inst.then_inc(E.sem.h, 1)
            idx = E.count
        else:
            idx = E.count + 1
        self._commit((E.sem, idx), reads, writes)

    def dma(self, Q, out_ap, in_ap, reads, writes, owner, **kw):
        owner = self._res(owner)
        self._waits(Q, reads, writes)
        if owner.dsem is None:
            owner.dsem = self.newsem("d_" + owner.name)
            self.dma_owners.append(owner)
        owner.dcount += 16
        Q.eng.dma_start(out=out_ap, in_=in_ap, **kw).then_inc(owner.dsem.h, 16)
        self.ninst += 1
        self._commit((owner.dsem, owner.dcount), reads, writes)

    def barrier(self):
        M = self.act
        for E in self.engs:
            if E is M or E.count == 0:
                continue
            if M.seen.get(E.sem.uid, 0) < E.count:
                M.eng.wait_ge(E.sem.h, E.count)
                M.seen[E.sem.uid] = E.count
        for o in self.dma_owners:
            if o.dcount and M.seen.get(o.dsem.uid, 0) < o.dcount:
                M.eng.wait_ge(o.dsem.h, o.dcount)
                M.seen[o.dsem.uid] = o.dcount
        if M.seen.get(M.sem.uid, 0) < M.count:
            M.eng.wait_ge(M.sem.h, M.count)
            M.seen[M.sem.uid] = M.count
        M.count += 1
        M.eng.activation(out=self.bar_t[:, 0:1], in_=self.bar_t[:, 1:2], func=AF.Copy).then_inc(M.sem.h, 1)
        for E in self.engs:
            if E is M:
                continue
            E.eng.wait_ge(M.sem.h, M.count)
            E.seen[M.sem.uid] = M.count
            for E2 in self.engs:
                E.seen[E2.sem.uid] = max(E.seen.get(E2.sem.uid, 0), E2.count if E2 is not M else M.count)
            for o in self.dma_owners:
                E.seen[o.dsem.uid] = max(E.seen.get(o.dsem.uid, 0), o.dcount)
        for E2 in self.engs:
            M.seen[E2.sem.uid] = max(M.seen.get(E2.sem.uid, 0), E2.count)

    def V(self, reads, writes, fn):
        self.op(self.dve, reads, writes, fn)

    def A(self, reads, writes, fn):
        self.op(self.act, reads, writes, fn)

    def G(self, reads, writes, fn):
        self.op(self.pool, reads, writes, fn)

    def PE(self, reads, writes, fn, inc=True):
        self.op(self.pe, reads, writes, fn, inc=inc)


class Rot:
    def __init__(self, bufs):
        self.bufs = bufs
        self.i = 0

    def next(self):
        b = self.bufs[self.i % len(self.bufs)]
        self.i += 1
        return b


def build(dbg=None):
    k = KB()
    nc = k.nc
    V, A, G, PE = k.V, k.A, k.G, k.PE
    vec, act, pool, ten = nc.vector, nc.scalar, nc.gpsimd, nc.tensor
    dbg = dbg or ()

    def din(name, shape, dt=F32):
        return k.dram(name, shape, dt, kind="ExternalInput")

    xs = din("xs", [T, D])
    keybias = din("keybias", [1, T - OWN])
    memb = din("mem", [256, D])
    norm_mix = din("norm_mix", [1, D])
    w_in = din("w_in", [D, 8272])
    lbl = din("lbl", [2, 1024])
    onorm = din("onorm", [1, 128])
    qnorm = din("qnorm", [1, 128])
    knorm = din("knorm", [1, 128])
    w_out = din("w_out", [D, D])
    norm_cross = din("norm_cross", [1, D])
    mem_norm = din("mem_norm", [1, D])
    wq_x = din("wq_x", [D, 512])
    wk_x = din("wk_x", [D, 512])
    wv_x = din("wv_x", [D, 512])
    wo_x = din("wo_x", [512, D])
    xqn = din("xqn", [1, 128])
    xkn = din("xkn", [1, 128])
    norm_mlp = din("norm_mlp", [1, D])
    w_up = din("w_up", [D, 8192])
    w_down = din("w_down", [8192, D])
    out = k.dram("out", [OWN, D], F32, kind="ExternalOutput")
    OUT = Buf(None, "OUT")

    xnTs = k.dram("xnTs", [8, 128, KC * 512], BF16)
    XNTS = Buf(None, "xnTs")
    KTs = k.dram("KTs", [128, 8, T], BF16)
    KTS = Buf(None, "KTs")
    Vs = k.dram("Vs", [8, 128, NT * 129], BF16)
    VS = Buf(None, "Vs")

    yTs = k.dram("yTs", [128, KC, OWN], BF16)
    YTS = Buf(None, "yTs")

    dbg_out = {}

    def dbg_tensor(name, shape, dt=F32):
        dbg_out[name] = k.dram("dbg_" + name, shape, dt, kind="ExternalOutput")
        return dbg_out[name]

    k.bar_t = k.sb("bar_t", [128, 2], F32).t
    ident = k.sb("ident", [128, 128], BF16)
    iota_i = k.sb("iota_i", [128, 128], I32)
    def walloc(es, tag):
        return [k.sb(f"W{tag}{i}", [128, KC, 512], BF16, es) for i in range(4)]
    PSB = [k.ps(f"psb{i}", [128, 512], F32) for i in range(6)]
    psr4 = Rot(PSB[0:4])
    por = Rot(PSB[4:6])
    PTP = [k.ps(f"ptp{i}", [128, 1024], BF16) for i in range(2)]
    psr = Rot(PSB)
    tpr = Rot(PTP)
    evac_i = [0]

    def evac_copy(out_ap, in_ap, reads, writes):
        evac_i[0] += 1
        if evac_i[0] % 2:
            A(reads, writes, lambda: act.copy(out=out_ap, in_=in_ap))
        else:
            V(reads, writes, lambda: vec.tensor_copy(out=out_ap, in_=in_ap))

    nc.vector.memset(k.bar_t[:], 0.0)
    G([], [iota_i], lambda: pool.iota(out=iota_i[:], pattern=[[1, 128]], base=0, channel_multiplier=-1))
    V([iota_i], [ident], lambda: vec.tensor_single_scalar(out=ident[:], in_=iota_i[:], scalar=0.0, op=ALU.is_equal))

    l01 = k.sb("l01", [128, 2, 8], F32)
    lbv = k.sb("lbv", [128, 8], F32)
    oml = k.sb("oml", [128, 8], F32)
    noml = k.sb("noml", [128, 8], F32)
    mreset = k.sb("mreset", [128, 512], F32)
    triT = k.sb("triT", [128, 128], F32)
    on_b4 = k.sb("on_b4", [128, 512], F32)
    with contextlib.ExitStack() as es0:
        mri = k.sb("mri", [128, 512], I32, es0)
        k.dma(k.sp, l01.t[:, :, :], lbl.rearrange("r (h q) -> q r h", q=128), [], [l01], l01,
              allow_slow_non_contiguous=True)
        for i4 in range(4):
            k.dma(k.sp, on_b4.t[:, i4 * 128:(i4 + 1) * 128], onorm[0:1, :].partition_broadcast(128), [], [on_b4], on_b4)
        V([l01], [lbv], lambda: vec.tensor_tensor(out=lbv.t[:, :], in0=l01.t[:, 0, :], in1=l01.t[:, 1, :], op=ALU.subtract))
        A([lbv], [lbv], lambda: act.activation(out=lbv.t[:, :], in_=lbv.t[:, :], func=AF.Sigmoid))
        V([lbv], [oml], lambda: vec.tensor_scalar(out=oml.t[:, :], in0=lbv.t[:, :], scalar1=-1.0, scalar2=1.0,
                                                  op0=ALU.mult, op1=ALU.add))
        V([lbv], [noml], lambda: vec.tensor_scalar(out=noml.t[:, :], in0=lbv.t[:, :], scalar1=-1.0, scalar2=None,
                                                   op0=ALU.add))
        G([], [mri], lambda: pool.iota(out=mri.t[:, :].rearrange("p (j t) -> p j t", t=128), pattern=[[0, 4], [1, 128]],
                                       base=0, channel_multiplier=0))
        V([mri], [mreset], lambda: vec.tensor_single_scalar(out=mreset.t[:, :], in_=mri.t[:, :], scalar=0.0, op=ALU.is_gt))
        V([iota_i], [triT], lambda: vec.tensor_single_scalar(out=triT.t[:, :], in_=iota_i.t[:, :], scalar=0.0, op=ALU.is_ge))
        k.barrier()

    def load_w(slot, src2d, c0, ncols=512, rows=D):
        kc = rows // 128
        src = src2d.rearrange("(c p) n -> p c n", p=128)[:, :, c0:c0 + ncols]
        k.dma(k.pool, slot.t[:, 0:kc, 0:ncols], src, [], [slot], slot, max_dma_last_dim=4096)

    def bcast_row(dst, src_row, n):
        k.dma(k.sp, dst.t[:, 0:n], src_row.partition_broadcast(128), [], [dst], dst)

    def rstd_from_ss(ss, n, width, es_bufs):
        tmp = es_bufs
        V([ss], [tmp], lambda: vec.tensor_scalar(out=tmp[:, 0:n], in0=ss[:, 0:n], scalar1=1.0 / width, scalar2=EPS,
                                                  op0=ALU.mult, op1=ALU.add))
        A([tmp], [tmp], lambda: act.activation(out=tmp[:, 0:n], in_=tmp[:, 0:n], func=AF.Sqrt))
        V([tmp], [ss], lambda: vec.reciprocal(out=ss[:, 0:n], in_=tmp[:, 0:n]))

    def norm_part1(src_ap, src_res, gain_b, xnb, ss, tmp, junk):
        A([src_res], [xnb, ss], lambda: act.activation(out=xnb[:, 0:D], in_=src_ap, func=AF.Square,
                                                      accum_out=ss[:, 0:1]))
        rstd_from_ss(ss, 1, D, tmp)
        V([src_res, ss, gain_b], [xnb], lambda: vec.scalar_tensor_tensor(
            out=xnb[:, :], in0=src_ap, scalar=ss[:, 0:1], in1=gain_b[:, :], op0=ALU.mult, op1=ALU.mult))

    def norm_part2(xnb, dstT, col0, dst_res=None):
        dst_res = dst_res if dst_res is not None else dstT
        for g in range(KC // 8):
            tp = tpr.next()
            for j in range(8):
                c = g * 8 + j
                PE([xnb, ident], [tp], lambda c=c, j=j: ten.transpose(
                    out=tp.t[:, j * 128:(j + 1) * 128], in_=xnb[:, c * 128:(c + 1) * 128], identity=ident[:]),
                   inc=(j == 7))
            evac_copy(dstT.t[:, g * 8:(g + 1) * 8, col0:col0 + 128],
                      tp.t[:, :].rearrange("p (c t) -> p c t", c=8), [tp], [dst_res])

    def norm_tile_to_T(src_ap, src_res, gain_b, xnb, dstT, col0, ss, tmp, junk):
        norm_part1(src_ap, src_res, gain_b, xnb, ss, tmp, junk)
        norm_part2(xnb, dstT, col0)

    def headnorm_T(lhs_fn, lhs_res, Wlist, g_b, dst, dcol0, ssr, tmpr, knbr, junk, ncc=KC, defer=False):
        nh = 4 * len(Wlist)
        ss = ssr.next()
        tmp = tmpr.next()
        knb = knbr.next()
        kps = []
        for cg, Wc in enumerate(Wlist):
            ps = psr.next()
            kps.append(ps)
            for c in range(ncc):
                PE([lhs_res, Wc], [ps], lambda c=c, Wc=Wc, ps=ps: ten.matmul(
                    ps.t[:, :], lhs_fn(c), Wc.t[:, c, :],
                    start=(c == 0), stop=(c == ncc - 1)), inc=(c == ncc - 1))
            for hh in range(4):
                h = cg * 4 + hh
                A([ps], [junk, ss], lambda ps=ps, hh=hh, h=h: act.activation(
                    out=junk[:, 0:128], in_=ps.t[:, hh * 128:(hh + 1) * 128], func=AF.Square,
                    accum_out=ss[:, h:h + 1]))
        rstd_from_ss(ss, nh, 128, tmp)
        for cg in range(len(Wlist)):
            ps = kps[cg]
            for hh in range(4):
                h = cg * 4 + hh
                V([ps, ss, g_b], [knb], lambda ps=ps, hh=hh, h=h: vec.scalar_tensor_tensor(
                    out=knb.t[:, h, :], in0=ps.t[:, hh * 128:(hh + 1) * 128], scalar=ss[:, h:h + 1],
                    in1=g_b[:, :], op0=ALU.mult, op1=ALU.mult))
        def part_b():
            tp = tpr.next()
            for h in range(nh):
                PE([knb, ident], [tp], lambda h=h: ten.transpose(
                    out=tp.t[:, h * 128:(h + 1) * 128], in_=knb.t[:, h, :], identity=ident[:]), inc=(h == nh - 1))
            evac_copy(dst.t[:, :, dcol0:dcol0 + 128], tp.t[:, 0:nh * 128].rearrange("p (h t) -> p h t", h=nh), [tp], [dst])

        if defer:
            return part_b
        part_b()

    es_att = contextlib.ExitStack()
    kidxT = [k.sb(f"kidxT{v_}", [128, T], BF16, es_att) for v_ in range(2)]
    V([], [kidxT[0]], lambda: vec.memset(kidxT[0].t[64:128, :], 0.0))
    V([], [kidxT[1]], lambda: vec.memset(kidxT[1].t[0:64, :], 0.0))
    kbias_b = k.sb("kbias_b", [128, T - OWN], BF16, es_att)
    tri_neg = k.sb("tri_neg", [128, 128], F32, es_att)
    cpow = k.sb("cpow", [128, NBIS + 2], F32, es_att)
    k.dma(k.pool, kbias_b.t[:, :], keybias[0:1, :].partition_broadcast(128), [], [kbias_b], kbias_b,
          max_dma_last_dim=4096)
    V([iota_i], [tri_neg], lambda: vec.tensor_scalar(out=tri_neg[:, :], in0=iota_i[:, :], scalar1=0.0, scalar2=NEG,
                                                     op0=ALU.is_gt, op1=ALU.mult))
    for kk_ in range(NBIS + 2):
        V([], [cpow], lambda kk_=kk_: vec.memset(cpow.t[:, kk_:kk_ + 1], float(2.0 ** -kk_)))
    with contextlib.ExitStack() as es:
        W = walloc(es, "a")
        Wik = k.sb("Wik", [128, KC, 128], BF16, es)
        gain_b = k.sb("gain_b", [128, D], F32, es)
        gK_b = k.sb("gK_b", [128, 128], F32, es)
        xst = Rot([k.sb(f"xst{i}", [128, D], F32, es) for i in range(2)])
        junk = k.sb("junk", [128, 128], BF16, es)
        ssr = Rot([k.sb(f"ss{i}", [128, 8], F32, es) for i in range(4)])
        tmpr = Rot([k.sb(f"tmp{i}", [128, 8], F32, es) for i in range(4)])
        xnTr = Rot([k.sb(f"xnT{i}", [128, KC, 512], BF16, es) for i in range(2)])
        Kst = Rot([k.sb(f"Kst{i}", [128, 8, 512], BF16, es) for i in range(2)])
        Vst = Rot([k.sb(f"Vst{i}", [128, 8, 4, 129], BF16, es) for i in range(2)])
        knbr = Rot([k.sb(f"knb{i}", [128, 8, 128], BF16, es) for i in range(2)])

        for i in range(2):
            load_w(W[i], w_in, C_AK + i * 512)
        for i in range(2):
            load_w(W[2 + i], w_in, C_AV + i * 512)
        for half in range(2):
            src = w_in.rearrange("(c p) n -> p c n", p=128)[:, :, C_IK:C_IK + 64]
            k.dma(k.pool, Wik.t[:, :, half * 64:(half + 1) * 64], src, [], [Wik], Wik, max_dma_last_dim=4096)
        bcast_row(gain_b, norm_mix[0:1, :], D)
        bcast_row(gK_b, knorm[0:1, :], 128)
        for vb in Vst.bufs:
            V([], [vb], lambda vb=vb: vec.memset(vb.t[:, :, :, 128:129], 1.0))

        xnbr = Rot([k.sb(f"xnbp{i}", [128, D], BF16, es) for i in range(3)])

        def p1_part1(tile):
            xb = xst.next()
            k.dma(k.sp, xb.t[:, :], xs[tile * 128:(tile + 1) * 128, :], [], [xb], xb)
            xnb = xnbr.next()
            norm_part1(xb.t[:, :], xb, gain_b, xnb, ssr.next(), tmpr.next(), junk)
            return xnb

        blkbuf = {}

        def bufs_of(blk):
            if blk not in blkbuf:
                blkbuf[blk] = (xnTr.next(), Kst.next(), Vst.next())
            return blkbuf[blk]

        def p1_T(tile, xnb):
            xnT = bufs_of(tile // 4)[0]
            norm_part2(xnb, xnT, (tile % 4) * 128, dst_res=xnT.sub(tile % 4))

        def p1_block_stores(blk):
            xnT, kst, vst = bufs_of(blk)
            k.dma(k.sp, KTs[:, :, blk * 512:(blk + 1) * 512], kst.t[:, :, :], [kst], [KTS.sub(blk)], kst)
            k.dma(k.sp, Vs.rearrange("h p f -> p h f")[:, :, blk * 516:(blk + 1) * 516],
                  vst.t[:, :, :, :].rearrange("p h j d -> p h (j d)"), [vst], [VS.sub(blk)], vst)

        xnb_of = {0: p1_part1(0), 1: p1_part1(1)}
        p1_T(0, xnb_of.pop(0))
        kb_prev = None
        for tile in range(NT):
            blk, j = tile // 4, tile % 4
            xnT, kst, vst = bufs_of(blk)
            if tile + 2 < NT:
                xnb_of[tile + 2] = p1_part1(tile + 2)
            if tile + 1 < NT:
                p1_T(tile + 1, xnb_of.pop(tile + 1))
            xr = xnT.sub(j)
            kb_part = headnorm_T(lambda c, xnT=xnT, j=j: xnT.t[:, c, j * 128:(j + 1) * 128], xr, [W[0], W[1]], gK_b, kst,
                                 j * 128, ssr, tmpr, knbr, junk, defer=True)
            for cg in range(2):
                ps = psr.next()
                for c in range(KC):
                    PE([xr, W[2 + cg]], [ps], lambda c=c, cg=cg, ps=ps: ten.matmul(
                        ps.t[:, :], xnT.t[:, c, j * 128:(j + 1) * 128], W[2 + cg].t[:, c, :],
                        start=(c == 0), stop=(c == KC - 1)), inc=(c == KC - 1))
                evac_copy(vst.t[:, cg * 4:(cg + 1) * 4, j, 0:128],
                          ps.t[:, :].rearrange("p (h d) -> p h d", h=4), [ps], [vst])
            if kb_prev is not None:
                kb_prev()
                if j == 0:
                    p1_block_stores(blk - 1)
            kb_prev = kb_part
            if j == 3:
                allx = [xnT.sub(jj) for jj in range(4)]
                k.dma(k.sp, xnTs[blk], xnT.t[:, :, :].rearrange("p c t -> p (c t)"), allx, [XNTS.sub(blk)], xnT)
                ps = psr.next()
                for c in range(KC):
                    PE(allx + [Wik], [ps], lambda c=c, ps=ps: ten.matmul(
                        ps.t[:, :], Wik.t[:, c, :], xnT.t[:, c, :], start=(c == 0), stop=(c == KC - 1)), inc=(c == KC - 1))
                evac_copy(kidxT[0].t[0:64, blk * 512:(blk + 1) * 512], ps.t[0:64, :], [ps], [kidxT[0]])
                evac_copy(kidxT[1].t[64:128, blk * 512:(blk + 1) * 512], ps.t[64:128, :], [ps], [kidxT[1]])
        kb_prev()
        p1_block_stores(7)
        k.barrier()

    QT = k.sb("QT", [128, 8, OWN], BF16, es_att)
    qidxT = k.sb("qidxT", [128, 8, OWN], BF16, es_att)
    w_own = k.sb("w_own", [128, 8, 16], F32, es_att)
    with contextlib.ExitStack() as es:
        W = walloc(es, "b")
        Wiw = k.sb("Wiw", [128, KC, 16], BF16, es)
        gQ_b = k.sb("gQ_b", [128, 128], F32, es)
        junk = k.sb("junk2", [128, 128], BF16, es)
        ssr = Rot([k.sb(f"ss2{i}", [128, 8], F32, es) for i in range(4)])
        tmpr = Rot([k.sb(f"tmp2{i}", [128, 8], F32, es) for i in range(4)])
        xnTr = Rot([k.sb(f"xnT2{i}", [128, KC, 512], BF16, es) for i in range(2)])
        knbr = Rot([k.sb(f"knb2{i}", [128, 8, 128], BF16, es) for i in range(2)])
        for i in range(2):
            load_w(W[i], w_in, C_AQ + i * 512)
        for i in range(2):
            load_w(W[2 + i], w_in, C_IQ + i * 512)
        src = w_in.rearrange("(c p) n -> p c n", p=128)[:, :, C_IW:C_IW + 16]
        k.dma(k.pool, Wiw.t[:, :, :], src, [], [Wiw], Wiw, max_dma_last_dim=4096)
        bcast_row(gQ_b, qnorm[0:1, :], 128)
        for ob in range(2):
            xnT = xnTr.next()
            k.dma(k.sp, xnT.t[:, :, :].rearrange("p c t -> p (c t)"), xnTs[6 + ob], [XNTS.sub(6 + ob)], [xnT], xnT)
            for j in range(4):
                qt = ob * 4 + j
                headnorm_T(lambda c, xnT=xnT, j=j: xnT.t[:, c, j * 128:(j + 1) * 128], xnT, [W[0], W[1]], gQ_b, QT, qt * 128, ssr, tmpr, knbr, junk)
                ps = psr.next()
                for c in range(KC):
                    PE([xnT, Wiw], [ps], lambda c=c, ps=ps: ten.matmul(
                        ps.t[:, 0:16], xnT.t[:, c, j * 128:(j + 1) * 128], Wiw.t[:, c, :],
                        start=(c == 0), stop=(c == KC - 1)), inc=(c == KC - 1))
                A([ps], [w_own], lambda ps=ps, qt=qt: act.activation(
                    out=w_own.t[:, qt, :], in_=ps.t[:, 0:16], func=AF.Copy, scale=1.0 / 32.0))
            for pair in range(8):
                ps = psr.next()
                Wc = W[2 + pair // 4]
                for c in range(KC):
                    PE([xnT, Wc], [ps], lambda c=c, ps=ps, Wc=Wc, pair=pair: ten.matmul(
                        ps.t[:, :], Wc.t[:, c, (pair % 4) * 128:(pair % 4 + 1) * 128], xnT.t[:, c, :],
                        start=(c == 0), stop=(c == KC - 1)), inc=(c == KC - 1))
                evac_copy(qidxT.t[:, pair, ob * 512:(ob + 1) * 512], ps.t[:, :], [ps], [qidxT])
        k.barrier()

    maskT = k.sb("maskT", [128, 8, NT, 128], BF16, es_att)
    with contextlib.ExitStack() as es:
        accr = Rot([k.sb(f"acc{i}", [128, T], F32, es) for i in range(2)])
        Rr = Rot([k.sb(f"Rr{i}", [128, 512], BF16, es) for i in range(4)])
        Dsr = Rot([k.sb(f"Dsgn{i}", [128, 16, 128], BF16, es) for i in range(2)])
        absw = k.sb("absw", [128, 8, 16], F32, es)
        sgnw = k.sb("sgnw", [128, 8, 16], F32, es)
        mkr = Rot([k.sb(f"mk{i}", [128, T], BF16, es) for i in range(2)])
        junkb = k.sb("junkb", [128, T], BF16, es)
        str_ = Rot([k.sb(f"st{i}", [128, 8], F32, es) for i in range(2)])
        hwr = Rot([k.sb(f"hwt{i}", [128, NBIS + 2], F32, es) for i in range(2)])
        V([w_own], [sgnw], lambda: vec.tensor_scalar(out=sgnw.t[:, :, :], in0=w_own.t[:, :, :], scalar1=0.0, scalar2=2.0,
                                                     op0=ALU.is_ge, op1=ALU.mult))
        V([sgnw], [sgnw], lambda: vec.tensor_scalar(out=sgnw.t[:, :, :], in0=sgnw.t[:, :, :], scalar1=-1.0, scalar2=None,
                                                    op0=ALU.add))
        V([w_own, sgnw], [absw], lambda: vec.tensor_tensor(out=absw.t[:, :, :], in0=w_own.t[:, :, :], in1=sgnw.t[:, :, :],
                                                           op=ALU.mult))

        psr3, par3 = Rot(PSB[0:3]), Rot(PSB[3:6])

        def indexer(i, acc):
            nk = (T - OWN) + (i + 1) * 128
            nch = (nk + 511) // 512
            Ds = Dsr.next()
            for h in range(16):
                G([ident, sgnw], [Ds], lambda h=h, Ds=Ds: pool.tensor_scalar(
                    out=Ds.t[:, h, :], in0=ident.t[:, :], scalar1=sgnw.t[:, i, h:h + 1], scalar2=None, op0=ALU.mult))
            for ch in range(nch):
                k0 = ch * 512
                wd = min(512, nk - k0)
                pa = par3.next()

                def L(h):
                    pair, kv = h // 2, kidxT[h % 2]
                    ps = psr3.next()
                    PE([qidxT, kv], [ps], lambda: ten.matmul(
                        ps.t[:, 0:wd], qidxT.t[:, pair, i * 128:(i + 1) * 128],
                        kv.t[:, k0:k0 + wd], start=True, stop=True))
                    R = Rr.next()
                    A([ps, absw], [R], lambda: act.activation(out=R.t[:, 0:wd], in_=ps.t[:, 0:wd], func=AF.Relu,
                                                             scale=absw.t[:, i, h:h + 1]))
                    return R

                def Dm(h, R):
                    PE([Ds, R], [pa], lambda: ten.matmul(pa.t[:, 0:wd], Ds.t[:, h, :], R.t[:, 0:wd],
                                                         start=(h == 0), stop=(h == 15)), inc=(h == 15))

                pend = [L(0), L(1)]
                for h in range(16):
                    if h + 2 < 16:
                        pend.append(L(h + 2))
                    Dm(h, pend[h])
                V([pa], [acc.sub(ch)], lambda pa=pa, k0=k0, wd=wd: vec.tensor_copy(out=acc.t[:, k0:k0 + wd], in_=pa.t[:, 0:wd]))
                yield

        def bisect(i, acc):
            nk = (T - OWN) + (i + 1) * 128
            nch = (nk + 511) // 512
            live = [acc.sub(ch) for ch in range(nch)]
            st, hwt, mk = str_.next(), hwr.next(), mkr.next()
            V(live, [st], lambda: vec.tensor_reduce(out=st.t[:, 1:2], in_=acc.t[:, 0:nk], axis=AX.X, op=ALU.max))
            V(live, [st], lambda: vec.tensor_reduce(out=st.t[:, 0:1], in_=acc.t[:, 0:nk], axis=AX.X, op=ALU.min))
            yield
            V(live + [kbias_b], live, lambda: vec.tensor_tensor(out=acc.t[:, 0:T - OWN], in0=acc.t[:, 0:T - OWN],
                                                                in1=kbias_b.t[:, :], op=ALU.add))
            d0 = (T - OWN) + i * 128
            V(live + [tri_neg], live, lambda: vec.tensor_tensor(out=acc.t[:, d0:d0 + 128], in0=acc.t[:, d0:d0 + 128],
                                                                in1=tri_neg.t[:, :], op=ALU.add))
            V([st], [st], lambda: vec.tensor_tensor(out=st.t[:, 5:6], in0=st.t[:, 1:2], in1=st.t[:, 0:1], op=ALU.subtract))
            V([st, cpow], [hwt], lambda: vec.tensor_scalar(out=hwt.t[:, :], in0=cpow.t[:, :], scalar1=st.t[:, 5:6], scalar2=None,
                                                           op0=ALU.mult))
            V([st, hwt], [st], lambda: vec.tensor_tensor(out=st.t[:, 2:3], in0=st.t[:, 0:1], in1=hwt.t[:, 1:2], op=ALU.add))
            yield
            for it in range(NBIS):
                V(live + [st], [junkb, st], lambda: vec.tensor_scalar(
                    out=junkb.t[:, 0:nk], in0=acc.t[:, 0:nk], scalar1=st.t[:, 2:3], scalar2=None,
                    op0=ALU.is_ge, op1=ALU.add, accum_out=st.t[:, 3:4]))
                V([st], [st], lambda: vec.tensor_scalar(out=st.t[:, 4:5], in0=st.t[:, 3:4], scalar1=255.5, scalar2=0.5,
                                                        op0=ALU.is_ge, op1=ALU.subtract))
                V([st, hwt], [st], lambda it=it: vec.scalar_tensor_tensor(
                    out=st.t[:, 2:3], in0=st.t[:, 4:5], scalar=hwt.t[:, it + 1:it + 2], in1=st.t[:, 2:3],
                    op0=ALU.mult, op1=ALU.add))
                yield
            V(live + [st, hwt], [mk], lambda: vec.tensor_scalar(
                out=mk.t[:, 0:nk], in0=acc.t[:, 0:nk], scalar1=hwt.t[:, NBIS + 1:NBIS + 2], scalar2=st.t[:, 2:3],
                op0=ALU.add, op1=ALU.is_ge))
            mk_of[i] = mk
            yield

        def mask_transposes(i):
            mk = mk_of.pop(i)
            nkb = ((T - OWN) + (i + 1) * 128) // 128
            for g0 in range(0, nkb, 8):
                n = min(8, nkb - g0)
                tp = tpr.next()
                for jj in range(n):
                    kb_ = g0 + jj
                    PE([mk, ident], [tp], lambda kb_=kb_, jj=jj, tp=tp: ten.transpose(
                        out=tp.t[:, jj * 128:(jj + 1) * 128], in_=mk.t[:, kb_ * 128:(kb_ + 1) * 128], identity=ident[:]),
                       inc=(jj == n - 1))
                evac_copy(maskT.t[:, i, g0:g0 + n, :], tp.t[:, 0:n * 128].rearrange("p (k t) -> p k t", k=n),
                          [tp], [maskT.sub(i)])

        mk_of = {}
        prev = None
        order = list(range(7, -1, -1))
        for pos, i in enumerate(order):
            acc = accr.next()
            for ci, _ in enumerate(indexer(i, acc)):
                if ci == 1 and pos >= 2:
                    mask_transposes(order[pos - 2])
                if prev is not None:
                    for _r in range(3):
                        if next(prev, "end") == "end":
                            prev = None
                            break
            if prev is not None:
                for _ in prev:
                    pass
            prev = bisect(i, acc)
        mask_transposes(order[6])
        for _ in prev:
            pass
        mask_transposes(order[7])
        k.barrier()

    if "pa" in dbg:
        d_mt = dbg_tensor("maskT", [128, 8 * NT * 128], BF16)
        k.dma(k.sp, d_mt[:, :], maskT.t[:, :, :, :].rearrange("p a b c -> p (a b c)"), [maskT.sub(i) for i in range(8)],
              [OUT.sub("mt")], maskT)
    with contextlib.ExitStack() as es:
        yatr = Rot([k.sb(f"yat{i}", [128, OWN], BF16, es) for i in range(2)])
        KTh = Rot([k.sb(f"KTh{i}", [128, T], BF16, es) for i in range(2)])
        Vh = Rot([k.sb(f"Vh{i}", [128, NT * 129], BF16, es) for i in range(2)])
        ptbr = Rot([k.sb(f"ptb{i}", [128, 512], BF16, es) for i in range(3)])
        ptmr = Rot([k.sb(f"ptm{i}", [128, 512], BF16, es) for i in range(3)])
        yabr = Rot([k.sb(f"yab{i}", [128, 8, 128], BF16, es) for i in range(2)])
        rsr = Rot([k.sb(f"rs{i}", [128, 1], F32, es) for i in range(4)])
        sc_att = float(128 ** -0.5)
        mm_i = [0]
        LA = 3
        ptbr = Rot(ptbr.bufs + [k.sb(f"ptbx{i}", [128, 512], BF16, es) for i in range(3)])
        ptmr = Rot(ptmr.bufs + [k.sb(f"ptmx{i}", [128, 512], BF16, es) for i in range(3)])
        items = []
        for h in range(8):
            for i in range(8):
                nkb = (T - OWN) // 128 + i + 1
                for g0 in range(0, nkb, 4):
                    items.append((h, i, g0, min(4, nkb - g0), nkb))
        kv_of = {}

        def ensure_loaded(h):
            if h in kv_of or h >= 8:
                return
            kth, vh = KTh.next(), Vh.next()
            k.dma(k.sp, kth.t[:, :], KTs[:, h, :], [KTS.sub(b_) for b_ in range(8)], [kth], kth)
            k.dma(k.sp, vh.t[:, :], Vs[h], [VS.sub(b_) for b_ in range(8)], [vh], vh)
            kv_of[h] = (kth, vh)

        def emit_S(h, i, g0, n):
            kth = kv_of[h][0]
            ps = psr4.next()
            for jj in range(n):
                kb_ = g0 + jj
                PE([kth, QT], [ps], lambda jj=jj, kb_=kb_: ten.matmul(
                    ps.t[:, jj * 128:(jj + 1) * 128], kth.t[:, kb_ * 128:(kb_ + 1) * 128],
                    QT.t[:, h, i * 128:(i + 1) * 128], start=True, stop=True), inc=(jj == n - 1))
            ptb = ptbr.next()
            A([ps], [ptb], lambda: act.activation(out=ptb.t[:, 0:n * 128], in_=ps.t[:, 0:n * 128], func=AF.Exp, scale=sc_att))
            ptm = ptmr.next()
            mm_i[0] += 1
            usev = (mm_i[0] % 2 == 1)
            e = vec if usev else pool
            fn = lambda: e.tensor_tensor(out=ptm.t[:, 0:n * 128], in0=ptb.t[:, 0:n * 128],
                                         in1=maskT.t[:, i, g0:g0 + n, :].rearrange("p k t -> p (k t)"), op=ALU.mult)
            (V if usev else G)([ptb, maskT.sub(i)], [ptm], fn)
            return ptm

        po_of, ptm_of, yab_of = {}, {}, {}
        ensure_loaded(0)
        for idx in range(len(items) + LA):
            if idx < len(items):
                h, i, g0, n, nkb = items[idx]
                if i == 0 and g0 == 0:
                    ensure_loaded(h)
                ptm_of[idx] = emit_S(h, i, g0, n)
            jx = idx - LA
            if jx < 0:
                continue
            h, i, g0, n, nkb = items[jx]
            vh = kv_of[h][1]
            if g0 == 0:
                po_of[(h, i)] = por.next()
                if i == 0:
                    yab_of[h] = yabr.next()
                    ensure_loaded(h + 1)
            po, ptm, yab = po_of[(h, i)], ptm_of.pop(jx), yab_of[h]
            for jj in range(n):
                kb_ = g0 + jj
                PE([ptm, vh], [po], lambda jj=jj, kb_=kb_: ten.matmul(
                    po.t[:, 0:129], ptm.t[:, jj * 128:(jj + 1) * 128], vh.t[:, kb_ * 129:(kb_ + 1) * 129],
                    start=(kb_ == 0), stop=(kb_ == nkb - 1)), inc=(kb_ == nkb - 1 or jj == n - 1))
            if g0 + n == nkb:
                rs = rsr.next()
                V([po], [rs], lambda: vec.reciprocal(out=rs.t[:, 0:1], in_=po.t[:, 128:129]))
                A([po, rs], [yab], lambda: act.activation(out=yab.t[:, i, :], in_=po.t[:, 0:128], func=AF.Copy,
                                                          scale=rs.t[:, 0:1]))
                if i == 7:
                    tp = tpr.next()
                    for i2 in range(8):
                        PE([yab, ident], [tp], lambda i2=i2: ten.transpose(
                            out=tp.t[:, i2 * 128:(i2 + 1) * 128], in_=yab.t[:, i2, :], identity=ident[:]), inc=(i2 == 7))
                    yat = yatr.next()
                    evac_copy(yat.t[:, :], tp.t[:, :], [tp], [yat])
                    k.dma(k.sp, yTs[:, 8 + h, :], yat.t[:, :], [yat], [YTS.sub(8 + h)], yat)
        k.barrier()
    es_att.close()

    if "pa" in dbg:
        d_ya = dbg_tensor("yaT", [128, 8, OWN], BF16)
        with contextlib.ExitStack() as es:
            t1 = k.sb("dbgya", [128, 8, OWN], BF16, es)
            k.dma(k.sp, t1.t[:, :, :], yTs[:, 8:16, :], [YTS.sub(8 + h) for h in range(8)], [t1], t1)
            k.dma(k.sp, d_ya[:, :, :], t1.t[:, :, :], [t1], [OUT.sub("ya")], t1)
            k.barrier()
        return nc, dbg_out

    for hh in range(2):
        with contextlib.ExitStack() as es:
            W = walloc(es, f"h{hh}")
            S = k.sb("S", [128, 4, 128], F32, es)
            xnTr = Rot([k.sb(f"xnTh{i}", [128, KC, 512], BF16, es) for i in range(2)])
            vbr = Rot([k.sb(f"vb{i}", [128, 4, 512], BF16, es) for i in range(2)])
            s_l = [k.sb(f"s_t{i}", [128, 512], F32, es) for i in range(4)]
            lf_l = [k.sb(f"lf_t{i}", [128, 512], F32, es) for i in range(4)]
            kk_l = [k.sb(f"kk_t{i}", [128, 512], F32, es) for i in range(4)]
            A_l = [k.sb(f"A_t{i}", [128, 512], F32, es) for i in range(4)]
            eNA_l = lf_l
            eA_l = [k.sb(f"eA{i}", [128, 512], F32, es) for i in range(4)]
            sq_l = [k.sb(f"sq_t{i}", [128, 512], F32, es) for i in range(4)]
            ktT_s = [[k.sb(f"ktT{q}{i}", [128, 512], BF16, es) for i in range(4)] for q in range(2)]
            qtT_s = [[k.sb(f"qtT{q}{i}", [128, 512], BF16, es) for i in range(4)] for q in range(2)]
            sm_s = [[k.sb(f"sm{q}{i}", [128, 24], F32, es) for i in range(4)] for q in range(2)]
            ktok_l = [k.sb(f"ktok{i}", [128, 4, 128], BF16, es) for i in range(4)]
            Smid_r = Rot([k.sb(f"Smid{i}", [128, 128], BF16, es) for i in range(2)])
            scm_r = Rot([k.sb(f"scm{i}", [128, 128], BF16, es) for i in range(2)])
            o_sb = k.sb("o_sb", [128, 4, 512], F32, es)
            ss_o = k.sb("ss_o", [128, 16], F32, es)
            tmp_o = k.sb("tmp_o", [128, 16], F32, es)
            junkh = k.sb("junkh", [128, 128], BF16, es)
            sg_r = Rot([k.sb(f"sg{i}", [128, 512], F32, es) for i in range(2)])
            yhb_r = Rot([k.sb(f"yhb{i}", [128, 512], BF16, es) for i in range(2)])
            yht = k.sb("yht", [128, 4, OWN], BF16, es)

            load_w(W[1], w_in, C_HI + hh * 512)
            load_w(W[0], w_in, C_HF + hh * 512)
            load_w(W[2], w_in, C_HQ + hh * 512)
            load_w(W[3], w_in, C_HG + hh * 512)
            for hl in range(4):
                V([], [S.sub(hl)], lambda hl=hl: vec.memset(S.t[:, hl, :], 0.0))

            blkctx = {}

            def stageA1(blk):
                own = blk >= 6
                xnT = xnTr.next()
                k.dma(k.sp, xnT.t[:, :, :].rearrange("p c t -> p (c t)"), xnTs[blk], [XNTS.sub(blk)], [xnT], xnT)
                vb = vbr.next()
                blkctx[blk] = (xnT, vb)
                for j in range(4):
                    ps = psr4.next()
                    for c in range(KC):
                        PE([xnT, W[1]], [ps], lambda c=c, ps=ps, j=j: ten.matmul(
                            ps.t[:, :], xnT.t[:, c, j * 128:(j + 1) * 128], W[1].t[:, c, :],
                            start=(c == 0), stop=(c == KC - 1)), inc=(c == KC - 1))
                    evac_copy(vb.t[:, j, :], ps.t[:, :], [ps], [vb])
                for hl in range(4):
                    ps = psr4.next()
                    for c in range(KC):
                        PE([xnT, W[0]], [ps], lambda c=c, ps=ps, hl=hl: ten.matmul(
                            ps.t[:, :], W[0].t[:, c, hl * 128:(hl + 1) * 128], xnT.t[:, c, :],
                            start=(c == 0), stop=(c == KC - 1)), inc=(c == KC - 1))
                    A([ps], [s_l[hl]], lambda ps=ps, hl=hl: act.activation(out=s_l[hl].t[:, :], in_=ps.t[:, :], func=AF.Sigmoid))
                if own:
                    for hl in range(4):
                        ps = psr4.next()
                        for c in range(KC):
                            PE([xnT, W[2]], [ps], lambda c=c, ps=ps, hl=hl: ten.matmul(
                                ps.t[:, :], W[2].t[:, c, hl * 128:(hl + 1) * 128], xnT.t[:, c, :],
                                start=(c == 0), stop=(c == KC - 1)), inc=(c == KC - 1))
                        A([ps], [sq_l[hl]], lambda ps=ps, hl=hl: act.activation(out=sq_l[hl].t[:, :], in_=ps.t[:, :], func=AF.Silu))

            def stageA2(blk):
                own = blk >= 6
                q = blk % 2
                H4 = range(4)
                for hl in H4:
                    h = hh * 4 + hl
                    A([s_l[hl], oml, lbv], [lf_l[hl]], lambda hl=hl, h=h: act.activation(
                        out=lf_l[hl].t[:, :], in_=s_l[hl].t[:, :], func=AF.Ln, scale=oml.t[:, h:h + 1], bias=lbv.t[:, h:h + 1]))
                for hl in H4:
                    h = hh * 4 + hl
                    V([s_l[hl], oml, noml], [kk_l[hl]], lambda hl=hl, h=h: vec.tensor_scalar(
                        out=kk_l[hl].t[:, :], in0=s_l[hl].t[:, :], scalar1=noml.t[:, h:h + 1], scalar2=oml.t[:, h:h + 1],
                        op0=ALU.mult, op1=ALU.add))
                    V([mreset, lf_l[hl]], [A_l[hl]], lambda hl=hl: vec.tensor_tensor_scan(
                        out=A_l[hl].t[:, :], data0=mreset.t[:, :], data1=lf_l[hl].t[:, :], initial=0.0, op0=ALU.mult, op1=ALU.add))
                Avs = [A_l[hl].t[:, :].rearrange("p (j t) -> p j t", t=128) for hl in H4]
                for hl in H4:
                    sm, Av = sm_s[q][hl], Avs[hl]
                    V([A_l[hl]], [sm], lambda sm=sm, Av=Av: vec.tensor_scalar(out=sm.t[:, 0:4], in0=Av[:, :, 63], scalar1=-1.0,
                                                                              scalar2=None, op0=ALU.mult))
                    V([A_l[hl]], [sm], lambda sm=sm, Av=Av: vec.tensor_tensor(out=sm.t[:, 16:20], in0=Av[:, :, 127], in1=Av[:, :, 63],
                                                                              op=ALU.subtract))
                for hl in H4:
                    sm, Av = sm_s[q][hl], Avs[hl]
                    A([A_l[hl]], [sm], lambda sm=sm, Av=Av: act.activation(out=sm.t[:, 4:8], in_=Av[:, :, 127], func=AF.Exp))
                    A([sm], [sm], lambda sm=sm: act.activation(out=sm.t[:, 8:12], in_=sm.t[:, 16:20], func=AF.Exp))
                    if own:
                        A([A_l[hl]], [sm], lambda sm=sm, Av=Av: act.activation(out=sm.t[:, 12:16], in_=Av[:, :, 63], func=AF.Exp))
                for hl in H4:
                    for j in range(4):
                        A([A_l[hl]], [eNA_l[hl]], lambda j=j, hl=hl: act.activation(
                            out=eNA_l[hl].t[:, j * 128:(j + 1) * 128], in_=A_l[hl].t[:, j * 128:(j + 1) * 128], func=AF.Exp,
                            scale=-1.0, bias=A_l[hl].t[:, j * 128 + 63:j * 128 + 64]))
                    G([kk_l[hl], eNA_l[hl]], [ktT_s[q][hl]], lambda hl=hl: pool.tensor_tensor(
                        out=ktT_s[q][hl].t[:, :], in0=kk_l[hl].t[:, :], in1=eNA_l[hl].t[:, :], op=ALU.mult))
                if own:
                    for hl in H4:
                        sm = sm_s[q][hl]
                        for j in range(4):
                            A([A_l[hl], sm], [eA_l[hl]], lambda j=j, hl=hl, sm=sm: act.activation(
                                out=eA_l[hl].t[:, j * 128:(j + 1) * 128], in_=A_l[hl].t[:, j * 128:(j + 1) * 128], func=AF.Exp,
                                bias=sm.t[:, j:j + 1]))
                        V([sq_l[hl], eA_l[hl]], [qtT_s[q][hl]], lambda hl=hl: vec.scalar_tensor_tensor(
                            out=qtT_s[q][hl].t[:, :], in0=sq_l[hl].t[:, :], scalar=float(128 ** -0.5), in1=eA_l[hl].t[:, :],
                            op0=ALU.mult, op1=ALU.mult))

            def stageB(blk):
                own = blk >= 6
                q = blk % 2
                xnT, vb = blkctx.pop(blk)
                for hl in range(4):
                    ktT = ktT_s[q][hl]
                    tp = tpr.next()
                    for j in range(4):
                        PE([ktT, ident], [tp], lambda j=j, tp=tp, ktT=ktT: ten.transpose(
                            out=tp.t[:, j * 128:(j + 1) * 128], in_=ktT.t[:, j * 128:(j + 1) * 128], identity=ident[:]),
                           inc=(j == 3))
                    evac_copy(ktok_l[hl].t[:, :, :], tp.t[:, 0:512].rearrange("p (j q) -> p j q", j=4), [tp], [ktok_l[hl]])
                for j in range(4):
                    tsl = slice(j * 128, (j + 1) * 128)
                    for hl in range(4):
                        sm, ktT, qtT, ktok = sm_s[q][hl], ktT_s[q][hl], qtT_s[q][hl], ktok_l[hl]
                        vsl = slice(hl * 128, (hl + 1) * 128)
                        if own:
                            Smid, scm = Smid_r.next(), scm_r.next()
                            V([S.sub(hl), sm], [Smid], lambda j=j, Smid=Smid, sm=sm, hl=hl: vec.tensor_scalar(
                                out=Smid.t[:, :], in0=S.t[:, hl, :], scalar1=sm.t[:, 12 + j:13 + j], scalar2=None, op0=ALU.mult))
                            ps = psr4.next()
                            PE([ktT, qtT], [ps], lambda ps=ps, tsl=tsl, ktT=ktT, qtT=qtT: ten.matmul(
                                ps.t[:, 0:128], ktT.t[:, tsl], qtT.t[:, tsl], start=True, stop=True))
                            V([ps, triT], [scm], lambda ps=ps, scm=scm: vec.tensor_tensor(
                                out=scm.t[:, :], in0=ps.t[:, 0:128], in1=triT.t[:, :], op=ALU.mult))
                            po = por.next()
                            PE([scm, vb], [po], lambda po=po, scm=scm, j=j, vsl=vsl: ten.matmul(
                                po.t[:, 0:128], scm.t[:, :], vb.t[:, j, vsl], start=True, stop=False), inc=False)
                            PE([qtT, Smid], [po], lambda po=po, Smid=Smid, tsl=tsl, qtT=qtT: ten.matmul(
                                po.t[:, 0:128], qtT.t[:, tsl], Smid.t[:, :], start=False, stop=True))
                            A([po], [o_sb.sub(j)], lambda po=po, j=j, vsl=vsl: act.copy(out=o_sb.t[:, j, vsl], in_=po.t[:, 0:128]))
                            A([po], [junkh, ss_o], lambda po=po, j=j, hl=hl: act.activation(
                                out=junkh.t[:, :], in_=po.t[:, 0:128], func=AF.Square,
                                accum_out=ss_o.t[:, j * 4 + hl:j * 4 + hl + 1]))
                        pu = psr4.next()
                        PE([ktok, vb], [pu], lambda pu=pu, j=j, ktok=ktok, vsl=vsl: ten.matmul(
                            pu.t[:, 0:128], ktok.t[:, j, :], vb.t[:, j, vsl], start=True, stop=True))
                        V([S.sub(hl), sm], [S.sub(hl)], lambda j=j, sm=sm, hl=hl: vec.tensor_scalar(
                            out=S.t[:, hl, :], in0=S.t[:, hl, :], scalar1=sm.t[:, 4 + j:5 + j], scalar2=None, op0=ALU.mult))
                        V([pu, sm, S.sub(hl)], [S.sub(hl)], lambda pu=pu, j=j, sm=sm, hl=hl: vec.scalar_tensor_tensor(
                            out=S.t[:, hl, :], in0=pu.t[:, 0:128], scalar=sm.t[:, 8 + j:9 + j], in1=S.t[:, hl, :],
                            op0=ALU.mult, op1=ALU.add))
                if own:
                    V([ss_o], [tmp_o], lambda: vec.tensor_scalar(out=tmp_o.t[:, :], in0=ss_o.t[:, :], scalar1=1.0 / 128, scalar2=EPS,
                                                                 op0=ALU.mult, op1=ALU.add))
                    A([tmp_o], [tmp_o], lambda: act.activation(out=tmp_o.t[:, :], in_=tmp_o.t[:, :], func=AF.Sqrt))
                    V([tmp_o], [ss_o], lambda: vec.reciprocal(out=ss_o.t[:, :], in_=tmp_o.t[:, :]))
                    for j in range(4):
                        qt = (blk - 6) * 4 + j
                        sg, yhb = sg_r.next(), yhb_r.next()
                        ps = psr4.next()
                        for c in range(KC):
                            PE([xnT, W[3]], [ps], lambda c=c, ps=ps, j=j: ten.matmul(
                                ps.t[:, :], xnT.t[:, c, j * 128:(j + 1) * 128], W[3].t[:, c, :],
                                start=(c == 0), stop=(c == KC - 1)), inc=(c == KC - 1))
                        A([ps], [sg], lambda ps=ps, sg=sg: act.activation(out=sg.t[:, :], in_=ps.t[:, :], func=AF.Silu))
                        V([sg, on_b4], [sg], lambda sg=sg: vec.tensor_tensor(out=sg.t[:, :], in0=sg.t[:, :], in1=on_b4.t[:, :],
                                                                             op=ALU.mult))
                        for h2 in range(4):
                            v2 = slice(h2 * 128, (h2 + 1) * 128)
                            V([o_sb.sub(j), ss_o, sg], [yhb], lambda j=j, h2=h2, v2=v2, sg=sg, yhb=yhb: vec.scalar_tensor_tensor(
                                out=yhb.t[:, v2], in0=o_sb.t[:, j, v2], scalar=ss_o.t[:, j * 4 + h2:j * 4 + h2 + 1],
                                in1=sg.t[:, v2], op0=ALU.mult, op1=ALU.mult))
                        tp = tpr.next()
                        for h2 in range(4):
                            PE([yhb, ident], [tp], lambda h2=h2, tp=tp, yhb=yhb: ten.transpose(
                                out=tp.t[:, h2 * 128:(h2 + 1) * 128], in_=yhb.t[:, h2 * 128:(h2 + 1) * 128], identity=ident[:]),
                               inc=(h2 == 3))
                        evac_copy(yht.t[:, :, qt * 128:(qt + 1) * 128], tp.t[:, 0:512].rearrange("p (h t) -> p h t", h=4),
                                  [tp], [yht])

            stageA1(0)
            stageA2(0)
            for blk in range(8):
                if blk + 1 < 8:
                    stageA1(blk + 1)
                stageB(blk)
                if blk + 1 < 8:
                    stageA2(blk + 1)
            k.dma(k.sp, yTs[:, hh * 4:(hh + 1) * 4, :], yht.t[:, :, :], [yht], [YTS.sub(f"h{hh}")], yht)
            k.barrier()

    if "ph" in dbg:
        d_yh = dbg_tensor("yhT", [128, 8, OWN], BF16)
        with contextlib.ExitStack() as es:
            t1 = k.sb("dbgyh", [128, 8, OWN], BF16, es)
            k.dma(k.sp, t1.t[:, :, :], yTs[:, 0:8, :], [YTS.sub("h0"), YTS.sub("h1")], [t1], t1)
            k.dma(k.sp, d_yh[:, :, :], t1.t[:, :, :], [t1], [OUT.sub("yh")], t1)
            k.barrier()
        return nc, dbg_out

    hres = k.sb("hres", [128, 8, D], F32)
    with contextlib.ExitStack() as es:
        W = walloc(es, "o")
        yT = k.sb("yT", [128, KC, OWN], BF16, es)
        xrr = Rot([k.sb(f"xres{i}", [128, D], F32, es) for i in range(2)])
        for n in range(4):
            load_w(W[n], w_out, n * 512)
        k.dma(k.sp, yT.t[:, :, :], yTs[:, :, :], [YTS.sub(x_) for x_ in ["h0", "h1"] + [8 + h for h in range(8)]], [yT], yT)
        for qt in range(8):
            xb = xrr.next()
            k.dma(k.sp, xb.t[:, :], xs[(NT - 8 + qt) * 128:(NT - 7 + qt) * 128, :], [], [xb], xb)
            for n in range(4):
                ps = psr.next()
                for c in range(KC):
                    PE([yT, W[n]], [ps], lambda c=c, ps=ps, n=n, qt=qt: ten.matmul(
                        ps.t[:, :], yT.t[:, c, qt * 128:(qt + 1) * 128], W[n].t[:, c, :],
                        start=(c == 0), stop=(c == KC - 1)), inc=(c == KC - 1))
                V([ps, xb], [hres.sub(qt)], lambda ps=ps, xb=xb, n=n, qt=qt: vec.tensor_tensor(
                    out=hres.t[:, qt, n * 512:(n + 1) * 512], in0=ps.t[:, :], in1=xb.t[:, n * 512:(n + 1) * 512], op=ALU.add))
        k.barrier()

    with contextlib.ExitStack() as es:
        W = walloc(es, "x")
        Wo4 = W[3].t[:, :, :].rearrange("p c n -> p (c n)").rearrange("p (c n) -> p c n", c=4)
        gc_b = k.sb("gc_b", [128, D], F32, es)
        gm_b = gc_b
        gxq_b = k.sb("gxq_b", [128, 128], F32, es)
        gxk_b = k.sb("gxk_b", [128, 128], F32, es)
        hnTr = Rot([k.sb(f"hnTx{i}", [128, KC, 128], BF16, es) for i in range(2)])
        memT = k.sb("memT", [128, KC, 256], BF16, es)
        kTx = k.sb("kTx", [128, 4, 256], BF16, es)
        vaug = k.sb("vaug", [128, 2, 4, 129], BF16, es)
        xnbr = Rot([k.sb(f"xnbx{i}", [128, D], BF16, es) for i in range(2)])
        mst = Rot([k.sb(f"mst{i}", [128, D], F32, es) for i in range(1)])
        junk = k.sb("junkx", [128, 128], BF16, es)
        ssr = Rot([k.sb(f"ssx{i}", [128, 8], F32, es) for i in range(4)])
        tmpr = Rot([k.sb(f"tmpx{i}", [128, 8], F32, es) for i in range(4)])
        knbr = Rot([k.sb(f"knbx{i}", [128, 4, 128], BF16, es) for i in range(2)])
        ptxr = Rot([k.sb(f"ptx{i}", [128, 256], BF16, es) for i in range(3)])
        obr = Rot([k.sb(f"obx{i}", [128, 512], BF16, es) for i in range(3)])
        rsr = Rot([k.sb(f"rsx{i}", [128, 1], F32, es) for i in range(4)])
        load_w(W[0], wq_x, 0)
        load_w(W[1], wk_x, 0)
        load_w(W[2], wv_x, 0)
        for c4 in range(4):
            k.dma(k.pool, Wo4[:, c4, :], wo_x[c4 * 128:(c4 + 1) * 128, :], [], [W[3]], W[3], max_dma_last_dim=4096)
        bcast_row(gm_b, mem_norm[0:1, :], D)
        bcast_row(gxq_b, xqn[0:1, :], 128)
        bcast_row(gxk_b, xkn[0:1, :], 128)
        V([], [vaug], lambda: vec.memset(vaug.t[:, :, :, 128:129], 1.0))
        for mt in range(2):
            mb = mst.next()
            k.dma(k.sp, mb.t[:, :], memb[mt * 128:(mt + 1) * 128, :], [], [mb], mb)
            norm_tile_to_T(mb.t[:, :], mb, gm_b, xnbr.next(), memT, mt * 128, ssr.next(), tmpr.next(), junk)
        for mt in range(2):
            headnorm_T(lambda c, mt=mt: memT.t[:, c, mt * 128:(mt + 1) * 128], memT, [W[1]], gxk_b, kTx, mt * 128,
                       ssr, tmpr, knbr, junk)
            ps = psr.next()
            for c in range(KC):
                PE([memT, W[2]], [ps], lambda c=c, ps=ps, mt=mt: ten.matmul(
                    ps.t[:, :], memT.t[:, c, mt * 128:(mt + 1) * 128], W[2].t[:, c, :],
                    start=(c == 0), stop=(c == KC - 1)), inc=(c == KC - 1))
            evac_copy(vaug.t[:, mt, :, 0:128], ps.t[:, :].rearrange("p (h d) -> p h d", h=4), [ps], [vaug])
        sc_att = float(128 ** -0.5)
        bcast_row(gc_b, norm_cross[0:1, :], D)
        qTxa = k.sb("qTxa", [128, 4, OWN], BF16, es)
        oTxa = k.sb("oTxa", [128, 4, OWN], BF16, es)
        xnbr = Rot(xnbr.bufs + [k.sb("xnbx2", [128, D], BF16, es)])
        knbr = Rot(knbr.bufs + [k.sb("knbx2", [128, 4, 128], BF16, es)])

        def x_part1(qt):
            xnb = xnbr.next()
            norm_part1(hres.t[:, qt, :], hres.sub(qt), gc_b, xnb, ssr.next(), tmpr.next(), junk)
            return xnb

        def x_T(qt, xnb):
            hnT = hnTr.next()
            norm_part2(xnb, hnT, 0)
            return hnT

        xnb_q = {0: x_part1(0), 1: x_part1(1)}
        hn_q = {0: x_T(0, xnb_q.pop(0))}
        qb_prev = None
        for qt in range(8):
            if qt + 2 < 8:
                xnb_q[qt + 2] = x_part1(qt + 2)
            if qt + 1 < 8:
                hn_q[qt + 1] = x_T(qt + 1, xnb_q.pop(qt + 1))
            hnT = hn_q.pop(qt)
            qb = headnorm_T(lambda c, hnT=hnT: hnT.t[:, c, :], hnT, [W[0]], gxq_b, qTxa, qt * 128,
                            ssr, tmpr, knbr, junk, defer=True)
            if qb_prev is not None:
                qb_prev()
            qb_prev = qb
        qb_prev()

        xitems = [(qt, h) for qt in range(8) for h in range(4)]
        ptxr = Rot(ptxr.bufs + [k.sb(f"ptxx{i}", [128, 256], BF16, es) for i in range(2)])
        pt_of, ob_of = {}, {}
        LAX = 2
        for idx in range(len(xitems) + LAX):
            if idx < len(xitems):
                qt, h = xitems[idx]
                ps = psr4.next()
                for mt in range(2):
                    PE([kTx, qTxa], [ps], lambda ps=ps, mt=mt, h=h, qt=qt: ten.matmul(
                        ps.t[:, mt * 128:(mt + 1) * 128], kTx.t[:, h, mt * 128:(mt + 1) * 128],
                        qTxa.t[:, h, qt * 128:(qt + 1) * 128], start=True, stop=True), inc=(mt == 1))
                pt = ptxr.next()
                A([ps], [pt], lambda ps=ps, pt=pt: act.activation(out=pt.t[:, :], in_=ps.t[:, 0:256], func=AF.Exp, scale=sc_att))
                pt_of[idx] = pt
            jx = idx - LAX
            if jx < 0:
                continue
            qt, h = xitems[jx]
            pt = pt_of.pop(jx)
            if h == 0:
                ob_of[qt] = obr.next()
            ob = ob_of[qt]
            po = por.next()
            for mt in range(2):
                PE([pt, vaug], [po], lambda po=po, pt=pt, mt=mt, h=h: ten.matmul(
                    po.t[:, 0:129], pt.t[:, mt * 128:(mt + 1) * 128], vaug.t[:, mt, h, :],
                    start=(mt == 0), stop=(mt == 1)), inc=(mt == 1))
            rs = rsr.next()
            V([po], [rs], lambda rs=rs, po=po: vec.reciprocal(out=rs.t[:, 0:1], in_=po.t[:, 128:129]))
            A([po, rs], [ob], lambda rs=rs, po=po, h=h, ob=ob: act.activation(
                out=ob.t[:, h * 128:(h + 1) * 128], in_=po.t[:, 0:128], func=AF.Copy, scale=rs.t[:, 0:1]))
            if h == 3:
                tp = tpr.next()
                for h2 in range(4):
                    PE([ob, ident], [tp], lambda h2=h2, tp=tp, ob=ob: ten.transpose(
                        out=tp.t[:, h2 * 128:(h2 + 1) * 128], in_=ob.t[:, h2 * 128:(h2 + 1) * 128], identity=ident[:]), inc=(h2 == 3))
                evac_copy(oTxa.t[:, :, qt * 128:(qt + 1) * 128], tp.t[:, 0:512].rearrange("p (h t) -> p h t", h=4), [tp],
                          [oTxa.sub(qt)])

        for qt in range(8):
            for n in range(4):
                ps = psr4.next()
                for c in range(4):
                    PE([oTxa.sub(qt), W[3]], [ps], lambda c=c, ps=ps, n=n, qt=qt: ten.matmul(
                        ps.t[:, :], oTxa.t[:, c, qt * 128:(qt + 1) * 128], Wo4[:, c, n * 512:(n + 1) * 512],
                        start=(c == 0), stop=(c == 3)), inc=(c == 3))
                V([ps, hres.sub(qt)], [hres.sub(qt)], lambda ps=ps, n=n, qt=qt: vec.tensor_tensor(
                    out=hres.t[:, qt, n * 512:(n + 1) * 512], in0=ps.t[:, :], in1=hres.t[:, qt, n * 512:(n + 1) * 512], op=ALU.add))
        k.barrier()

    with contextlib.ExitStack() as es:
        gm_b = k.sb("gmlp_b", [128, D], F32, es)
        hnT = k.sb("hnTm", [128, KC, OWN], BF16, es)
        xnbr = Rot([k.sb(f"xnbm{i}", [128, D], BF16, es) for i in range(2)])
        junk = k.sb("junkm", [128, 128], BF16, es)
        ssr = Rot([k.sb(f"ssm{i}", [128, 8], F32, es) for i in range(4)])
        tmpr = Rot([k.sb(f"tmpm{i}", [128, 8], F32, es) for i in range(4)])
        Wur = Rot([k.sb(f"Wu{i}", [128, KC, 512], BF16, es) for i in range(2)])
        Wdr = Rot([k.sb(f"Wd{i}", [128, 4, D], BF16, es) for i in range(2)])
        actr = Rot([k.sb(f"actT{i}", [128, 4, OWN], BF16, es) for i in range(2)])
        rlr = Rot([k.sb(f"rl{i}", [128, 512], F32, es) for i in range(3)])
        bcast_row(gm_b, norm_mlp[0:1, :], D)
        NG = 16

        def load_group(g):
            wu, wd = Wur.next(), Wdr.next()
            load_w(wu, w_up, g * 512)
            for c4 in range(4):
                k.dma(k.pool, wd.t[:, c4, :], w_down[g * 512 + c4 * 128:g * 512 + (c4 + 1) * 128, :],
                      [], [wd], wd, max_dma_last_dim=4096)
            return wu, wd

        nxt = load_group(0)
        for qt in range(8):
            norm_tile_to_T(hres.t[:, qt, :], hres.sub(qt), gm_b, xnbr.next(), hnT, qt * 128, ssr.next(), tmpr.next(), junk)
        sq_i = [0]
        for g in range(NG):
            wu, wd = nxt
            if g + 1 < NG:
                nxt = load_group(g + 1)
            actT = actr.next()
            for f in range(4):
                for th in range(2):
                    ps = psr.next()
                    for c in range(KC):
                        PE([hnT, wu], [ps], lambda c=c, ps=ps, f=f, th=th, wu=wu: ten.matmul(
                            ps.t[:, :], wu.t[:, c, f * 128:(f + 1) * 128], hnT.t[:, c, th * 512:(th + 1) * 512],
                            start=(c == 0), stop=(c == KC - 1)), inc=(c == KC - 1))
                    rl = rlr.next()
                    A([ps], [rl], lambda ps=ps, rl=rl: act.activation(out=rl.t[:, :], in_=ps.t[:, :], func=AF.Relu))
                    sq_i[0] += 1
                    if sq_i[0] % 2:
                        V([rl], [actT], lambda rl=rl, actT=actT, f=f, th=th: vec.tensor_tensor(
                            out=actT.t[:, f, th * 512:(th + 1) * 512], in0=rl.t[:, :], in1=rl.t[:, :], op=ALU.mult))
                    else:
                        G([rl], [actT], lambda rl=rl, actT=actT, f=f, th=th: pool.tensor_tensor(
                            out=actT.t[:, f, th * 512:(th + 1) * 512], in0=rl.t[:, :], in1=rl.t[:, :], op=ALU.mult))
            for qt in range(8):
                for n in range(4):
                    ps = psr.next()
                    for f in range(4):
                        PE([actT, wd], [ps], lambda f=f, ps=ps, n=n, qt=qt, wd=wd, actT=actT: ten.matmul(
                            ps.t[:, :], actT.t[:, f, qt * 128:(qt + 1) * 128], wd.t[:, f, n * 512:(n + 1) * 512],
                            start=(f == 0), stop=(f == 3)), inc=(f == 3))
                    V([ps, hres.sub(qt)], [hres.sub(qt)], lambda ps=ps, n=n, qt=qt: vec.tensor_tensor(
                        out=hres.t[:, qt, n * 512:(n + 1) * 512], in0=ps.t[:, :], in1=hres.t[:, qt, n * 512:(n + 1) * 512], op=ALU.add))
        for qt in range(8):
            k.dma(k.sp, out[qt * 128:(qt + 1) * 128, :], hres.t[:, qt, :], [hres.sub(qt)], [OUT.sub(qt)], hres.sub(qt))
        k.barrier()
    return nc, dbg_out


def make_in_maps(inputs):
    x = np.asarray(inputs["x"], dtype=np.float32)
    mem = np.asarray(inputs["mem"], dtype=np.float32)
    shared = {
        "norm_mix": inputs["norm_mix"][0:1], "w_in": inputs["w_in"][0], "lbl": inputs["hgrn_lb_logits"],
        "onorm": inputs["hgrn_onorm"][0:1], "qnorm": inputs["attn_qnorm"][0:1], "knorm": inputs["attn_knorm"][0:1],
        "w_out": inputs["w_out"][0], "norm_cross": inputs["norm_cross"][0:1], "mem_norm": inputs["mem_norm"][0:1],
        "wq_x": inputs["wq_x"][0], "wk_x": inputs["wk_x"][0], "wv_x": inputs["wv_x"][0], "wo_x": inputs["wo_x"][0],
        "xqn": inputs["xq_norm"][0:1], "xkn": inputs["xk_norm"][0:1], "norm_mlp": inputs["norm_mlp"][0:1],
        "w_up": inputs["w_up"][0], "w_down": inputs["w_down"][0],
    }
    shared = {n: np.ascontiguousarray(np.asarray(v, dtype=np.float32)) for n, v in shared.items()}
    maps = []
    for c in range(8):
        b, q = c // 4, c % 4
        npad = (3 - q) * OWN
        xw = np.zeros((T, D), np.float32)
        xw[npad:] = x[b, :(q + 1) * OWN]
        kb = np.zeros((1, T - OWN), np.float32)
        kb[0, :npad] = NEG
        m = dict(shared)
        m["xs"] = xw
        m["keybias"] = kb
        m["mem"] = np.ascontiguousarray(mem[b])
        maps.append(m)
    return maps


def kernel(**inputs):
    nc, _ = build()
    res = run_bass_kernel_spmd(nc, make_in_maps(inputs), core_ids=list(range(8)))
    outp = np.zeros((2, 4096, D), np.float32)
    for c in range(8):
        b, q = c // 4, c % 4
        outp[b, q * OWN:(q + 1) * OWN] = res.results[c]["out"]
    return outp
```

```python
import contextlib
import numpy as np
import concourse.bass as bass
import concourse.mybir as mybir
from concourse.bass_utils import run_bass_kernel_spmd

F32 = mybir.dt.float32
BF16 = mybir.dt.bfloat16
I32 = mybir.dt.int32
AF = mybir.ActivationFunctionType
ALU = mybir.AluOpType
AX = mybir.AxisListType

D = 2048
T = 4096
OWN = 1024
NT = T // 128
KC = D // 128
EPS = 1e-6
NEG = -1.0e30
C_HQ, C_HF, C_HI, C_HG, C_AQ, C_AK, C_AV, C_IQ, C_IK, C_IW = 0, 1024, 2048, 3072, 4096, 5120, 6144, 7168, 8192, 8256
NBIS = 14


class Sem:
    _n = 0

    def __init__(self, h):
        self.h = h
        Sem._n += 1
        self.uid = Sem._n


class Res:
    def __init__(self, name):
        self.name = name
        self.lw = None
        self.rd = {}
        self.dsem = None
        self.dcount = 0


class Buf:
    def __init__(self, t, name):
        self.t = t
        self.r = Res(name)
        self.name = name
        self._subs = {}

    def sub(self, key):
        if key not in self._subs:
            self._subs[key] = Res(f"{self.name}.{key}")
        return self._subs[key]

    def __getitem__(self, idx):
        return self.t[idx]


class Eng:
    def __init__(self, kb, name, eng, pe=False):
        self.kb = kb
        self.name = name
        self.eng = eng
        self.pe = pe
        self.sem = kb.newsem("e_" + name)
        self.count = 0
        self.seen = {}


class KB:
    def __init__(self):
        self.nc = bass.Bass("TRN2", target_bir_lowering=False)
        self.es = contextlib.ExitStack()
        self.nsem = 0
        nc = self.nc
        self.pe = Eng(self, "pe", nc.tensor, pe=True)
        self.act = Eng(self, "act", nc.scalar)
        self.dve = Eng(self, "dve", nc.vector)
        self.pool = Eng(self, "pool", nc.gpsimd)
        self.sp = Eng(self, "sp", nc.sync)
        self.engs = [self.pe, self.act, self.dve, self.pool, self.sp]
        self.dma_owners = []
        self.ninst = 0

    def newsem(self, name):
        self.nsem += 1
        return Sem(self.es.enter_context(self.nc.semaphore(f"{name}_{self.nsem}")))

    def dram(self, name, shape, dt, kind="Internal"):
        return self.nc.dram_tensor(name, list(shape), dt, kind=kind).ap()

    def sb(self, name, shape, dt, es=None):
        es = es or self.es
        self.nsb = getattr(self, "nsb", 0) + 1
        name = f"{name}_{self.nsb}"
        return Buf(es.enter_context(self.nc.sbuf_tensor(name, list(shape), dt)), name)

    def ps(self, name, shape, dt, es=None):
        es = es or self.es
        return Buf(es.enter_context(self.nc.psum_tensor(name, list(shape), dt)), name)

    @staticmethod
    def _res(x):
        return x.r if isinstance(x, Buf) else x

    def _waits(self, E, reads, writes):
        deps = {}

        def add(d):
            if d is None:
                return
            s, v = d
            if s.uid not in deps or deps[s.uid][1] < v:
                deps[s.uid] = (s, v)

        for r in reads:
            add(self._res(r).lw)
        for w in writes:
            w = self._res(w)
            add(w.lw)
            for d in w.rd.values():
                add(d)
        for uid, (s, v) in deps.items():
            if E.pe and s is E.sem:
                continue
            if E.seen.get(uid, 0) >= v:
                continue
            E.eng.wait_ge(s.h, v)
            E.seen[uid] = v

    def _commit(self, dep, reads, writes):
        s, v = dep
        for r in reads:
            r = self._res(r)
            if s.uid not in r.rd or r.rd[s.uid][1] < v:
                r.rd[s.uid] = dep
        for w in writes:
            w = self._res(w)
            w.lw = dep
            w.rd = {}

    def op(self, E, reads, writes, fn, inc=True):
        self._waits(E, reads, writes)
        inst = fn()
        self.ninst += 1
        if inc:
            E.count += 1
            inst.then_inc(E.sem.h, 1)
            idx = E.count
        else:
            idx = E.count + 1
        self._commit((E.sem, idx), reads, writes)

    def dma(self, Q, out_ap, in_ap, reads, writes, owner, **kw):
        owner = self._res(owner)
        self._waits(Q, reads, writes)
        if owner.dsem is None:
            owner.dsem = self.newsem("d_" + owner.name)
            self.dma_owners.append(owner)
        owner.dcount += 16
        Q.eng.dma_start(out=out_ap, in_=in_ap, **kw).then_inc(owner.dsem.h, 16)
        self.ninst += 1
        self._commit((owner.dsem, owner.dcount), reads, writes)

    def barrier(self):
        M = self.act
        for E in self.engs:
            if E is M or E.count == 0:
                continue
            if M.seen.get(E.sem.uid, 0) < E.count:
                M.eng.wait_ge(E.sem.h, E.count)
                M.seen[E.sem.uid] = E.count
        for o in self.dma_owners:
            if o.dcount and M.seen.get(o.dsem.uid, 0) < o.dcount:
                M.eng.wait_ge(o.dsem.h, o.dcount)
                M.seen[o.dsem.uid] = o.dcount
        if M.seen.get(M.sem.uid, 0) < M.count:
            M.eng.wait_ge(M.sem.h, M.count)
            M.seen[M.sem.uid] = M.count
        M.count += 1
        M.eng.activation(out=self.bar_t[:, 0:1], in_=self.bar_t[:, 1:2], func=AF.Copy).then_inc(M.sem.h, 1)
        for E in self.engs:
            if E is M:
                continue
            E.eng.wait_ge(M.sem.h, M.count)
            E.seen[M.sem.uid] = M.count
            for E2 in self.engs:
                E.seen[E2.sem.uid] = max(E.seen.get(E2.sem.uid, 0), E2.count if E2 is not M else M.count)
            for o in self.dma_owners:
                E.seen[o.dsem.uid] = max(E.seen.get(o.dsem.uid, 0), o.dcount)
        for E2 in self.engs:
            M.seen[E2.sem.uid] = max(M.seen.get(E2.sem.uid, 0), E2.count)

    def V(self, reads, writes, fn):
        self.op(self.dve, reads, writes, fn)

    def A(self, reads, writes, fn):
        self.op(self.act, reads, writes, fn)

    def G(self, reads, writes, fn):
        self.op(self.pool, reads, writes, fn)

    def PE(self, reads, writes, fn, inc=True):
        self.op(self.pe, reads, writes, fn, inc=inc)


class Rot:
    def __init__(self, bufs):
        self.bufs = bufs
        self.i = 0

    def next(self):
        b = self.bufs[self.i % len(self.bufs)]
        self.i += 1
        return b


def build(dbg=None):
    k = KB()
    nc = k.nc
    V, A, G, PE = k.V, k.A, k.G, k.PE
    vec, act, pool, ten = nc.vector, nc.scalar, nc.gpsimd, nc.tensor
    dbg = dbg or ()

    def din(name, shape, dt=F32):
        return k.dram(name, shape, dt, kind="ExternalInput")

    xs = din("xs", [T, D])
    keybias = din("keybias", [1, T - OWN])
    memb = din("mem", [256, D])
    norm_mix = din("norm_mix", [1, D])
    w_in = din("w_in", [D, 8272])
    lbl = din("lbl", [2, 1024])
    onorm = din("onorm", [1, 128])
    qnorm = din("qnorm", [1, 128])
    knorm = din("knorm", [1, 128])
    w_out = din("w_out", [D, D])
    norm_cross = din("norm_cross", [1, D])
    mem_norm = din("mem_norm", [1, D])
    wq_x = din("wq_x", [D, 512])
    wk_x = din("wk_x", [D, 512])
    wv_x = din("wv_x", [D, 512])
    wo_x = din("wo_x", [512, D])
    xqn = din("xqn", [1, 128])
    xkn = din("xkn", [1, 128])
    norm_mlp = din("norm_mlp", [1, D])
    w_up = din("w_up", [D, 8192])
    w_down = din("w_down", [8192, D])
    out = k.dram("out", [OWN, D], F32, kind="ExternalOutput")
    OUT = Buf(None, "OUT")

    xnTs = k.dram("xnTs", [8, 128, KC * 512], BF16)
    XNTS = Buf(None, "xnTs")
    KTs = k.dram("KTs", [128, 8, T], BF16)
    KTS = Buf(None, "KTs")
    Vs = k.dram("Vs", [8, 128, NT * 129], BF16)
    VS = Buf(None, "Vs")

    yTs = k.dram("yTs", [128, KC, OWN], BF16)
    YTS = Buf(None, "yTs")

    dbg_out = {}

    def dbg_tensor(name, shape, dt=F32):
        dbg_out[name] = k.dram("dbg_" + name, shape, dt, kind="ExternalOutput")
        return dbg_out[name]

    k.bar_t = k.sb("bar_t", [128, 2], F32).t
    ident = k.sb("ident", [128, 128], BF16)
    iota_i = k.sb("iota_i", [128, 128], I32)
    def walloc(es, tag):
        return [k.sb(f"W{tag}{i}", [128, KC, 512], BF16, es) for i in range(4)]
    PSB = [k.ps(f"psb{i}", [128, 512], F32) for i in range(6)]
    psr4 = Rot(PSB[0:4])
    por = Rot(PSB[4:6])
    PTP = [k.ps(f"ptp{i}", [128, 1024], BF16) for i in range(2)]
    psr = Rot(PSB)
    tpr = Rot(PTP)
    evac_i = [0]

    def evac_copy(out_ap, in_ap, reads, writes):
        evac_i[0] += 1
        if evac_i[0] % 2:
            A(reads, writes, lambda: act.copy(out=out_ap, in_=in_ap))
        else:
            V(reads, writes, lambda: vec.tensor_copy(out=out_ap, in_=in_ap))

    nc.vector.memset(k.bar_t[:], 0.0)
    G([], [iota_i], lambda: pool.iota(out=iota_i[:], pattern=[[1, 128]], base=0, channel_multiplier=-1))
    V([iota_i], [ident], lambda: vec.tensor_single_scalar(out=ident[:], in_=iota_i[:], scalar=0.0, op=ALU.is_equal))

    l01 = k.sb("l01", [128, 2, 8], F32)
    lbv = k.sb("lbv", [128, 8], F32)
    oml = k.sb("oml", [128, 8], F32)
    noml = k.sb("noml", [128, 8], F32)
    mreset = k.sb("mreset", [128, 512], F32)
    triT = k.sb("triT", [128, 128], F32)
    on_b4 = k.sb("on_b4", [128, 512], F32)
    with contextlib.ExitStack() as es0:
        mri = k.sb("mri", [128, 512], I32, es0)
        k.dma(k.sp, l01.t[:, :, :], lbl.rearrange("r (h q) -> q r h", q=128), [], [l01], l01,
              allow_slow_non_contiguous=True)
        for i4 in range(4):
            k.dma(k.sp, on_b4.t[:, i4 * 128:(i4 + 1) * 128], onorm[0:1, :].partition_broadcast(128), [], [on_b4], on_b4)
        V([l01], [lbv], lambda: vec.tensor_tensor(out=lbv.t[:, :], in0=l01.t[:, 0, :], in1=l01.t[:, 1, :], op=ALU.subtract))
        A([lbv], [lbv], lambda: act.activation(out=lbv.t[:, :], in_=lbv.t[:, :], func=AF.Sigmoid))
        V([lbv], [oml], lambda: vec.tensor_scalar(out=oml.t[:, :], in0=lbv.t[:, :], scalar1=-1.0, scalar2=1.0,
                                                  op0=ALU.mult, op1=ALU.add))
        V([lbv], [noml], lambda: vec.tensor_scalar(out=noml.t[:, :], in0=lbv.t[:, :], scalar1=-1.0, scalar2=None,
                                                   op0=ALU.add))
        G([], [mri], lambda: pool.iota(out=mri.t[:, :].rearrange("p (j t) -> p j t", t=128), pattern=[[0, 4], [1, 128]],
                                       base=0, channel_multiplier=0))
        V([mri], [mreset], lambda: vec.tensor_single_scalar(out=mreset.t[:, :], in_=mri.t[:, :], scalar=0.0, op=ALU.is_gt))
        V([iota_i], [triT], lambda: vec.tensor_single_scalar(out=triT.t[:, :], in_=iota_i.t[:, :], scalar=0.0, op=ALU.is_ge))
        k.barrier()

    def load_w(slot, src2d, c0, ncols=512, rows=D):
        kc = rows // 128
        src = src2d.rearrange("(c p) n -> p c n", p=128)[:, :, c0:c0 + ncols]
        k.dma(k.pool, slot.t[:, 0:kc, 0:ncols], src, [], [slot], slot, max_dma_last_dim=4096)

    def bcast_row(dst, src_row, n):
        k.dma(k.sp, dst.t[:, 0:n], src_row.partition_broadcast(128), [], [dst], dst)

    def rstd_from_ss(ss, n, width, es_bufs):
        tmp = es_bufs
        V([ss], [tmp], lambda: vec.tensor_scalar(out=tmp[:, 0:n], in0=ss[:, 0:n], scalar1=1.0 / width, scalar2=EPS,
                                                  op0=ALU.mult, op1=ALU.add))
        A([tmp], [tmp], lambda: act.activation(out=tmp[:, 0:n], in_=tmp[:, 0:n], func=AF.Sqrt))
        V([tmp], [ss], lambda: vec.reciprocal(out=ss[:, 0:n], in_=tmp[:, 0:n]))

    def norm_part1(src_ap, src_res, gain_b, xnb, ss, tmp, junk):
        A([src_res], [xnb, ss], lambda: act.activation(out=xnb[:, 0:D], in_=src_ap, func=AF.Square,
                                                      accum_out=ss[:, 0:1]))
        rstd_from_ss(ss, 1, D, tmp)
        V([src_res, ss, gain_b], [xnb], lambda: vec.scalar_tensor_tensor(
            out=xnb[:, :], in0=src_ap, scalar=ss[:, 0:1], in1=gain_b[:, :], op0=ALU.mult, op1=ALU.mult))

    def norm_part2(xnb, dstT, col0, dst_res=None):
        dst_res = dst_res if dst_res is not None else dstT
        for g in range(KC // 8):
            tp = tpr.next()
            for j in range(8):
                c = g * 8 + j
                PE([xnb, ident], [tp], lambda c=c, j=j: ten.transpose(
                    out=tp.t[:, j * 128:(j + 1) * 128], in_=xnb[:, c * 128:(c + 1) * 128], identity=ident[:]),
                   inc=(j == 7))
            evac_copy(dstT.t[:, g * 8:(g + 1) * 8, col0:col0 + 128],
                      tp.t[:, :].rearrange("p (c t) -> p c t", c=8), [tp], [dst_res])

    def norm_tile_to_T(src_ap, src_res, gain_b, xnb, dstT, col0, ss, tmp, junk):
        norm_part1(src_ap, src_res, gain_b, xnb, ss, tmp, junk)
        norm_part2(xnb, dstT, col0)

    def headnorm_T(lhs_fn, lhs_res, Wlist, g_b, dst, dcol0, ssr, tmpr, knbr, junk, ncc=KC, defer=False):
        nh = 4 * len(Wlist)
        ss = ssr.next()
        tmp = tmpr.next()
        knb = knbr.next()
        kps = []
        for cg, Wc in enumerate(Wlist):
            ps = psr.next()
            kps.append(ps)
            for c in range(ncc):
                PE([lhs_res, Wc], [ps], lambda c=c, Wc=Wc, ps=ps: ten.matmul(
                    ps.t[:, :], lhs_fn(c), Wc.t[:, c, :],
                    start=(c == 0), stop=(c == ncc - 1)), inc=(c == ncc - 1))
            for hh in range(4):
                h = cg * 4 + hh
                A([ps], [junk, ss], lambda ps=ps, hh=hh, h=h: act.activation(
                    out=junk[:, 0:128], in_=ps.t[:, hh * 128:(hh + 1) * 128], func=AF.Square,
                    accum_out=ss[:, h:h + 1]))
        rstd_from_ss(ss, nh, 128, tmp)
        for cg in range(len(Wlist)):
            ps = kps[cg]
            for hh in range(4):
                h = cg * 4 + hh
                V([ps, ss, g_b], [knb], lambda ps=ps, hh=hh, h=h: vec.scalar_tensor_tensor(
                    out=knb.t[:, h, :], in0=ps.t[:, hh * 128:(hh + 1) * 128], scalar=ss[:, h:h + 1],
                    in1=g_b[:, :], op0=ALU.mult, op1=ALU.mult))
        def part_b():
            tp = tpr.next()
            for h in range(nh):
                PE([knb, ident], [tp], lambda h=h: ten.transpose(
                    out=tp.t[:, h * 128:(h + 1) * 128], in_=knb.t[:, h, :], identity=ident[:]), inc=(h == nh - 1))
            evac_copy(dst.t[:, :, dcol0:dcol0 + 128], tp.t[:, 0:nh * 128].rearrange("p (h t) -> p h t", h=nh), [tp], [dst])

        if defer:
            return part_b
        part_b()

    es_att = contextlib.ExitStack()
    kidxT = [k.sb(f"kidxT{v_}", [128, T], BF16, es_att) for v_ in range(2)]
    V([], [kidxT[0]], lambda: vec.memset(kidxT[0].t[64:128, :], 0.0))
    V([], [kidxT[1]], lambda: vec.memset(kidxT[1].t[0:64, :], 0.0))
    kbias_b = k.sb("kbias_b", [128, T - OWN], BF16, es_att)
    tri_neg = k.sb("tri_neg", [128, 128], F32, es_att)
    cpow = k.sb("cpow", [128, NBIS + 2], F32, es_att)
    k.dma(k.pool, kbias_b.t[:, :], keybias[0:1, :].partition_broadcast(128), [], [kbias_b], kbias_b,
          max_dma_last_dim=4096)
    V([iota_i], [tri_neg], lambda: vec.tensor_scalar(out=tri_neg[:, :], in0=iota_i[:, :], scalar1=0.0, scalar2=NEG,
                                                     op0=ALU.is_gt, op1=ALU.mult))
    for kk_ in range(NBIS + 2):
        V([], [cpow], lambda kk_=kk_: vec.memset(cpow.t[:, kk_:kk_ + 1], float(2.0 ** -kk_)))
    with contextlib.ExitStack() as es:
        W = walloc(es, "a")
        Wik = k.sb("Wik", [128, KC, 128], BF16, es)
        gain_b = k.sb("gain_b", [128, D], F32, es)
        gK_b = k.sb("gK_b", [128, 128], F32, es)
        xst = Rot([k.sb(f"xst{i}", [128, D], F32, es) for i in range(2)])
        junk = k.sb("junk", [128, 128], BF16, es)
        ssr = Rot([k.sb(f"ss{i}", [128, 8], F32, es) for i in range(4)])
        tmpr = Rot([k.sb(f"tmp{i}", [128, 8], F32, es) for i in range(4)])
        xnTr = Rot([k.sb(f"xnT{i}", [128, KC, 512], BF16, es) for i in range(2)])
        Kst = Rot([k.sb(f"Kst{i}", [128, 8, 512], BF16, es) for i in range(2)])
        Vst = Rot([k.sb(f"Vst{i}", [128, 8, 4, 129], BF16, es) for i in range(2)])
        knbr = Rot([k.sb(f"knb{i}", [128, 8, 128], BF16, es) for i in range(2)])

        for i in range(2):
            load_w(W[i], w_in, C_AK + i * 512)
        for i in range(2):
            load_w(W[2 + i], w_in, C_AV + i * 512)
        for half in range(2):
            src = w_in.rearrange("(c p) n -> p c n", p=128)[:, :, C_IK:C_IK + 64]
            k.dma(k.pool, Wik.t[:, :, half * 64:(half + 1) * 64], src, [], [Wik], Wik, max_dma_last_dim=4096)
        bcast_row(gain_b, norm_mix[0:1, :], D)
        bcast_row(gK_b, knorm[0:1, :], 128)
        for vb in Vst.bufs:
            V([], [vb], lambda vb=vb: vec.memset(vb.t[:, :, :, 128:129], 1.0))

        xnbr = Rot([k.sb(f"xnbp{i}", [128, D], BF16, es) for i in range(3)])

        def p1_part1(tile):
            xb = xst.next()
            k.dma(k.sp, xb.t[:, :], xs[tile * 128:(tile + 1) * 128, :], [], [xb], xb)
            xnb = xnbr.next()
            norm_part1(xb.t[:, :], xb, gain_b, xnb, ssr.next(), tmpr.next(), junk)
            return xnb

        blkbuf = {}

        def bufs_of(blk):
            if blk not in blkbuf:
                blkbuf[blk] = (xnTr.next(), Kst.next(), Vst.next())
            return blkbuf[blk]

        def p1_T(tile, xnb):
            xnT = bufs_of(tile // 4)[0]
            norm_part2(xnb, xnT, (tile % 4) * 128, dst_res=xnT.sub(tile % 4))

        def p1_block_stores(blk):
            xnT, kst, vst = bufs_of(blk)
            k.dma(k.sp, KTs[:, :, blk * 512:(blk + 1) * 512], kst.t[:, :, :], [kst], [KTS.sub(blk)], kst)
            k.dma(k.sp, Vs.rearrange("h p f -> p h f")[:, :, blk * 516:(blk + 1) * 516],
                  vst.t[:, :, :, :].rearrange("p h j d -> p h (j d)"), [vst], [VS.sub(blk)], vst)

        xnb_of = {0: p1_part1(0), 1: p1_part1(1)}
        p1_T(0, xnb_of.pop(0))
        kb_prev = None
        for tile in range(NT):
            blk, j = tile // 4, tile % 4
            xnT, kst, vst = bufs_of(blk)
            if tile + 2 < NT:
                xnb_of[tile + 2] = p1_part1(tile + 2)
            if tile + 1 < NT:
                p1_T(tile + 1, xnb_of.pop(tile + 1))
            xr = xnT.sub(j)
            kb_part = headnorm_T(lambda c, xnT=xnT, j=j: xnT.t[:, c, j * 128:(j + 1) * 128], xr, [W[0], W[1]], gK_b, kst,
                                 j * 128, ssr, tmpr, knbr, junk, defer=True)
            for cg in range(2):
                ps = psr.next()
                for c in range(KC):
                    PE([xr, W[2 + cg]], [ps], lambda c=c, cg=cg, ps=ps: ten.matmul(
                        ps.t[:, :], xnT.t[:, c, j * 128:(j + 1) * 128], W[2 + cg].t[:, c, :],
                        start=(c == 0), stop=(c == KC - 1)), inc=(c == KC - 1))
                evac_copy(vst.t[:, cg * 4:(cg + 1) * 4, j, 0:128],
                          ps.t[:, :].rearrange("p (h d) -> p h d", h=4), [ps], [vst])
            if kb_prev is not None:
                kb_prev()
                if j == 0:
                    p1_block_stores(blk - 1)
            kb_prev = kb_part
            if j == 3:
                allx = [xnT.sub(jj) for jj in range(4)]
                k.dma(k.sp, xnTs[blk], xnT.t[:, :, :].rearrange("p c t -> p (c t)"), allx, [XNTS.sub(blk)], xnT)
                ps = psr.next()
                for c in range(KC):
                    PE(allx + [Wik], [ps], lambda c=c, ps=ps: ten.matmul(
                        ps.t[:, :], Wik.t[:, c, :], xnT.t[:, c, :], start=(c == 0), stop=(c == KC - 1)), inc=(c == KC - 1))
                evac_copy(kidxT[0].t[0:64, blk * 512:(blk + 1) * 512], ps.t[0:64, :], [ps], [kidxT[0]])
                evac_copy(kidxT[1].t[64:128, blk * 512:(blk + 1) * 512], ps.t[64:128, :], [ps], [kidxT[1]])
        kb_prev()
        p1_block_stores(7)
        k.barrier()

    QT = k.sb("QT", [128, 8, OWN], BF16, es_att)
    qidxT = k.sb("qidxT", [128, 8, OWN], BF16, es_att)
    w_own = k.sb("w_own", [128, 8, 16], F32, es_att)
    with contextlib.ExitStack() as es:
        W = walloc(es, "b")
        Wiw = k.sb("Wiw", [128, KC, 16], BF16, es)
        gQ_b = k.sb("gQ_b", [128, 128], F32, es)
        junk = k.sb("junk2", [128, 128], BF16, es)
        ssr = Rot([k.sb(f"ss2{i}", [128, 8], F32, es) for i in range(4)])
        tmpr = Rot([k.sb(f"tmp2{i}", [128, 8], F32, es) for i in range(4)])
        xnTr = Rot([k.sb(f"xnT2{i}", [128, KC, 512], BF16, es) for i in range(2)])
        knbr = Rot([k.sb(f"knb2{i}", [128, 8, 128], BF16, es) for i in range(2)])
        for i in range(2):
            load_w(W[i], w_in, C_AQ + i * 512)
        for i in range(2):
            load_w(W[2 + i], w_in, C_IQ + i * 512)
        src = w_in.rearrange("(c p) n -> p c n", p=128)[:, :, C_IW:C_IW + 16]
        k.dma(k.pool, Wiw.t[:, :, :], src, [], [Wiw], Wiw, max_dma_last_dim=4096)
        bcast_row(gQ_b, qnorm[0:1, :], 128)
        for ob in range(2):
            xnT = xnTr.next()
            k.dma(k.sp, xnT.t[:, :, :].rearrange("p c t -> p (c t)"), xnTs[6 + ob], [XNTS.sub(6 + ob)], [xnT], xnT)
            for j in range(4):
                qt = ob * 4 + j
                headnorm_T(lambda c, xnT=xnT, j=j: xnT.t[:, c, j * 128:(j + 1) * 128], xnT, [W[0], W[1]], gQ_b, QT, qt * 128, ssr, tmpr, knbr, junk)
                ps = psr.next()
                for c in range(KC):
                    PE([xnT, Wiw], [ps], lambda c=c, ps=ps: ten.matmul(
                        ps.t[:, 0:16], xnT.t[:, c, j * 128:(j + 1) * 128], Wiw.t[:, c, :],
                        start=(c == 0), stop=(c == KC - 1)), inc=(c == KC - 1))
                A([ps], [w_own], lambda ps=ps, qt=qt: act.activation(
                    out=w_own.t[:, qt, :], in_=ps.t[:, 0:16], func=AF.Copy, scale=1.0 / 32.0))
            for pair in range(8):
                ps = psr.next()
                Wc = W[2 + pair // 4]
                for c in range(KC):
                    PE([xnT, Wc], [ps], lambda c=c, ps=ps, Wc=Wc, pair=pair: ten.matmul(
                        ps.t[:, :], Wc.t[:, c, (pair % 4) * 128:(pair % 4 + 1) * 128], xnT.t[:, c, :],
                        start=(c == 0), stop=(c == KC - 1)), inc=(c == KC - 1))
                evac_copy(qidxT.t[:, pair, ob * 512:(ob + 1) * 512], ps.t[:, :], [ps], [qidxT])
        k.barrier()

    maskT = k.sb("maskT", [128, 8, NT, 128], BF16, es_att)
    with contextlib.ExitStack() as es:
        accr = Rot([k.sb(f"acc{i}", [128, T], F32, es) for i in range(2)])
        Rr = Rot([k.sb(f"Rr{i}", [128, 512], BF16, es) for i in range(4)])
        Dsr = Rot([k.sb(f"Dsgn{i}", [128, 16, 128], BF16, es) for i in range(2)])
        absw = k.sb("absw", [128, 8, 16], F32, es)
        sgnw = k.sb("sgnw", [128, 8, 16], F32, es)
        mkr = Rot([k.sb(f"mk{i}", [128, T], BF16, es) for i in range(2)])
        junkb = k.sb("junkb", [128, T], BF16, es)
        str_ = Rot([k.sb(f"st{i}", [128, 8], F32, es) for i in range(2)])
        hwr = Rot([k.sb(f"hwt{i}", [128, NBIS + 2], F32, es) for i in range(2)])
        V([w_own], [sgnw], lambda: vec.tensor_scalar(out=sgnw.t[:, :, :], in0=w_own.t[:, :, :], scalar1=0.0, scalar2=2.0,
                                                     op0=ALU.is_ge, op1=ALU.mult))
        V([sgnw], [sgnw], lambda: vec.tensor_scalar(out=sgnw.t[:, :, :], in0=sgnw.t[:, :, :], scalar1=-1.0, scalar2=None,
                                                    op0=ALU.add))
        V([w_own, sgnw], [absw], lambda: vec.tensor_tensor(out=absw.t[:, :, :], in0=w_own.t[:, :, :], in1=sgnw.t[:, :, :],
                                                           op=ALU.mult))

        psr3, par3 = Rot(PSB[0:3]), Rot(PSB[3:6])

        def indexer(i, acc):
            nk = (T - OWN) + (i + 1) * 128
            nch = (nk + 511) // 512
            Ds = Dsr.next()
            for h in range(16):
                G([ident, sgnw], [Ds], lambda h=h, Ds=Ds: pool.tensor_scalar(
                    out=Ds.t[:, h, :], in0=ident.t[:, :], scalar1=sgnw.t[:, i, h:h + 1], scalar2=None, op0=ALU.mult))
            for ch in range(nch):
                k0 = ch * 512
                wd = min(512, nk - k0)
                pa = par3.next()

                def L(h):
                    pair, kv = h // 2, kidxT[h % 2]
                    ps = psr3.next()
                    PE([qidxT, kv], [ps], lambda: ten.matmul(
                        ps.t[:, 0:wd], qidxT.t[:, pair, i * 128:(i + 1) * 128],
                        kv.t[:, k0:k0 + wd], start=True, stop=True))
                    R = Rr.next()
                    A([ps, absw], [R], lambda: act.activation(out=R.t[:, 0:wd], in_=ps.t[:, 0:wd], func=AF.Relu,
                                                             scale=absw.t[:, i, h:h + 1]))
                    return R

                def Dm(h, R):
                    PE([Ds, R], [pa], lambda: ten.matmul(pa.t[:, 0:wd], Ds.t[:, h, :], R.t[:, 0:wd],
                                                         start=(h == 0), stop=(h == 15)), inc=(h == 15))

                pend = [L(0), L(1)]
                for h in range(16):
                    if h + 2 < 16:
                        pend.append(L(h + 2))
                    Dm(h, pend[h])
                V([pa], [acc.sub(ch)], lambda pa=pa, k0=k0, wd=wd: vec.tensor_copy(out=acc.t[:, k0:k0 + wd], in_=pa.t[:, 0:wd]))
                yield

        def bisect(i, acc):
            nk = (T - OWN) + (i + 1) * 128
            nch = (nk + 511) // 512
            live = [acc.sub(ch) for ch in range(nch)]
            st, hwt, mk = str_.next(), hwr.next(), mkr.next()
            V(live, [st], lambda: vec.tensor_reduce(out=st.t[:, 1:2], in_=acc.t[:, 0:nk], axis=AX.X, op=ALU.max))
            V(live, [st], lambda: vec.tensor_reduce(out=st.t[:, 0:1], in_=acc.t[:, 0:nk], axis=AX.X, op=ALU.min))
            yield
            V(live + [kbias_b], live, lambda: vec.tensor_tensor(out=acc.t[:, 0:T - OWN], in0=acc.t[:, 0:T - OWN],
                                                                in1=kbias_b.t[:, :], op=ALU.add))
            d0 = (T - OWN) + i * 128
            V(live + [tri_neg], live, lambda: vec.tensor_tensor(out=acc.t[:, d0:d0 + 128], in0=acc.t[:, d0:d0 + 128],
                                                                in1=tri_neg.t[:, :], op=ALU.add))
            V([st], [st], lambda: vec.tensor_tensor(out=st.t[:, 5:6], in0=st.t[:, 1:2], in1=st.t[:, 0:1], op=ALU.subtract))
            V([st, cpow], [hwt], lambda: vec.tensor_scalar(out=hwt.t[:, :], in0=cpow.t[:, :], scalar1=st.t[:, 5:6], scalar2=None,
                                                           op0=ALU.mult))
            V([st, hwt], [st], lambda: vec.tensor_tensor(out=st.t[:, 2:3], in0=st.t[:, 0:1], in1=hwt.t[:, 1:2], op=ALU.add))
            yield
            for it in range(NBIS):
                V(live + [st], [junkb, st], lambda: vec.tensor_scalar(
                    out=junkb.t[:, 0:nk], in0=acc.t[:, 0:nk], scalar1=st.t[:, 2:3], scalar2=None,
                    op0=ALU.is_ge, op1=ALU.add, accum_out=st.t[:, 3:4]))
                V([st], [st], lambda: vec.tensor_scalar(out=st.t[:, 4:5], in0=st.t[:, 3:4], scalar1=255.5, scalar2=0.5,
                                                        op0=ALU.is_ge, op1=ALU.subtract))
                V([st, hwt], [st], lambda it=it: vec.scalar_tensor_tensor(
                    out=st.t[:, 2:3], in0=st.t[:, 4:5], scalar=hwt.t[:, it + 1:it + 2], in1=st.t[:, 2:3],
                    op0=ALU.mult, op1=ALU.add))
                yield
            V(live + [st, hwt], [mk], lambda: vec.tensor_scalar(
                out=mk.t[:, 0:nk], in0=acc.t[:, 0:nk], scalar1=hwt.t[:, NBIS + 1:NBIS + 2], scalar2=st.t[:, 2:3],
                op0=ALU.add, op1=ALU.is_ge))
            mk_of[i] = mk
            yield

        def mask_transposes(i):
            mk = mk_of.pop(i)
            nkb = ((T - OWN) + (i + 1) * 128) // 128
            for g0 in range(0, nkb, 8):
                n = min(8, nkb - g0)
                tp = tpr.next()
                for jj in range(n):
                    kb_ = g0 + jj
                    PE([mk, ident], [tp], lambda kb_=kb_, jj=jj, tp=tp: ten.transpose(
                        out=tp.t[:, jj * 128:(jj + 1) * 128], in_=mk.t[:, kb_ * 128:(kb_ + 1) * 128], identity=ident[:]),
                       inc=(jj == n - 1))
                evac_copy(maskT.t[:, i, g0:g0 + n, :], tp.t[:, 0:n * 128].rearrange("p (k t) -> p k t", k=n),
                          [tp], [maskT.sub(i)])

        mk_of = {}
        prev = None
        order = list(range(7, -1, -1))
        for pos, i in enumerate(order):
            acc = accr.next()
            for ci, _ in enumerate(indexer(i, acc)):
                if ci == 1 and pos >= 2:
                    mask_transposes(order[pos - 2])
                if prev is not None:
                    for _r in range(3):
                        if next(prev, "end") == "end":
                            prev = None
                            break
            if prev is not None:
                for _ in prev:
                    pass
            prev = bisect(i, acc)
        mask_transposes(order[6])
        for _ in prev:
            pass
        mask_transposes(order[7])
        k.barrier()

    if "pa" in dbg:
        d_mt = dbg_tensor("maskT", [128, 8 * NT * 128], BF16)
        k.dma(k.sp, d_mt[:, :], maskT.t[:, :, :, :].rearrange("p a b c -> p (a b c)"), [maskT.sub(i) for i in range(8)],
              [OUT.sub("mt")], maskT)
    with contextlib.ExitStack() as es:
        yatr = Rot([k.sb(f"yat{i}", [128, OWN], BF16, es) for i in range(2)])
        KTh = Rot([k.sb(f"KTh{i}", [128, T], BF16, es) for i in range(2)])
        Vh = Rot([k.sb(f"Vh{i}", [128, NT * 129], BF16, es) for i in range(2)])
        ptbr = Rot([k.sb(f"ptb{i}", [128, 512], BF16, es) for i in range(3)])
        ptmr = Rot([k.sb(f"ptm{i}", [128, 512], BF16, es) for i in range(3)])
        yabr = Rot([k.sb(f"yab{i}", [128, 8, 128], BF16, es) for i in range(2)])
        rsr = Rot([k.sb(f"rs{i}", [128, 1], F32, es) for i in range(4)])
        sc_att = float(128 ** -0.5)
        mm_i = [0]
        LA = 3
        ptbr = Rot(ptbr.bufs + [k.sb(f"ptbx{i}", [128, 512], BF16, es) for i in range(3)])
        ptmr = Rot(ptmr.bufs + [k.sb(f"ptmx{i}", [128, 512], BF16, es) for i in range(3)])
        items = []
        for h in range(8):
            for i in range(8):
                nkb = (T - OWN) // 128 + i + 1
                for g0 in range(0, nkb, 4):
                    items.append((h, i, g0, min(4, nkb - g0), nkb))
        kv_of = {}

        def ensure_loaded(h):
            if h in kv_of or h >= 8:
                return
            kth, vh = KTh.next(), Vh.next()
            k.dma(k.sp, kth.t[:, :], KTs[:, h, :], [KTS.sub(b_) for b_ in range(8)], [kth], kth)
            k.dma(k.sp, vh.t[:, :], Vs[h], [VS.sub(b_) for b_ in range(8)], [vh], vh)
            kv_of[h] = (kth, vh)

        def emit_S(h, i, g0, n):
            kth = kv_of[h][0]
            ps = psr4.next()
            for jj in range(n):
                kb_ = g0 + jj
                PE([kth, QT], [ps], lambda jj=jj, kb_=kb_: ten.matmul(
                    ps.t[:, jj * 128:(jj + 1) * 128], kth.t[:, kb_ * 128:(kb_ + 1) * 128],
                    QT.t[:, h, i * 128:(i + 1) * 128], start=True, stop=True), inc=(jj == n - 1))
            ptb = ptbr.next()
            A([ps], [ptb], lambda: act.activation(out=ptb.t[:, 0:n * 128], in_=ps.t[:, 0:n * 128], func=AF.Exp, scale=sc_att))
            ptm = ptmr.next()
            mm_i[0] += 1
            usev = (mm_i[0] % 2 == 1)
            e = vec if usev else pool
            fn = lambda: e.tensor_tensor(out=ptm.t[:, 0:n * 128], in0=ptb.t[:, 0:n * 128],
                                         in1=maskT.t[:, i, g0:g0 + n, :].rearrange("p k t -> p (k t)"), op=ALU.mult)
            (V if usev else G)([ptb, maskT.sub(i)], [ptm], fn)
            return ptm

        po_of, ptm_of, yab_of = {}, {}, {}
        ensure_loaded(0)
        for idx in range(len(items) + LA):
            if idx < len(items):
                h, i, g0, n, nkb = items[idx]
                if i == 0 and g0 == 0:
                    ensure_loaded(h)
                ptm_of[idx] = emit_S(h, i, g0, n)
            jx = idx - LA
            if jx < 0:
                continue
            h, i, g0, n, nkb = items[jx]
            vh = kv_of[h][1]
            if g0 == 0:
                po_of[(h, i)] = por.next()
                if i == 0:
                    yab_of[h] = yabr.next()
                    ensure_loaded(h + 1)
            po, ptm, yab = po_of[(h, i)], ptm_of.pop(jx), yab_of[h]
            for jj in range(n):
                kb_ = g0 + jj
                PE([ptm, vh], [po], lambda jj=jj, kb_=kb_: ten.matmul(
                    po.t[:, 0:129], ptm.t[:, jj * 128:(jj + 1) * 128], vh.t[:, kb_ * 129:(kb_ + 1) * 129],
                    start=(kb_ == 0), stop=(kb_ == nkb - 1)), inc=(kb_ == nkb - 1 or jj == n - 1))
            if g0 + n == nkb:
                rs = rsr.next()
                V([po], [rs], lambda: vec.reciprocal(out=rs.t[:, 0:1], in_=po.t[:, 128:129]))
                A([po, rs], [yab], lambda: act.activation(out=yab.t[:, i, :], in_=po.t[:, 0:128], func=AF.Copy,
                                                          scale=rs.t[:, 0:1]))
                if i == 7:
                    tp = tpr.next()
                    for i2 in range(8):
                        PE([yab, ident], [tp], lambda i2=i2: ten.transpose(
                            out=tp.t[:, i2 * 128:(i2 + 1) * 128], in_=yab.t[:, i2, :], identity=ident[:]), inc=(i2 == 7))
                    yat = yatr.next()
                    evac_copy(yat.t[:, :], tp.t[:, :], [tp], [yat])
                    k.dma(k.sp, yTs[:, 8 + h, :], yat.t[:, :], [yat], [YTS.sub(8 + h)], yat)
        k.barrier()
    es_att.close()

    if "pa" in dbg:
        d_ya = dbg_tensor("yaT", [128, 8, OWN], BF16)
        with contextlib.ExitStack() as es:
            t1 = k.sb("dbgya", [128, 8, OWN], BF16, es)
            k.dma(k.sp, t1.t[:, :, :], yTs[:, 8:16, :], [YTS.sub(8 + h) for h in range(8)], [t1], t1)
            k.dma(k.sp, d_ya[:, :, :], t1.t[:, :, :], [t1], [OUT.sub("ya")], t1)
            k.barrier()
        return nc, dbg_out

    if True:
        with contextlib.ExitStack() as es:
            W = walloc(es, "h")
            S = k.sb("S", [128, 4, 128], F32, es)
            xnTr = Rot([k.sb(f"xnTh{i}", [128, KC, 512], BF16, es) for i in range(2)])
            vbr = Rot([k.sb(f"vb{i}", [128, 4, 512], BF16, es) for i in range(2)])
            s_l = [k.sb(f"s_t{i}", [128, 512], F32, es) for i in range(4)]
            lf_l = [k.sb(f"lf_t{i}", [128, 512], F32, es) for i in range(4)]
            kk_l = [k.sb(f"kk_t{i}", [128, 512], F32, es) for i in range(4)]
            A_l = [k.sb(f"A_t{i}", [128, 512], F32, es) for i in range(4)]
            eNA_l = lf_l
            eA_l = [k.sb(f"eA{i}", [128, 512], F32, es) for i in range(4)]
            sq_l = [k.sb(f"sq_t{i}", [128, 512], F32, es) for i in range(4)]
            ktT_s = [[k.sb(f"ktT{q}{i}", [128, 512], BF16, es) for i in range(4)] for q in range(2)]
            qtT_s = [[k.sb(f"qtT{q}{i}", [128, 512], BF16, es) for i in range(4)] for q in range(2)]
            sm_s = [[k.sb(f"sm{q}{i}", [128, 24], F32, es) for i in range(4)] for q in range(2)]
            ktok_l = [k.sb(f"ktok{i}", [128, 4, 128], BF16, es) for i in range(4)]
            Smid_r = Rot([k.sb(f"Smid{i}", [128, 128], BF16, es) for i in range(2)])
            scm_r = Rot([k.sb(f"scm{i}", [128, 128], BF16, es) for i in range(2)])
            o_sb = k.sb("o_sb", [128, 4, 512], F32, es)
            ss_o = k.sb("ss_o", [128, 16], F32, es)
            tmp_o = k.sb("tmp_o", [128, 16], F32, es)
            junkh = k.sb("junkh", [128, 128], BF16, es)
            sg_r = Rot([k.sb(f"sg{i}", [128, 512], F32, es) for i in range(2)])
            yhb_r = Rot([k.sb(f"yhb{i}", [128, 512], BF16, es) for i in range(2)])
            yht = k.sb("yht", [128, 4, OWN], BF16, es)

            def load_first(hh):
                load_w(W[1], w_in, C_HI + hh * 512)
                load_w(W[0], w_in, C_HF + hh * 512)

            def load_rest(hh):
                load_w(W[2], w_in, C_HQ + hh * 512)
                load_w(W[3], w_in, C_HG + hh * 512)

            load_first(0)
            load_rest(0)
            for hl in range(4):
                V([], [S.sub(hl)], lambda hl=hl: vec.memset(S.t[:, hl, :], 0.0))

            blkctx = {}

            def stageA1(hh, blk):
                own = blk >= 6
                xnT = xnTr.next()
                k.dma(k.sp, xnT.t[:, :, :].rearrange("p c t -> p (c t)"), xnTs[blk], [XNTS.sub(blk)], [xnT], xnT)
                vb = vbr.next()
                blkctx[(hh, blk)] = (xnT, vb)
                for j in range(4):
                    ps = psr4.next()
                    for c in range(KC):
                        PE([xnT, W[1]], [ps], lambda c=c, ps=ps, j=j: ten.matmul(
                            ps.t[:, :], xnT.t[:, c, j * 128:(j + 1) * 128], W[1].t[:, c, :],
                            start=(c == 0), stop=(c == KC - 1)), inc=(c == KC - 1))
                    evac_copy(vb.t[:, j, :], ps.t[:, :], [ps], [vb])
                for hl in range(4):
                    ps = psr4.next()
                    for c in range(KC):
                        PE([xnT, W[0]], [ps], lambda c=c, ps=ps, hl=hl: ten.matmul(
                            ps.t[:, :], W[0].t[:, c, hl * 128:(hl + 1) * 128], xnT.t[:, c, :],
                            start=(c == 0), stop=(c == KC - 1)), inc=(c == KC - 1))
                    A([ps], [s_l[hl]], lambda ps=ps, hl=hl: act.activation(out=s_l[hl].t[:, :], in_=ps.t[:, :], func=AF.Sigmoid))
                if own:
                    for hl in range(4):
                        ps = psr4.next()
                        for c in range(KC):
                            PE([xnT, W[2]], [ps], lambda c=c, ps=ps, hl=hl: ten.matmul(
                                ps.t[:, :], W[2].t[:, c, hl * 128:(hl + 1) * 128], xnT.t[:, c, :],
                                start=(c == 0), stop=(c == KC - 1)), inc=(c == KC - 1))
                        A([ps], [sq_l[hl]], lambda ps=ps, hl=hl: act.activation(out=sq_l[hl].t[:, :], in_=ps.t[:, :], func=AF.Silu))

            def stageA2(hh, blk):
                own = blk >= 6
                q = blk % 2
                H4 = range(4)
                for hl in H4:
                    h = hh * 4 + hl
                    A([s_l[hl], oml, lbv], [lf_l[hl]], lambda hl=hl, h=h: act.activation(
                        out=lf_l[hl].t[:, :], in_=s_l[hl].t[:, :], func=AF.Ln, scale=oml.t[:, h:h + 1], bias=lbv.t[:, h:h + 1]))
                for hl in H4:
                    h = hh * 4 + hl
                    V([s_l[hl], oml, noml], [kk_l[hl]], lambda hl=hl, h=h: vec.tensor_scalar(
                        out=kk_l[hl].t[:, :], in0=s_l[hl].t[:, :], scalar1=noml.t[:, h:h + 1], scalar2=oml.t[:, h:h + 1],
                        op0=ALU.mult, op1=ALU.add))
                    V([mreset, lf_l[hl]], [A_l[hl]], lambda hl=hl: vec.tensor_tensor_scan(
                        out=A_l[hl].t[:, :], data0=mreset.t[:, :], data1=lf_l[hl].t[:, :], initial=0.0, op0=ALU.mult, op1=ALU.add))
                Avs = [A_l[hl].t[:, :].rearrange("p (j t) -> p j t", t=128) for hl in H4]
                for hl in H4:
                    sm, Av = sm_s[q][hl], Avs[hl]
                    V([A_l[hl]], [sm], lambda sm=sm, Av=Av: vec.tensor_scalar(out=sm.t[:, 0:4], in0=Av[:, :, 63], scalar1=-1.0,
                                                                              scalar2=None, op0=ALU.mult))
                    V([A_l[hl]], [sm], lambda sm=sm, Av=Av: vec.tensor_tensor(out=sm.t[:, 16:20], in0=Av[:, :, 127], in1=Av[:, :, 63],
                                                                              op=ALU.subtract))
                for hl in H4:
                    sm, Av = sm_s[q][hl], Avs[hl]
                    A([A_l[hl]], [sm], lambda sm=sm, Av=Av: act.activation(out=sm.t[:, 4:8], in_=Av[:, :, 127], func=AF.Exp))
                    A([sm], [sm], lambda sm=sm: act.activation(out=sm.t[:, 8:12], in_=sm.t[:, 16:20], func=AF.Exp))
                    if own:
                        A([A_l[hl]], [sm], lambda sm=sm, Av=Av: act.activation(out=sm.t[:, 12:16], in_=Av[:, :, 63], func=AF.Exp))
                for hl in H4:
                    for j in range(4):
                        A([A_l[hl]], [eNA_l[hl]], lambda j=j, hl=hl: act.activation(
                            out=eNA_l[hl].t[:, j * 128:(j + 1) * 128], in_=A_l[hl].t[:, j * 128:(j + 1) * 128], func=AF.Exp,
                            scale=-1.0, bias=A_l[hl].t[:, j * 128 + 63:j * 128 + 64]))
                    G([kk_l[hl], eNA_l[hl]], [ktT_s[q][hl]], lambda hl=hl: pool.tensor_tensor(
                        out=ktT_s[q][hl].t[:, :], in0=kk_l[hl].t[:, :], in1=eNA_l[hl].t[:, :], op=ALU.mult))
                if own:
                    for hl in H4:
                        sm = sm_s[q][hl]
                        for j in range(4):
                            A([A_l[hl], sm], [eA_l[hl]], lambda j=j, hl=hl, sm=sm: act.activation(
                                out=eA_l[hl].t[:, j * 128:(j + 1) * 128], in_=A_l[hl].t[:, j * 128:(j + 1) * 128], func=AF.Exp,
                                bias=sm.t[:, j:j + 1]))
                        V([sq_l[hl], eA_l[hl]], [qtT_s[q][hl]], lambda hl=hl: vec.scalar_tensor_tensor(
                            out=qtT_s[q][hl].t[:, :], in0=sq_l[hl].t[:, :], scalar=float(128 ** -0.5), in1=eA_l[hl].t[:, :],
                            op0=ALU.mult, op1=ALU.mult))

            def stageB(hh, blk):
                own = blk >= 6
                q = blk % 2
                xnT, vb = blkctx.pop((hh, blk))
                for hl in range(4):
                    ktT = ktT_s[q][hl]
                    tp = tpr.next()
                    for j in range(4):
                        PE([ktT, ident], [tp], lambda j=j, tp=tp, ktT=ktT: ten.transpose(
                            out=tp.t[:, j * 128:(j + 1) * 128], in_=ktT.t[:, j * 128:(j + 1) * 128], identity=ident[:]),
                           inc=(j == 3))
                    evac_copy(ktok_l[hl].t[:, :, :], tp.t[:, 0:512].rearrange("p (j q) -> p j q", j=4), [tp], [ktok_l[hl]])
                for j in range(4):
                    tsl = slice(j * 128, (j + 1) * 128)
                    for hl in range(4):
                        sm, ktT, qtT, ktok = sm_s[q][hl], ktT_s[q][hl], qtT_s[q][hl], ktok_l[hl]
                        vsl = slice(hl * 128, (hl + 1) * 128)
                        if own:
                            Smid, scm = Smid_r.next(), scm_r.next()
                            V([S.sub(hl), sm], [Smid], lambda j=j, Smid=Smid, sm=sm, hl=hl: vec.tensor_scalar(
                                out=Smid.t[:, :], in0=S.t[:, hl, :], scalar1=sm.t[:, 12 + j:13 + j], scalar2=None, op0=ALU.mult))
                            ps = psr4.next()
                            PE([ktT, qtT], [ps], lambda ps=ps, tsl=tsl, ktT=ktT, qtT=qtT: ten.matmul(
                                ps.t[:, 0:128], ktT.t[:, tsl], qtT.t[:, tsl], start=True, stop=True))
                            V([ps, triT], [scm], lambda ps=ps, scm=scm: vec.tensor_tensor(
                                out=scm.t[:, :], in0=ps.t[:, 0:128], in1=triT.t[:, :], op=ALU.mult))
                            po = por.next()
                            PE([scm, vb], [po], lambda po=po, scm=scm, j=j, vsl=vsl: ten.matmul(
                                po.t[:, 0:128], scm.t[:, :], vb.t[:, j, vsl], start=True, stop=False), inc=False)
                            PE([qtT, Smid], [po], lambda po=po, Smid=Smid, tsl=tsl, qtT=qtT: ten.matmul(
                                po.t[:, 0:128], qtT.t[:, tsl], Smid.t[:, :], start=False, stop=True))
                            A([po], [o_sb.sub(j)], lambda po=po, j=j, vsl=vsl: act.copy(out=o_sb.t[:, j, vsl], in_=po.t[:, 0:128]))
                            A([po], [junkh, ss_o], lambda po=po, j=j, hl=hl: act.activation(
                                out=junkh.t[:, :], in_=po.t[:, 0:128], func=AF.Square,
                                accum_out=ss_o.t[:, j * 4 + hl:j * 4 + hl + 1]))
                        pu = psr4.next()
                        PE([ktok, vb], [pu], lambda pu=pu, j=j, ktok=ktok, vsl=vsl: ten.matmul(
                            pu.t[:, 0:128], ktok.t[:, j, :], vb.t[:, j, vsl], start=True, stop=True))
                        V([S.sub(hl), sm], [S.sub(hl)], lambda j=j, sm=sm, hl=hl: vec.tensor_scalar(
                            out=S.t[:, hl, :], in0=S.t[:, hl, :], scalar1=sm.t[:, 4 + j:5 + j], scalar2=None, op0=ALU.mult))
                        V([pu, sm, S.sub(hl)], [S.sub(hl)], lambda pu=pu, j=j, sm=sm, hl=hl: vec.scalar_tensor_tensor(
                            out=S.t[:, hl, :], in0=pu.t[:, 0:128], scalar=sm.t[:, 8 + j:9 + j], in1=S.t[:, hl, :],
                            op0=ALU.mult, op1=ALU.add))
                if own:
                    V([ss_o], [tmp_o], lambda: vec.tensor_scalar(out=tmp_o.t[:, :], in0=ss_o.t[:, :], scalar1=1.0 / 128, scalar2=EPS,
                                                                 op0=ALU.mult, op1=ALU.add))
                    A([tmp_o], [tmp_o], lambda: act.activation(out=tmp_o.t[:, :], in_=tmp_o.t[:, :], func=AF.Sqrt))
                    V([tmp_o], [ss_o], lambda: vec.reciprocal(out=ss_o.t[:, :], in_=tmp_o.t[:, :]))
                    for j in range(4):
                        qt = (blk - 6) * 4 + j
                        sg, yhb = sg_r.next(), yhb_r.next()
                        ps = psr4.next()
                        for c in range(KC):
                            PE([xnT, W[3]], [ps], lambda c=c, ps=ps, j=j: ten.matmul(
                                ps.t[:, :], xnT.t[:, c, j * 128:(j + 1) * 128], W[3].t[:, c, :],
                                start=(c == 0), stop=(c == KC - 1)), inc=(c == KC - 1))
                        A([ps], [sg], lambda ps=ps, sg=sg: act.activation(out=sg.t[:, :], in_=ps.t[:, :], func=AF.Silu))
                        V([sg, on_b4], [sg], lambda sg=sg: vec.tensor_tensor(out=sg.t[:, :], in0=sg.t[:, :], in1=on_b4.t[:, :],
                                                                             op=ALU.mult))
                        for h2 in range(4):
                            v2 = slice(h2 * 128, (h2 + 1) * 128)
                            V([o_sb.sub(j), ss_o, sg], [yhb], lambda j=j, h2=h2, v2=v2, sg=sg, yhb=yhb: vec.scalar_tensor_tensor(
                                out=yhb.t[:, v2], in0=o_sb.t[:, j, v2], scalar=ss_o.t[:, j * 4 + h2:j * 4 + h2 + 1],
                                in1=sg.t[:, v2], op0=ALU.mult, op1=ALU.mult))
                        tp = tpr.next()
                        for h2 in range(4):
                            PE([yhb, ident], [tp], lambda h2=h2, tp=tp, yhb=yhb: ten.transpose(
                                out=tp.t[:, h2 * 128:(h2 + 1) * 128], in_=yhb.t[:, h2 * 128:(h2 + 1) * 128], identity=ident[:]),
                               inc=(h2 == 3))
                        evac_copy(yht.t[:, :, qt * 128:(qt + 1) * 128], tp.t[:, 0:512].rearrange("p (h t) -> p h t", h=4),
                                  [tp], [yht])

            seq = [(hh_, blk_) for hh_ in range(2) for blk_ in range(8)]
            stageA1(*seq[0])
            stageA2(*seq[0])
            for n_, (hh, blk) in enumerate(seq):
                nxt_ = seq[n_ + 1] if n_ + 1 < len(seq) else None
                if nxt_ is not None:
                    if nxt_[1] == 0:
                        load_first(nxt_[0])
                    stageA1(*nxt_)
                stageB(hh, blk)
                if blk == 7:
                    k.dma(k.sp, yTs[:, hh * 4:(hh + 1) * 4, :], yht.t[:, :, :], [yht], [YTS.sub(f"h{hh}")], yht)
                    if nxt_ is not None:
                        load_rest(nxt_[0])
                        for hl in range(4):
                            V([], [S.sub(hl)], lambda hl=hl: vec.memset(S.t[:, hl, :], 0.0))
                if nxt_ is not None:
                    stageA2(*nxt_)
            k.barrier()

    if "ph" in dbg:
        d_yh = dbg_tensor("yhT", [128, 8, OWN], BF16)
        with contextlib.ExitStack() as es:
            t1 = k.sb("dbgyh", [128, 8, OWN], BF16, es)
            k.dma(k.sp, t1.t[:, :, :], yTs[:, 0:8, :], [YTS.sub("h0"), YTS.sub("h1")], [t1], t1)
            k.dma(k.sp, d_yh[:, :, :], t1.t[:, :, :], [t1], [OUT.sub("yh")], t1)
            k.barrier()
        return nc, dbg_out

    hres = k.sb("hres", [128, 8, D], F32)
    with contextlib.ExitStack() as es:
        W = walloc(es, "o")
        yT = k.sb("yT", [128, KC, OWN], BF16, es)
        xrr = Rot([k.sb(f"xres{i}", [128, D], F32, es) for i in range(2)])
        for n in range(4):
            load_w(W[n], w_out, n * 512)
        k.dma(k.sp, yT.t[:, :, :], yTs[:, :, :], [YTS.sub(x_) for x_ in ["h0", "h1"] + [8 + h for h in range(8)]], [yT], yT)
        for qt in range(8):
            xb = xrr.next()
            k.dma(k.sp, xb.t[:, :], xs[(NT - 8 + qt) * 128:(NT - 7 + qt) * 128, :], [], [xb], xb)
            for n in range(4):
                ps = psr.next()
                for c in range(KC):
                    PE([yT, W[n]], [ps], lambda c=c, ps=ps, n=n, qt=qt: ten.matmul(
                        ps.t[:, :], yT.t[:, c, qt * 128:(qt + 1) * 128], W[n].t[:, c, :],
                        start=(c == 0), stop=(c == KC - 1)), inc=(c == KC - 1))
                V([ps, xb], [hres.sub(qt)], lambda ps=ps, xb=xb, n=n, qt=qt: vec.tensor_tensor(
                    out=hres.t[:, qt, n * 512:(n + 1) * 512], in0=ps.t[:, :], in1=xb.t[:, n * 512:(n + 1) * 512], op=ALU.add))
        k.barrier()

    with contextlib.ExitStack() as es:
        W = walloc(es, "x")
        Wo4 = W[3].t[:, :, :].rearrange("p c n -> p (c n)").rearrange("p (c n) -> p c n", c=4)
        gc_b = k.sb("gc_b", [128, D], F32, es)
        gm_b = gc_b
        gxq_b = k.sb("gxq_b", [128, 128], F32, es)
        gxk_b = k.sb("gxk_b", [128, 128], F32, es)
        hnTr = Rot([k.sb(f"hnTx{i}", [128, KC, 128], BF16, es) for i in range(2)])
        memT = k.sb("memT", [128, KC, 256], BF16, es)
        kTx = k.sb("kTx", [128, 4, 256], BF16, es)
        vaug = k.sb("vaug", [128, 2, 4, 129], BF16, es)
        xnbr = Rot([k.sb(f"xnbx{i}", [128, D], BF16, es) for i in range(2)])
        mst = Rot([k.sb(f"mst{i}", [128, D], F32, es) for i in range(1)])
        junk = k.sb("junkx", [128, 128], BF16, es)
        ssr = Rot([k.sb(f"ssx{i}", [128, 8], F32, es) for i in range(4)])
        tmpr = Rot([k.sb(f"tmpx{i}", [128, 8], F32, es) for i in range(4)])
        knbr = Rot([k.sb(f"knbx{i}", [128, 4, 128], BF16, es) for i in range(2)])
        ptxr = Rot([k.sb(f"ptx{i}", [128, 256], BF16, es) for i in range(3)])
        obr = Rot([k.sb(f"obx{i}", [128, 512], BF16, es) for i in range(3)])
        rsr = Rot([k.sb(f"rsx{i}", [128, 1], F32, es) for i in range(4)])
        load_w(W[0], wq_x, 0)
        load_w(W[1], wk_x, 0)
        load_w(W[2], wv_x, 0)
        for c4 in range(4):
            k.dma(k.pool, Wo4[:, c4, :], wo_x[c4 * 128:(c4 + 1) * 128, :], [], [W[3]], W[3], max_dma_last_dim=4096)
        bcast_row(gm_b, mem_norm[0:1, :], D)
        bcast_row(gxq_b, xqn[0:1, :], 128)
        bcast_row(gxk_b, xkn[0:1, :], 128)
        V([], [vaug], lambda: vec.memset(vaug.t[:, :, :, 128:129], 1.0))
        for mt in range(2):
            mb = mst.next()
            k.dma(k.sp, mb.t[:, :], memb[mt * 128:(mt + 1) * 128, :], [], [mb], mb)
            norm_tile_to_T(mb.t[:, :], mb, gm_b, xnbr.next(), memT, mt * 128, ssr.next(), tmpr.next(), junk)
        for mt in range(2):
            headnorm_T(lambda c, mt=mt: memT.t[:, c, mt * 128:(mt + 1) * 128], memT, [W[1]], gxk_b, kTx, mt * 128,
                       ssr, tmpr, knbr, junk)
            ps = psr.next()
            for c in range(KC):
                PE([memT, W[2]], [ps], lambda c=c, ps=ps, mt=mt: ten.matmul(
                    ps.t[:, :], memT.t[:, c, mt * 128:(mt + 1) * 128], W[2].t[:, c, :],
                    start=(c == 0), stop=(c == KC - 1)), inc=(c == KC - 1))
            evac_copy(vaug.t[:, mt, :, 0:128], ps.t[:, :].rearrange("p (h d) -> p h d", h=4), [ps], [vaug])
        sc_att = float(128 ** -0.5)
        bcast_row(gc_b, norm_cross[0:1, :], D)
        qTxa = k.sb("qTxa", [128, 4, OWN], BF16, es)
        oTxa = k.sb("oTxa", [128, 4, OWN], BF16, es)
        xnbr = Rot(xnbr.bufs + [k.sb("xnbx2", [128, D], BF16, es)])
        knbr = Rot(knbr.bufs + [k.sb("knbx2", [128, 4, 128], BF16, es)])

        def x_part1(qt):
            xnb = xnbr.next()
            norm_part1(hres.t[:, qt, :], hres.sub(qt), gc_b, xnb, ssr.next(), tmpr.next(), junk)
            return xnb

        def x_T(qt, xnb):
            hnT = hnTr.next()
            norm_part2(xnb, hnT, 0)
            return hnT

        xnb_q = {0: x_part1(0), 1: x_part1(1)}
        hn_q = {0: x_T(0, xnb_q.pop(0))}
        qb_prev = None
        for qt in range(8):
            if qt + 2 < 8:
                xnb_q[qt + 2] = x_part1(qt + 2)
            if qt + 1 < 8:
                hn_q[qt + 1] = x_T(qt + 1, xnb_q.pop(qt + 1))
            hnT = hn_q.pop(qt)
            qb = headnorm_T(lambda c, hnT=hnT: hnT.t[:, c, :], hnT, [W[0]], gxq_b, qTxa, qt * 128,
                            ssr, tmpr, knbr, junk, defer=True)
            if qb_prev is not None:
                qb_prev()
            qb_prev = qb
        qb_prev()

        xitems = [(qt, h) for qt in range(8) for h in range(4)]
        ptxr = Rot(ptxr.bufs + [k.sb(f"ptxx{i}", [128, 256], BF16, es) for i in range(2)])
        pt_of, ob_of = {}, {}
        LAX = 2
        for idx in range(len(xitems) + LAX):
            if idx < len(xitems):
                qt, h = xitems[idx]
                ps = psr4.next()
                for mt in range(2):
                    PE([kTx, qTxa], [ps], lambda ps=ps, mt=mt, h=h, qt=qt: ten.matmul(
                        ps.t[:, mt * 128:(mt + 1) * 128], kTx.t[:, h, mt * 128:(mt + 1) * 128],
                        qTxa.t[:, h, qt * 128:(qt + 1) * 128], start=True, stop=True), inc=(mt == 1))
                pt = ptxr.next()
                A([ps], [pt], lambda ps=ps, pt=pt: act.activation(out=pt.t[:, :], in_=ps.t[:, 0:256], func=AF.Exp, scale=sc_att))
                pt_of[idx] = pt
            jx = idx - LAX
            if jx < 0:
                continue
            qt, h = xitems[jx]
            pt = pt_of.pop(jx)
            if h == 0:
                ob_of[qt] = obr.next()
            ob = ob_of[qt]
            po = por.next()
            for mt in range(2):
                PE([pt, vaug], [po], lambda po=po, pt=pt, mt=mt, h=h: ten.matmul(
                    po.t[:, 0:129], pt.t[:, mt * 128:(mt + 1) * 128], vaug.t[:, mt, h, :],
                    start=(mt == 0), stop=(mt == 1)), inc=(mt == 1))
            rs = rsr.next()
            V([po], [rs], lambda rs=rs, po=po: vec.reciprocal(out=rs.t[:, 0:1], in_=po.t[:, 128:129]))
            A([po, rs], [ob], lambda rs=rs, po=po, h=h, ob=ob: act.activation(
                out=ob.t[:, h * 128:(h + 1) * 128], in_=po.t[:, 0:128], func=AF.Copy, scale=rs.t[:, 0:1]))
            if h == 3:
                tp = tpr.next()
                for h2 in range(4):
                    PE([ob, ident], [tp], lambda h2=h2, tp=tp, ob=ob: ten.transpose(
                        out=tp.t[:, h2 * 128:(h2 + 1) * 128], in_=ob.t[:, h2 * 128:(h2 + 1) * 128], identity=ident[:]), inc=(h2 == 3))
                evac_copy(oTxa.t[:, :, qt * 128:(qt + 1) * 128], tp.t[:, 0:512].rearrange("p (h t) -> p h t", h=4), [tp],
                          [oTxa.sub(qt)])

        for qt in range(8):
            for n in range(4):
                ps = psr4.next()
                for c in range(4):
                    PE([oTxa.sub(qt), W[3]], [ps], lambda c=c, ps=ps, n=n, qt=qt: ten.matmul(
                        ps.t[:, :], oTxa.t[:, c, qt * 128:(qt + 1) * 128], Wo4[:, c, n * 512:(n + 1) * 512],
                        start=(c == 0), stop=(c == 3)), inc=(c == 3))
                V([ps, hres.sub(qt)], [hres.sub(qt)], lambda ps=ps, n=n, qt=qt: vec.tensor_tensor(
                    out=hres.t[:, qt, n * 512:(n + 1) * 512], in0=ps.t[:, :], in1=hres.t[:, qt, n * 512:(n + 1) * 512], op=ALU.add))
        k.barrier()

    with contextlib.ExitStack() as es:
        gm_b = k.sb("gmlp_b", [128, D], F32, es)
        hnT = k.sb("hnTm", [128, KC, OWN], BF16, es)
        xnbr = Rot([k.sb(f"xnbm{i}", [128, D], BF16, es) for i in range(2)])
        junk = k.sb("junkm", [128, 128], BF16, es)
        ssr = Rot([k.sb(f"ssm{i}", [128, 8], F32, es) for i in range(4)])
        tmpr = Rot([k.sb(f"tmpm{i}", [128, 8], F32, es) for i in range(4)])
        Wur = Rot([k.sb(f"Wu{i}", [128, KC, 512], BF16, es) for i in range(2)])
        Wdr = Rot([k.sb(f"Wd{i}", [128, 4, D], BF16, es) for i in range(2)])
        actr = Rot([k.sb(f"actT{i}", [128, 4, OWN], BF16, es) for i in range(2)])
        rlr = Rot([k.sb(f"rl{i}", [128, 512], F32, es) for i in range(3)])
        bcast_row(gm_b, norm_mlp[0:1, :], D)
        NG = 16

        def load_group(g):
            wu, wd = Wur.next(), Wdr.next()
            load_w(wu, w_up, g * 512)
            for c4 in range(4):
                k.dma(k.pool, wd.t[:, c4, :], w_down[g * 512 + c4 * 128:g * 512 + (c4 + 1) * 128, :],
                      [], [wd], wd, max_dma_last_dim=4096)
            return wu, wd

        nxt = load_group(0)
        for qt in range(8):
            norm_tile_to_T(hres.t[:, qt, :], hres.sub(qt), gm_b, xnbr.next(), hnT, qt * 128, ssr.next(), tmpr.next(), junk)
        sq_i = [0]
        for g in range(NG):
            wu, wd = nxt
            if g + 1 < NG:
                nxt = load_group(g + 1)
            actT = actr.next()
            for f in range(4):
                for th in range(2):
                    ps = psr.next()
                    for c in range(KC):
                        PE([hnT, wu], [ps], lambda c=c, ps=ps, f=f, th=th, wu=wu: ten.matmul(
                            ps.t[:, :], wu.t[:, c, f * 128:(f + 1) * 128], hnT.t[:, c, th * 512:(th + 1) * 512],
                            start=(c == 0), stop=(c == KC - 1)), inc=(c == KC - 1))
                    rl = rlr.next()
                    A([ps], [rl], lambda ps=ps, rl=rl: act.activation(out=rl.t[:, :], in_=ps.t[:, :], func=AF.Relu))
                    sq_i[0] += 1
                    if sq_i[0] % 2:
                        V([rl], [actT], lambda rl=rl, actT=actT, f=f, th=th: vec.tensor_tensor(
                            out=actT.t[:, f, th * 512:(th + 1) * 512], in0=rl.t[:, :], in1=rl.t[:, :], op=ALU.mult))
                    else:
                        G([rl], [actT], lambda rl=rl, actT=actT, f=f, th=th: pool.tensor_tensor(
                            out=actT.t[:, f, th * 512:(th + 1) * 512], in0=rl.t[:, :], in1=rl.t[:, :], op=ALU.mult))
            for qt in range(8):
                for n in range(4):
                    ps = psr.next()
                    for f in range(4):
                        PE([actT, wd], [ps], lambda f=f, ps=ps, n=n, qt=qt, wd=wd, actT=actT: ten.matmul(
                            ps.t[:, :], actT.t[:, f, qt * 128:(qt + 1) * 128], wd.t[:, f, n * 512:(n + 1) * 512],
                            start=(f == 0), stop=(f == 3)), inc=(f == 3))
                    V([ps, hres.sub(qt)], [hres.sub(qt)], lambda ps=ps, n=n, qt=qt: vec.tensor_tensor(
                        out=hres.t[:, qt, n * 512:(n + 1) * 512], in0=ps.t[:, :], in1=hres.t[:, qt, n * 512:(n + 1) * 512], op=ALU.add))
        for qt in range(8):
            k.dma(k.sp, out[qt * 128:(qt + 1) * 128, :], hres.t[:, qt, :], [hres.sub(qt)], [OUT.sub(qt)], hres.sub(qt))
        k.barrier()
    return nc, dbg_out


def make_in_maps(inputs):
    x = np.asarray(inputs["x"], dtype=np.float32)
    mem = np.asarray(inputs["mem"], dtype=np.float32)
    shared = {
        "norm_mix": inputs["norm_mix"][0:1], "w_in": inputs["w_in"][0], "lbl": inputs["hgrn_lb_logits"],
        "onorm": inputs["hgrn_onorm"][0:1], "qnorm": inputs["attn_qnorm"][0:1], "knorm": inputs["attn_knorm"][0:1],
        "w_out": inputs["w_out"][0], "norm_cross": inputs["norm_cross"][0:1], "mem_norm": inputs["mem_norm"][0:1],
        "wq_x": inputs["wq_x"][0], "wk_x": inputs["wk_x"][0], "wv_x": inputs["wv_x"][0], "wo_x": inputs["wo_x"][0],
        "xqn": inputs["xq_norm"][0:1], "xkn": inputs["xk_norm"][0:1], "norm_mlp": inputs["norm_mlp"][0:1],
        "w_up": inputs["w_up"][0], "w_down": inputs["w_down"][0],
    }
    shared = {n: np.ascontiguousarray(np.asarray(v, dtype=np.float32)) for n, v in shared.items()}
    maps = []
    for c in range(8):
        b, q = c // 4, c % 4
        npad = (3 - q) * OWN
        xw = np.zeros((T, D), np.float32)
        xw[npad:] = x[b, :(q + 1) * OWN]
        kb = np.zeros((1, T - OWN), np.float32)
        kb[0, :npad] = NEG
        m = dict(shared)
        m["xs"] = xw
        m["keybias"] = kb
        m["mem"] = np.ascontiguousarray(mem[b])
        maps.append(m)
    return maps


def kernel(**inputs):
    nc, _ = build()
    res = run_bass_kernel_spmd(nc, make_in_maps(inputs), core_ids=list(range(8)))
    outp = np.zeros((2, 4096, D), np.float32)
    for c in range(8):
        b, q = c // 4, c % 4
        outp[b, q * OWN:(q + 1) * OWN] = res.results[c]["out"]
    return outp
```

```python
import contextlib
import numpy as np
import concourse.bass as bass
import concourse.mybir as mybir
from concourse.bass_utils import run_bass_kernel_spmd

F32 = mybir.dt.float32
BF16 = mybir.dt.bfloat16
I32 = mybir.dt.int32
AF = mybir.ActivationFunctionType
ALU = mybir.AluOpType
AX = mybir.AxisListType

D = 2048
T = 4096
OWN = 1024
NT = T // 128
KC = D // 128
EPS = 1e-6
NEG = -1.0e30
C_HQ, C_HF, C_HI, C_HG, C_AQ, C_AK, C_AV, C_IQ, C_IK, C_IW = 0, 1024, 2048, 3072, 4096, 5120, 6144, 7168, 8192, 8256
NBIS = 14


class Sem:
    _n = 0

    def __init__(self, h):
        self.h = h
        Sem._n += 1
        self.uid = Sem._n


class Res:
    def __init__(self, name):
        self.name = name
        self.lw = None
        self.rd = {}
        self.dsem = None
        self.dcount = 0


class Buf:
    def __init__(self, t, name):
        self.t = t
        self.r = Res(name)
        self.name = name
        self._subs = {}

    def sub(self, key):
        if key not in self._subs:
            self._subs[key] = Res(f"{self.name}.{key}")
        return self._subs[key]

    def __getitem__(self, idx):
        return self.t[idx]


class Eng:
    def __init__(self, kb, name, eng, pe=False):
        self.kb = kb
        self.name = name
        self.eng = eng
        self.pe = pe
        self.sem = kb.newsem("e_" + name)
        self.count = 0
        self.seen = {}


class KB:
    def __init__(self):
        self.nc = bass.Bass("TRN2", target_bir_lowering=False)
        self.es = contextlib.ExitStack()
        self.nsem = 0
        nc = self.nc
        self.pe = Eng(self, "pe", nc.tensor, pe=True)
        self.act = Eng(self, "act", nc.scalar)
        self.dve = Eng(self, "dve", nc.vector)
        self.pool = Eng(self, "pool", nc.gpsimd)
        self.sp = Eng(self, "sp", nc.sync)
        self.engs = [self.pe, self.act, self.dve, self.pool, self.sp]
        self.dma_owners = []
        self.ninst = 0

    def newsem(self, name):
        self.nsem += 1
        return Sem(self.es.enter_context(self.nc.semaphore(f"{name}_{self.nsem}")))

    def dram(self, name, shape, dt, kind="Internal"):
        return self.nc.dram_tensor(name, list(shape), dt, kind=kind).ap()

    def sb(self, name, shape, dt, es=None):
        es = es or self.es
        self.nsb = getattr(self, "nsb", 0) + 1
        name = f"{name}_{self.nsb}"
        return Buf(es.enter_context(self.nc.sbuf_tensor(name, list(shape), dt)), name)

    def ps(self, name, shape, dt, es=None):
        es = es or self.es
        return Buf(es.enter_context(self.nc.psum_tensor(name, list(shape), dt)), name)

    @staticmethod
    def _res(x):
        return x.r if isinstance(x, Buf) else x

    def _waits(self, E, reads, writes):
        deps = {}

        def add(d):
            if d is None:
                return
            s, v = d
            if s.uid not in deps or deps[s.uid][1] < v:
                deps[s.uid] = (s, v)

        for r in reads:
            add(self._res(r).lw)
        for w in writes:
            w = self._res(w)
            add(w.lw)
            for d in w.rd.values():
                add(d)
        for uid, (s, v) in deps.items():
            if E.pe and s is E.sem:
                continue
            if E.seen.get(uid, 0) >= v:
                continue
            E.eng.wait_ge(s.h, v)
            E.seen[uid] = v

    def _commit(self, dep, reads, writes):
        s, v = dep
        for r in reads:
            r = self._res(r)
            if s.uid not in r.rd or r.rd[s.uid][1] < v:
                r.rd[s.uid] = dep
        for w in writes:
            w = self._res(w)
            w.lw = dep
            w.rd = {}

    def op(self, E, reads, writes, fn, inc=True):
        self._waits(E, reads, writes)
        inst = fn()
        self.ninst += 1
        if inc:
            E.count += 1
            inst.then_inc(E.sem.h, 1)
            idx = E.count
        else:
            idx = E.count + 1
        self._commit((E.sem, idx), reads, writes)

    def dma(self, Q, out_ap, in_ap, reads, writes, owner, **kw):
        owner = self._res(owner)
        self._waits(Q, reads, writes)
        if owner.dsem is None:
            owner.dsem = self.newsem("d_" + owner.name)
            self.dma_owners.append(owner)
        owner.dcount += 16
        Q.eng.dma_start(out=out_ap, in_=in_ap, **kw).then_inc(owner.dsem.h, 16)
        self.ninst += 1
        self._commit((owner.dsem, owner.dcount), reads, writes)

    def barrier(self):
        M = self.act
        for E in self.engs:
            if E is M or E.count == 0:
                continue
            if M.seen.get(E.sem.uid, 0) < E.count:
                M.eng.wait_ge(E.sem.h, E.count)
                M.seen[E.sem.uid] = E.count
        for o in self.dma_owners:
            if o.dcount and M.seen.get(o.dsem.uid, 0) < o.dcount:
                M.eng.wait_ge(o.dsem.h, o.dcount)
                M.seen[o.dsem.uid] = o.dcount
        if M.seen.get(M.sem.uid, 0) < M.count:
            M.eng.wait_ge(M.sem.h, M.count)
            M.seen[M.sem.uid] = M.count
        M.count += 1
        M.eng.activation(out=self.bar_t[:, 0:1], in_=self.bar_t[:, 1:2], func=AF.Copy).then_inc(M.sem.h, 1)
        for E in self.engs:
            if E is M:
                continue
            E.eng.wait_ge(M.sem.h, M.count)
            E.seen[M.sem.uid] = M.count
            for E2 in self.engs:
                E.seen[E2.sem.uid] = max(E.seen.get(E2.sem.uid, 0), E2.count if E2 is not M else M.count)
            for o in self.dma_owners:
                E.seen[o.dsem.uid] = max(E.seen.get(o.dsem.uid, 0), o.dcount)
        for E2 in self.engs:
            M.seen[E2.sem.uid] = max(M.seen.get(E2.sem.uid, 0), E2.count)

    def V(self, reads, writes, fn):
        self.op(self.dve, reads, writes, fn)

    def A(self, reads, writes, fn):
        self.op(self.act, reads, writes, fn)

    def G(self, reads, writes, fn):
        self.op(self.pool, reads, writes, fn)

    def PE(self, reads, writes, fn, inc=True):
        self.op(self.pe, reads, writes, fn, inc=inc)


class Rot:
    def __init__(self, bufs):
        self.bufs = bufs
        self.i = 0

    def next(self):
        b = self.bufs[self.i % len(self.bufs)]
        self.i += 1
        return b


def build(dbg=None):
    k = KB()
    nc = k.nc
    V, A, G, PE = k.V, k.A, k.G, k.PE
    vec, act, pool, ten = nc.vector, nc.scalar, nc.gpsimd, nc.tensor
    dbg = dbg or ()

    def din(name, shape, dt=F32):
        return k.dram(name, shape, dt, kind="ExternalInput")

    xs = din("xs", [T, D])
    keybias = din("keybias", [1, T - OWN])
    memb = din("mem", [256, D])
    norm_mix = din("norm_mix", [1, D])
    w_in = din("w_in", [D, 8272])
    lbl = din("lbl", [2, 1024])
    onorm = din("onorm", [1, 128])
    qnorm = din("qnorm", [1, 128])
    knorm = din("knorm", [1, 128])
    w_out = din("w_out", [D, D])
    norm_cross = din("norm_cross", [1, D])
    mem_norm = din("mem_norm", [1, D])
    wq_x = din("wq_x", [D, 512])
    wk_x = din("wk_x", [D, 512])
    wv_x = din("wv_x", [D, 512])
    wo_x = din("wo_x", [512, D])
    xqn = din("xqn", [1, 128])
    xkn = din("xkn", [1, 128])
    norm_mlp = din("norm_mlp", [1, D])
    w_up = din("w_up", [D, 8192])
    w_down = din("w_down", [8192, D])
    out = k.dram("out", [OWN, D], F32, kind="ExternalOutput")
    OUT = Buf(None, "OUT")

    xnTs = k.dram("xnTs", [8, 128, KC * 512], BF16)
    XNTS = Buf(None, "xnTs")
    KTs = k.dram("KTs", [128, 8, T], BF16)
    KTS = Buf(None, "KTs")
    Vs = k.dram("Vs", [8, 128, NT * 129], BF16)
    VS = Buf(None, "Vs")

    yTs = k.dram("yTs", [128, KC, OWN], BF16)
    YTS = Buf(None, "yTs")

    dbg_out = {}

    def dbg_tensor(name, shape, dt=F32):
        dbg_out[name] = k.dram("dbg_" + name, shape, dt, kind="ExternalOutput")
        return dbg_out[name]

    k.bar_t = k.sb("bar_t", [128, 2], F32).t
    ident = k.sb("ident", [128, 128], BF16)
    iota_i = k.sb("iota_i", [128, 128], I32)
    def walloc(es, tag):
        return [k.sb(f"W{tag}{i}", [128, KC, 512], BF16, es) for i in range(4)]
    PSB = [k.ps(f"psb{i}", [128, 512], F32) for i in range(6)]
    psr4 = Rot(PSB[0:4])
    por = Rot(PSB[4:6])
    PTP = [k.ps(f"ptp{i}", [128, 1024], BF16) for i in range(2)]
    psr = Rot(PSB)
    tpr = Rot(PTP)
    evac_i = [0]

    def evac_copy(out_ap, in_ap, reads, writes):
        evac_i[0] += 1
        if evac_i[0] % 2:
            A(reads, writes, lambda: act.copy(out=out_ap, in_=in_ap))
        else:
            V(reads, writes, lambda: vec.tensor_copy(out=out_ap, in_=in_ap))

    nc.vector.memset(k.bar_t[:], 0.0)
    G([], [iota_i], lambda: pool.iota(out=iota_i[:], pattern=[[1, 128]], base=0, channel_multiplier=-1))
    V([iota_i], [ident], lambda: vec.tensor_single_scalar(out=ident[:], in_=iota_i[:], scalar=0.0, op=ALU.is_equal))

    l01 = k.sb("l01", [128, 2, 8], F32)
    lbv = k.sb("lbv", [128, 8], F32)
    oml = k.sb("oml", [128, 8], F32)
    noml = k.sb("noml", [128, 8], F32)
    mreset = k.sb("mreset", [128, 512], F32)
    triT = k.sb("triT", [128, 128], F32)
    on_b4 = k.sb("on_b4", [128, 512], F32)
    with contextlib.ExitStack() as es0:
        mri = k.sb("mri", [128, 512], I32, es0)
        k.dma(k.sp, l01.t[:, :, :], lbl.rearrange("r (h q) -> q r h", q=128), [], [l01], l01,
              allow_slow_non_contiguous=True)
        for i4 in range(4):
            k.dma(k.sp, on_b4.t[:, i4 * 128:(i4 + 1) * 128], onorm[0:1, :].partition_broadcast(128), [], [on_b4], on_b4)
        V([l01], [lbv], lambda: vec.tensor_tensor(out=lbv.t[:, :], in0=l01.t[:, 0, :], in1=l01.t[:, 1, :], op=ALU.subtract))
        A([lbv], [lbv], lambda: act.activation(out=lbv.t[:, :], in_=lbv.t[:, :], func=AF.Sigmoid))
        V([lbv], [oml], lambda: vec.tensor_scalar(out=oml.t[:, :], in0=lbv.t[:, :], scalar1=-1.0, scalar2=1.0,
                                                  op0=ALU.mult, op1=ALU.add))
        V([lbv], [noml], lambda: vec.tensor_scalar(out=noml.t[:, :], in0=lbv.t[:, :], scalar1=-1.0, scalar2=None,
                                                   op0=ALU.add))
        G([], [mri], lambda: pool.iota(out=mri.t[:, :].rearrange("p (j t) -> p j t", t=128), pattern=[[0, 4], [1, 128]],
                                       base=0, channel_multiplier=0))
        V([mri], [mreset], lambda: vec.tensor_single_scalar(out=mreset.t[:, :], in_=mri.t[:, :], scalar=0.0, op=ALU.is_gt))
        V([iota_i], [triT], lambda: vec.tensor_single_scalar(out=triT.t[:, :], in_=iota_i.t[:, :], scalar=0.0, op=ALU.is_ge))
        k.barrier()

    def load_w(slot, src2d, c0, ncols=512, rows=D):
        kc = rows // 128
        src = src2d.rearrange("(c p) n -> p c n", p=128)[:, :, c0:c0 + ncols]
        k.dma(k.pool, slot.t[:, 0:kc, 0:ncols], src, [], [slot], slot, max_dma_last_dim=4096)

    def bcast_row(dst, src_row, n):
        k.dma(k.sp, dst.t[:, 0:n], src_row.partition_broadcast(128), [], [dst], dst)

    def rstd_from_ss(ss, n, width, es_bufs):
        tmp = es_bufs
        V([ss], [tmp], lambda: vec.tensor_scalar(out=tmp[:, 0:n], in0=ss[:, 0:n], scalar1=1.0 / width, scalar2=EPS,
                                                  op0=ALU.mult, op1=ALU.add))
        A([tmp], [tmp], lambda: act.activation(out=tmp[:, 0:n], in_=tmp[:, 0:n], func=AF.Sqrt))
        V([tmp], [ss], lambda: vec.reciprocal(out=ss[:, 0:n], in_=tmp[:, 0:n]))

    def norm_part1(src_ap, src_res, gain_b, xnb, ss, tmp, junk):
        A([src_res], [xnb, ss], lambda: act.activation(out=xnb[:, 0:D], in_=src_ap, func=AF.Square,
                                                      accum_out=ss[:, 0:1]))
        rstd_from_ss(ss, 1, D, tmp)
        V([src_res, ss, gain_b], [xnb], lambda: vec.scalar_tensor_tensor(
            out=xnb[:, :], in0=src_ap, scalar=ss[:, 0:1], in1=gain_b[:, :], op0=ALU.mult, op1=ALU.mult))

    def norm_part2(xnb, dstT, col0, dst_res=None):
        dst_res = dst_res if dst_res is not None else dstT
        for g in range(KC // 8):
            tp = tpr.next()
            for j in range(8):
                c = g * 8 + j
                PE([xnb, ident], [tp], lambda c=c, j=j: ten.transpose(
                    out=tp.t[:, j * 128:(j + 1) * 128], in_=xnb[:, c * 128:(c + 1) * 128], identity=ident[:]),
                   inc=(j == 7))
            evac_copy(dstT.t[:, g * 8:(g + 1) * 8, col0:col0 + 128],
                      tp.t[:, :].rearrange("p (c t) -> p c t", c=8), [tp], [dst_res])

    def norm_tile_to_T(src_ap, src_res, gain_b, xnb, dstT, col0, ss, tmp, junk):
        norm_part1(src_ap, src_res, gain_b, xnb, ss, tmp, junk)
        norm_part2(xnb, dstT, col0)

    def headnorm_T(lhs_fn, lhs_res, Wlist, g_b, dst, dcol0, ssr, tmpr, knbr, junk, ncc=KC, defer=False):
        nh = 4 * len(Wlist)
        ss = ssr.next()
        tmp = tmpr.next()
        knb = knbr.next()
        kps = []
        for cg, Wc in enumerate(Wlist):
            ps = psr.next()
            kps.append(ps)
            for c in range(ncc):
                PE([lhs_res, Wc], [ps], lambda c=c, Wc=Wc, ps=ps: ten.matmul(
                    ps.t[:, :], lhs_fn(c), Wc.t[:, c, :],
                    start=(c == 0), stop=(c == ncc - 1)), inc=(c == ncc - 1))
            for hh in range(4):
                h = cg * 4 + hh
                A([ps], [junk, ss], lambda ps=ps, hh=hh, h=h: act.activation(
                    out=junk[:, 0:128], in_=ps.t[:, hh * 128:(hh + 1) * 128], func=AF.Square,
                    accum_out=ss[:, h:h + 1]))
        rstd_from_ss(ss, nh, 128, tmp)
        for cg in range(len(Wlist)):
            ps = kps[cg]
            for hh in range(4):
                h = cg * 4 + hh
                V([ps, ss, g_b], [knb], lambda ps=ps, hh=hh, h=h: vec.scalar_tensor_tensor(
                    out=knb.t[:, h, :], in0=ps.t[:, hh * 128:(hh + 1) * 128], scalar=ss[:, h:h + 1],
                    in1=g_b[:, :], op0=ALU.mult, op1=ALU.mult))
        def part_b():
            tp = tpr.next()
            for h in range(nh):
                PE([knb, ident], [tp], lambda h=h: ten.transpose(
                    out=tp.t[:, h * 128:(h + 1) * 128], in_=knb.t[:, h, :], identity=ident[:]), inc=(h == nh - 1))
            evac_copy(dst.t[:, :, dcol0:dcol0 + 128], tp.t[:, 0:nh * 128].rearrange("p (h t) -> p h t", h=nh), [tp], [dst])

        if defer:
            return part_b
        part_b()

    es_att = contextlib.ExitStack()
    kidxT = [k.sb(f"kidxT{v_}", [128, T], BF16, es_att) for v_ in range(2)]
    V([], [kidxT[0]], lambda: vec.memset(kidxT[0].t[64:128, :], 0.0))
    V([], [kidxT[1]], lambda: vec.memset(kidxT[1].t[0:64, :], 0.0))
    kbias_b = k.sb("kbias_b", [128, T - OWN], BF16, es_att)
    tri_neg = k.sb("tri_neg", [128, 128], F32, es_att)
    cpow = k.sb("cpow", [128, NBIS + 2], F32, es_att)
    k.dma(k.pool, kbias_b.t[:, :], keybias[0:1, :].partition_broadcast(128), [], [kbias_b], kbias_b,
          max_dma_last_dim=4096)
    V([iota_i], [tri_neg], lambda: vec.tensor_scalar(out=tri_neg[:, :], in0=iota_i[:, :], scalar1=0.0, scalar2=NEG,
                                                     op0=ALU.is_gt, op1=ALU.mult))
    for kk_ in range(NBIS + 2):
        V([], [cpow], lambda kk_=kk_: vec.memset(cpow.t[:, kk_:kk_ + 1], float(2.0 ** -kk_)))
    with contextlib.ExitStack() as es:
        W = walloc(es, "a")
        Wik = k.sb("Wik", [128, KC, 128], BF16, es)
        gain_b = k.sb("gain_b", [128, D], F32, es)
        gK_b = k.sb("gK_b", [128, 128], F32, es)
        xst = Rot([k.sb(f"xst{i}", [128, D], F32, es) for i in range(2)])
        junk = k.sb("junk", [128, 128], BF16, es)
        ssr = Rot([k.sb(f"ss{i}", [128, 8], F32, es) for i in range(4)])
        tmpr = Rot([k.sb(f"tmp{i}", [128, 8], F32, es) for i in range(4)])
        xnTr = Rot([k.sb(f"xnT{i}", [128, KC, 512], BF16, es) for i in range(2)])
        Kst = Rot([k.sb(f"Kst{i}", [128, 8, 512], BF16, es) for i in range(2)])
        Vst = Rot([k.sb(f"Vst{i}", [128, 8, 4, 129], BF16, es) for i in range(2)])
        knbr = Rot([k.sb(f"knb{i}", [128, 8, 128], BF16, es) for i in range(2)])

        for i in range(2):
            load_w(W[i], w_in, C_AK + i * 512)
        for i in range(2):
            load_w(W[2 + i], w_in, C_AV + i * 512)
        for half in range(2):
            src = w_in.rearrange("(c p) n -> p c n", p=128)[:, :, C_IK:C_IK + 64]
            k.dma(k.pool, Wik.t[:, :, half * 64:(half + 1) * 64], src, [], [Wik], Wik, max_dma_last_dim=4096)
        bcast_row(gain_b, norm_mix[0:1, :], D)
        bcast_row(gK_b, knorm[0:1, :], 128)
        for vb in Vst.bufs:
            V([], [vb], lambda vb=vb: vec.memset(vb.t[:, :, :, 128:129], 1.0))

        xnbr = Rot([k.sb(f"xnbp{i}", [128, D], BF16, es) for i in range(3)])

        def p1_part1(tile):
            xb = xst.next()
            k.dma(k.sp, xb.t[:, :], xs[tile * 128:(tile + 1) * 128, :], [], [xb], xb)
            xnb = xnbr.next()
            norm_part1(xb.t[:, :], xb, gain_b, xnb, ssr.next(), tmpr.next(), junk)
            return xnb

        blkbuf = {}

        def bufs_of(blk):
            if blk not in blkbuf:
                blkbuf[blk] = (xnTr.next(), Kst.next(), Vst.next())
            return blkbuf[blk]

        def p1_T(tile, xnb):
            xnT = bufs_of(tile // 4)[0]
            norm_part2(xnb, xnT, (tile % 4) * 128, dst_res=xnT.sub(tile % 4))

        def p1_block_stores(blk):
            xnT, kst, vst = bufs_of(blk)
            k.dma(k.sp, KTs[:, :, blk * 512:(blk + 1) * 512], kst.t[:, :, :], [kst], [KTS.sub(blk)], kst)
            k.dma(k.sp, Vs.rearrange("h p f -> p h f")[:, :, blk * 516:(blk + 1) * 516],
                  vst.t[:, :, :, :].rearrange("p h j d -> p h (j d)"), [vst], [VS.sub(blk)], vst)

        xnb_of = {0: p1_part1(0), 1: p1_part1(1)}
        p1_T(0, xnb_of.pop(0))
        kb_prev = None
        for tile in range(NT):
            blk, j = tile // 4, tile % 4
            xnT, kst, vst = bufs_of(blk)
            if tile + 2 < NT:
                xnb_of[tile + 2] = p1_part1(tile + 2)
            if tile + 1 < NT:
                p1_T(tile + 1, xnb_of.pop(tile + 1))
            xr = xnT.sub(j)
            kb_part = headnorm_T(lambda c, xnT=xnT, j=j: xnT.t[:, c, j * 128:(j + 1) * 128], xr, [W[0], W[1]], gK_b, kst,
                                 j * 128, ssr, tmpr, knbr, junk, defer=True)
            for cg in range(2):
                ps = psr.next()
                for c in range(KC):
                    PE([xr, W[2 + cg]], [ps], lambda c=c, cg=cg, ps=ps: ten.matmul(
                        ps.t[:, :], xnT.t[:, c, j * 128:(j + 1) * 128], W[2 + cg].t[:, c, :],
                        start=(c == 0), stop=(c == KC - 1)), inc=(c == KC - 1))
                evac_copy(vst.t[:, cg * 4:(cg + 1) * 4, j, 0:128],
                          ps.t[:, :].rearrange("p (h d) -> p h d", h=4), [ps], [vst])
            if kb_prev is not None:
                kb_prev()
                if j == 0:
                    p1_block_stores(blk - 1)
            kb_prev = kb_part
            if j == 3:
                allx = [xnT.sub(jj) for jj in range(4)]
                k.dma(k.sp, xnTs[blk], xnT.t[:, :, :].rearrange("p c t -> p (c t)"), allx, [XNTS.sub(blk)], xnT)
                ps = psr.next()
                for c in range(KC):
                    PE(allx + [Wik], [ps], lambda c=c, ps=ps: ten.matmul(
                        ps.t[:, :], Wik.t[:, c, :], xnT.t[:, c, :], start=(c == 0), stop=(c == KC - 1)), inc=(c == KC - 1))
                evac_copy(kidxT[0].t[0:64, blk * 512:(blk + 1) * 512], ps.t[0:64, :], [ps], [kidxT[0]])
                evac_copy(kidxT[1].t[64:128, blk * 512:(blk + 1) * 512], ps.t[64:128, :], [ps], [kidxT[1]])
        kb_prev()
        p1_block_stores(7)
        k.barrier()

    QT = k.sb("QT", [128, 8, OWN], BF16, es_att)
    qidxT = k.sb("qidxT", [128, 8, OWN], BF16, es_att)
    w_own = k.sb("w_own", [128, 8, 16], F32, es_att)
    with contextlib.ExitStack() as es:
        W = walloc(es, "b")
        Wiw = k.sb("Wiw", [128, KC, 16], BF16, es)
        gQ_b = k.sb("gQ_b", [128, 128], F32, es)
        junk = k.sb("junk2", [128, 128], BF16, es)
        ssr = Rot([k.sb(f"ss2{i}", [128, 8], F32, es) for i in range(4)])
        tmpr = Rot([k.sb(f"tmp2{i}", [128, 8], F32, es) for i in range(4)])
        xnTr = Rot([k.sb(f"xnT2{i}", [128, KC, 512], BF16, es) for i in range(2)])
        knbr = Rot([k.sb(f"knb2{i}", [128, 8, 128], BF16, es) for i in range(2)])
        for i in range(2):
            load_w(W[i], w_in, C_AQ + i * 512)
        for i in range(2):
            load_w(W[2 + i], w_in, C_IQ + i * 512)
        src = w_in.rearrange("(c p) n -> p c n", p=128)[:, :, C_IW:C_IW + 16]
        k.dma(k.pool, Wiw.t[:, :, :], src, [], [Wiw], Wiw, max_dma_last_dim=4096)
        bcast_row(gQ_b, qnorm[0:1, :], 128)
        for ob in range(2):
            xnT = xnTr.next()
            k.dma(k.sp, xnT.t[:, :, :].rearrange("p c t -> p (c t)"), xnTs[6 + ob], [XNTS.sub(6 + ob)], [xnT], xnT)
            for j in range(4):
                qt = ob * 4 + j
                headnorm_T(lambda c, xnT=xnT, j=j: xnT.t[:, c, j * 128:(j + 1) * 128], xnT, [W[0], W[1]], gQ_b, QT, qt * 128, ssr, tmpr, knbr, junk)
                ps = psr.next()
                for c in range(KC):
                    PE([xnT, Wiw], [ps], lambda c=c, ps=ps: ten.matmul(
                        ps.t[:, 0:16], xnT.t[:, c, j * 128:(j + 1) * 128], Wiw.t[:, c, :],
                        start=(c == 0), stop=(c == KC - 1)), inc=(c == KC - 1))
                A([ps], [w_own], lambda ps=ps, qt=qt: act.activation(
                    out=w_own.t[:, qt, :], in_=ps.t[:, 0:16], func=AF.Copy, scale=1.0 / 32.0))
            for pair in range(8):
                ps = psr.next()
                Wc = W[2 + pair // 4]
                for c in range(KC):
                    PE([xnT, Wc], [ps], lambda c=c, ps=ps, Wc=Wc, pair=pair: ten.matmul(
                        ps.t[:, :], Wc.t[:, c, (pair % 4) * 128:(pair % 4 + 1) * 128], xnT.t[:, c, :],
                        start=(c == 0), stop=(c == KC - 1)), inc=(c == KC - 1))
                evac_copy(qidxT.t[:, pair, ob * 512:(ob + 1) * 512], ps.t[:, :], [ps], [qidxT])
        k.barrier()

    maskT = k.sb("maskT", [128, 8, NT, 128], BF16, es_att)
    with contextlib.ExitStack() as es:
        accr = Rot([k.sb(f"acc{i}", [128, T], F32, es) for i in range(2)])
        Rr = Rot([k.sb(f"Rr{i}", [128, 512], BF16, es) for i in range(4)])
        Dsr = Rot([k.sb(f"Dsgn{i}", [128, 16, 128], BF16, es) for i in range(2)])
        absw = k.sb("absw", [128, 8, 16], F32, es)
        sgnw = k.sb("sgnw", [128, 8, 16], F32, es)
        mkr = Rot([k.sb(f"mk{i}", [128, T], BF16, es) for i in range(2)])
        junkb = k.sb("junkb", [128, T], BF16, es)
        str_ = Rot([k.sb(f"st{i}", [128, 8], F32, es) for i in range(2)])
        hwr = Rot([k.sb(f"hwt{i}", [128, NBIS + 2], F32, es) for i in range(2)])
        V([w_own], [sgnw], lambda: vec.tensor_scalar(out=sgnw.t[:, :, :], in0=w_own.t[:, :, :], scalar1=0.0, scalar2=2.0,
                                                     op0=ALU.is_ge, op1=ALU.mult))
        V([sgnw], [sgnw], lambda: vec.tensor_scalar(out=sgnw.t[:, :, :], in0=sgnw.t[:, :, :], scalar1=-1.0, scalar2=None,
                                                    op0=ALU.add))
        V([w_own, sgnw], [absw], lambda: vec.tensor_tensor(out=absw.t[:, :, :], in0=w_own.t[:, :, :], in1=sgnw.t[:, :, :],
                                                           op=ALU.mult))

        psr3, par3 = Rot(PSB[0:3]), Rot(PSB[3:6])
        first_tile = [True]

        def indexer(i, acc):
            nk = (T - OWN) + (i + 1) * 128
            nch = (nk + 511) // 512
            Ds = Dsr.next()
            for h in range(16):
                if first_tile[0]:
                    V([ident, sgnw], [Ds], lambda h=h, Ds=Ds: vec.tensor_scalar(
                        out=Ds.t[:, h, :], in0=ident.t[:, :], scalar1=sgnw.t[:, i, h:h + 1], scalar2=None, op0=ALU.mult))
                else:
                    G([ident, sgnw], [Ds], lambda h=h, Ds=Ds: pool.tensor_scalar(
                        out=Ds.t[:, h, :], in0=ident.t[:, :], scalar1=sgnw.t[:, i, h:h + 1], scalar2=None, op0=ALU.mult))
            first_tile[0] = False
            for ch in range(nch):
                k0 = ch * 512
                wd = min(512, nk - k0)
                pa = par3.next()

                def L(h):
                    pair, kv = h // 2, kidxT[h % 2]
                    ps = psr3.next()
                    PE([qidxT, kv], [ps], lambda: ten.matmul(
                        ps.t[:, 0:wd], qidxT.t[:, pair, i * 128:(i + 1) * 128],
                        kv.t[:, k0:k0 + wd], start=True, stop=True))
                    R = Rr.next()
                    A([ps, absw], [R], lambda: act.activation(out=R.t[:, 0:wd], in_=ps.t[:, 0:wd], func=AF.Relu,
                                                             scale=absw.t[:, i, h:h + 1]))
                    return R

                def Dm(h, R):
                    PE([Ds, R], [pa], lambda: ten.matmul(pa.t[:, 0:wd], Ds.t[:, h, :], R.t[:, 0:wd],
                                                         start=(h == 0), stop=(h == 15)), inc=(h == 15))

                pend = [L(0), L(1)]
                for h in range(16):
                    if h + 2 < 16:
                        pend.append(L(h + 2))
                    Dm(h, pend[h])
                V([pa], [acc.sub(ch)], lambda pa=pa, k0=k0, wd=wd: vec.tensor_copy(out=acc.t[:, k0:k0 + wd], in_=pa.t[:, 0:wd]))
                yield

        def bisect(i, acc):
            nk = (T - OWN) + (i + 1) * 128
            nch = (nk + 511) // 512
            live = [acc.sub(ch) for ch in range(nch)]
            st, hwt, mk = str_.next(), hwr.next(), mkr.next()
            V(live, [st], lambda: vec.tensor_reduce(out=st.t[:, 1:2], in_=acc.t[:, 0:nk], axis=AX.X, op=ALU.max))
            V(live, [st], lambda: vec.tensor_reduce(out=st.t[:, 0:1], in_=acc.t[:, 0:nk], axis=AX.X, op=ALU.min))
            yield
            V(live + [kbias_b], live, lambda: vec.tensor_tensor(out=acc.t[:, 0:T - OWN], in0=acc.t[:, 0:T - OWN],
                                                                in1=kbias_b.t[:, :], op=ALU.add))
            d0 = (T - OWN) + i * 128
            V(live + [tri_neg], live, lambda: vec.tensor_tensor(out=acc.t[:, d0:d0 + 128], in0=acc.t[:, d0:d0 + 128],
                                                                in1=tri_neg.t[:, :], op=ALU.add))
            V([st], [st], lambda: vec.tensor_tensor(out=st.t[:, 5:6], in0=st.t[:, 1:2], in1=st.t[:, 0:1], op=ALU.subtract))
            V([st, cpow], [hwt], lambda: vec.tensor_scalar(out=hwt.t[:, :], in0=cpow.t[:, :], scalar1=st.t[:, 5:6], scalar2=None,
                                                           op0=ALU.mult))
            V([st, hwt], [st], lambda: vec.tensor_tensor(out=st.t[:, 2:3], in0=st.t[:, 0:1], in1=hwt.t[:, 1:2], op=ALU.add))
            yield
            for it in range(NBIS):
                V(live + [st], [junkb, st], lambda: vec.tensor_scalar(
                    out=junkb.t[:, 0:nk], in0=acc.t[:, 0:nk], scalar1=st.t[:, 2:3], scalar2=None,
                    op0=ALU.is_ge, op1=ALU.add, accum_out=st.t[:, 3:4]))
                V([st], [st], lambda: vec.tensor_scalar(out=st.t[:, 4:5], in0=st.t[:, 3:4], scalar1=255.5, scalar2=0.5,
                                                        op0=ALU.is_ge, op1=ALU.subtract))
                V([st, hwt], [st], lambda it=it: vec.scalar_tensor_tensor(
                    out=st.t[:, 2:3], in0=st.t[:, 4:5], scalar=hwt.t[:, it + 1:it + 2], in1=st.t[:, 2:3],
                    op0=ALU.mult, op1=ALU.add))
                yield
            V(live + [st, hwt], [mk], lambda: vec.tensor_scalar(
                out=mk.t[:, 0:nk], in0=acc.t[:, 0:nk], scalar1=hwt.t[:, NBIS + 1:NBIS + 2], scalar2=st.t[:, 2:3],
                op0=ALU.add, op1=ALU.is_ge))
            mk_of[i] = mk
            yield

        def mask_transposes(i):
            mk = mk_of.pop(i)
            nkb = ((T - OWN) + (i + 1) * 128) // 128
            for g0 in range(0, nkb, 8):
                n = min(8, nkb - g0)
                tp = tpr.next()
                for jj in range(n):
                    kb_ = g0 + jj
                    PE([mk, ident], [tp], lambda kb_=kb_, jj=jj, tp=tp: ten.transpose(
                        out=tp.t[:, jj * 128:(jj + 1) * 128], in_=mk.t[:, kb_ * 128:(kb_ + 1) * 128], identity=ident[:]),
                       inc=(jj == n - 1))
                evac_copy(maskT.t[:, i, g0:g0 + n, :], tp.t[:, 0:n * 128].rearrange("p (k t) -> p k t", k=n),
                          [tp], [maskT.sub(i)])

        mk_of = {}
        prev = None
        order = list(range(7, -1, -1))
        for pos, i in enumerate(order):
            acc = accr.next()
            for ci, _ in enumerate(indexer(i, acc)):
                if ci == 1 and pos >= 2:
                    mask_transposes(order[pos - 2])
                if prev is not None:
                    for _r in range(3):
                        if next(prev, "end") == "end":
                            prev = None
                            break
            if prev is not None:
                for _ in prev:
                    pass
            prev = bisect(i, acc)
        mask_transposes(order[6])
        for _ in prev:
            pass
        mask_transposes(order[7])
        k.barrier()

    if "pa" in dbg:
        d_mt = dbg_tensor("maskT", [128, 8 * NT * 128], BF16)
        k.dma(k.sp, d_mt[:, :], maskT.t[:, :, :, :].rearrange("p a b c -> p (a b c)"), [maskT.sub(i) for i in range(8)],
              [OUT.sub("mt")], maskT)
    with contextlib.ExitStack() as es:
        yatr = Rot([k.sb(f"yat{i}", [128, OWN], BF16, es) for i in range(2)])
        KTh = Rot([k.sb(f"KTh{i}", [128, T], BF16, es) for i in range(2)])
        Vh = Rot([k.sb(f"Vh{i}", [128, NT * 129], BF16, es) for i in range(2)])
        ptbr = Rot([k.sb(f"ptb{i}", [128, 512], BF16, es) for i in range(3)])
        ptmr = Rot([k.sb(f"ptm{i}", [128, 512], BF16, es) for i in range(3)])
        yabr = Rot([k.sb(f"yab{i}", [128, 8, 128], BF16, es) for i in range(2)])
        rsr = Rot([k.sb(f"rs{i}", [128, 1], F32, es) for i in range(4)])
        sc_att = float(128 ** -0.5)
        mm_i = [0]
        LA = 3
        ptbr = Rot(ptbr.bufs + [k.sb(f"ptbx{i}", [128, 512], BF16, es) for i in range(3)])
        ptmr = Rot(ptmr.bufs + [k.sb(f"ptmx{i}", [128, 512], BF16, es) for i in range(3)])
        items = []
        for h in range(8):
            for i in range(8):
                nkb = (T - OWN) // 128 + i + 1
                for g0 in range(0, nkb, 4):
                    items.append((h, i, g0, min(4, nkb - g0), nkb))
        kv_of = {}

        def ensure_loaded(h):
            if h in kv_of or h >= 8:
                return
            kth, vh = KTh.next(), Vh.next()
            k.dma(k.sp, kth.t[:, :], KTs[:, h, :], [KTS.sub(b_) for b_ in range(8)], [kth], kth)
            k.dma(k.sp, vh.t[:, :], Vs[h], [VS.sub(b_) for b_ in range(8)], [vh], vh)
            kv_of[h] = (kth, vh)

        def emit_S(h, i, g0, n):
            kth = kv_of[h][0]
            ps = psr4.next()
            for jj in range(n):
                kb_ = g0 + jj
                PE([kth, QT], [ps], lambda jj=jj, kb_=kb_: ten.matmul(
                    ps.t[:, jj * 128:(jj + 1) * 128], kth.t[:, kb_ * 128:(kb_ + 1) * 128],
                    QT.t[:, h, i * 128:(i + 1) * 128], start=True, stop=True), inc=(jj == n - 1))
            ptb = ptbr.next()
            A([ps], [ptb], lambda: act.activation(out=ptb.t[:, 0:n * 128], in_=ps.t[:, 0:n * 128], func=AF.Exp, scale=sc_att))
            ptm = ptmr.next()
            mm_i[0] += 1
            usev = (mm_i[0] % 2 == 1)
            e = vec if usev else pool
            fn = lambda: e.tensor_tensor(out=ptm.t[:, 0:n * 128], in0=ptb.t[:, 0:n * 128],
                                         in1=maskT.t[:, i, g0:g0 + n, :].rearrange("p k t -> p (k t)"), op=ALU.mult)
            (V if usev else G)([ptb, maskT.sub(i)], [ptm], fn)
            return ptm

        po_of, ptm_of, yab_of = {}, {}, {}
        ensure_loaded(0)
        for idx in range(len(items) + LA):
            if idx < len(items):
                h, i, g0, n, nkb = items[idx]
                if i == 0 and g0 == 0:
                    ensure_loaded(h)
                ptm_of[idx] = emit_S(h, i, g0, n)
            jx = idx - LA
            if jx < 0:
                continue
            h, i, g0, n, nkb = items[jx]
            vh = kv_of[h][1]
            if g0 == 0:
                po_of[(h, i)] = por.next()
                if i == 0:
                    yab_of[h] = yabr.next()
                    ensure_loaded(h + 1)
            po, ptm, yab = po_of[(h, i)], ptm_of.pop(jx), yab_of[h]
            for jj in range(n):
                kb_ = g0 + jj
                PE([ptm, vh], [po], lambda jj=jj, kb_=kb_: ten.matmul(
                    po.t[:, 0:129], ptm.t[:, jj * 128:(jj + 1) * 128], vh.t[:, kb_ * 129:(kb_ + 1) * 129],
                    start=(kb_ == 0), stop=(kb_ == nkb - 1)), inc=(kb_ == nkb - 1 or jj == n - 1))
            if g0 + n == nkb:
                rs = rsr.next()
                V([po], [rs], lambda: vec.reciprocal(out=rs.t[:, 0:1], in_=po.t[:, 128:129]))
                A([po, rs], [yab], lambda: act.activation(out=yab.t[:, i, :], in_=po.t[:, 0:128], func=AF.Copy,
                                                          scale=rs.t[:, 0:1]))
                if i == 7:
                    tp = tpr.next()
                    for i2 in range(8):
                        PE([yab, ident], [tp], lambda i2=i2: ten.transpose(
                            out=tp.t[:, i2 * 128:(i2 + 1) * 128], in_=yab.t[:, i2, :], identity=ident[:]), inc=(i2 == 7))
                    yat = yatr.next()
                    evac_copy(yat.t[:, :], tp.t[:, :], [tp], [yat])
                    k.dma(k.sp, yTs[:, 8 + h, :], yat.t[:, :], [yat], [YTS.sub(8 + h)], yat)
        k.barrier()
    es_att.close()

    if "pa" in dbg:
        d_ya = dbg_tensor("yaT", [128, 8, OWN], BF16)
        with contextlib.ExitStack() as es:
            t1 = k.sb("dbgya", [128, 8, OWN], BF16, es)
            k.dma(k.sp, t1.t[:, :, :], yTs[:, 8:16, :], [YTS.sub(8 + h) for h in range(8)], [t1], t1)
            k.dma(k.sp, d_ya[:, :, :], t1.t[:, :, :], [t1], [OUT.sub("ya")], t1)
            k.barrier()
        return nc, dbg_out

    if True:
        with contextlib.ExitStack() as es:
            W = walloc(es, "h")
            S = k.sb("S", [128, 4, 128], F32, es)
            xnTr = Rot([k.sb(f"xnTh{i}", [128, KC, 512], BF16, es) for i in range(2)])
            vbr = Rot([k.sb(f"vb{i}", [128, 4, 512], BF16, es) for i in range(2)])
            s_l = [k.sb(f"s_t{i}", [128, 512], F32, es) for i in range(4)]
            lf_l = [k.sb(f"lf_t{i}", [128, 512], F32, es) for i in range(4)]
            kk_l = [k.sb(f"kk_t{i}", [128, 512], F32, es) for i in range(4)]
            A_l = [k.sb(f"A_t{i}", [128, 512], F32, es) for i in range(4)]
            eNA_l = lf_l
            eA_l = [k.sb(f"eA{i}", [128, 512], F32, es) for i in range(4)]
            sq_l = [k.sb(f"sq_t{i}", [128, 512], F32, es) for i in range(4)]
            ktT_s = [[k.sb(f"ktT{q}{i}", [128, 512], BF16, es) for i in range(4)] for q in range(2)]
            qtT_s = [[k.sb(f"qtT{q}{i}", [128, 512], BF16, es) for i in range(4)] for q in range(2)]
            sm_s = [[k.sb(f"sm{q}{i}", [128, 24], F32, es) for i in range(4)] for q in range(2)]
            ktok_l = [k.sb(f"ktok{i}", [128, 4, 128], BF16, es) for i in range(4)]
            Smid_r = Rot([k.sb(f"Smid{i}", [128, 128], BF16, es) for i in range(2)])
            scm_r = Rot([k.sb(f"scm{i}", [128, 128], BF16, es) for i in range(2)])
            o_sb = k.sb("o_sb", [128, 4, 512], F32, es)
            ss_o = k.sb("ss_o", [128, 16], F32, es)
            tmp_o = k.sb("tmp_o", [128, 16], F32, es)
            junkh = k.sb("junkh", [128, 128], BF16, es)
            sg_r = Rot([k.sb(f"sg{i}", [128, 512], F32, es) for i in range(2)])
            yhb_r = Rot([k.sb(f"yhb{i}", [128, 512], BF16, es) for i in range(2)])
            yht = k.sb("yht", [128, 4, OWN], BF16, es)

            def load_first(hh):
                load_w(W[1], w_in, C_HI + hh * 512)
                load_w(W[0], w_in, C_HF + hh * 512)

            def load_rest(hh):
                load_w(W[2], w_in, C_HQ + hh * 512)
                load_w(W[3], w_in, C_HG + hh * 512)

            load_first(0)
            load_rest(0)
            for hl in range(4):
                V([], [S.sub(hl)], lambda hl=hl: vec.memset(S.t[:, hl, :], 0.0))

            blkctx = {}

            def stageA1(hh, blk):
                own = blk >= 6
                xnT = xnTr.next()
                k.dma(k.sp, xnT.t[:, :, :].rearrange("p c t -> p (c t)"), xnTs[blk], [XNTS.sub(blk)], [xnT], xnT)
                vb = vbr.next()
                blkctx[(hh, blk)] = (xnT, vb)
                for j in range(4):
                    ps = psr4.next()
                    for c in range(KC):
                        PE([xnT, W[1]], [ps], lambda c=c, ps=ps, j=j: ten.matmul(
                            ps.t[:, :], xnT.t[:, c, j * 128:(j + 1) * 128], W[1].t[:, c, :],
                            start=(c == 0), stop=(c == KC - 1)), inc=(c == KC - 1))
                    evac_copy(vb.t[:, j, :], ps.t[:, :], [ps], [vb])
                for hl in range(4):
                    ps = psr4.next()
                    for c in range(KC):
                        PE([xnT, W[0]], [ps], lambda c=c, ps=ps, hl=hl: ten.matmul(
                            ps.t[:, :], W[0].t[:, c, hl * 128:(hl + 1) * 128], xnT.t[:, c, :],
                            start=(c == 0), stop=(c == KC - 1)), inc=(c == KC - 1))
                    A([ps], [s_l[hl]], lambda ps=ps, hl=hl: act.activation(out=s_l[hl].t[:, :], in_=ps.t[:, :], func=AF.Sigmoid))
                if own:
                    for hl in range(4):
                        ps = psr4.next()
                        for c in range(KC):
                            PE([xnT, W[2]], [ps], lambda c=c, ps=ps, hl=hl: ten.matmul(
                                ps.t[:, :], W[2].t[:, c, hl * 128:(hl + 1) * 128], xnT.t[:, c, :],
                                start=(c == 0), stop=(c == KC - 1)), inc=(c == KC - 1))
                        A([ps], [sq_l[hl]], lambda ps=ps, hl=hl: act.activation(out=sq_l[hl].t[:, :], in_=ps.t[:, :], func=AF.Silu))

            def stageA2(hh, blk):
                own = blk >= 6
                q = blk % 2
                H4 = range(4)
                for hl in H4:
                    h = hh * 4 + hl
                    A([s_l[hl], oml, lbv], [lf_l[hl]], lambda hl=hl, h=h: act.activation(
                        out=lf_l[hl].t[:, :], in_=s_l[hl].t[:, :], func=AF.Ln, scale=oml.t[:, h:h + 1], bias=lbv.t[:, h:h + 1]))
                for hl in H4:
                    h = hh * 4 + hl
                    V([s_l[hl], oml, noml], [kk_l[hl]], lambda hl=hl, h=h: vec.tensor_scalar(
                        out=kk_l[hl].t[:, :], in0=s_l[hl].t[:, :], scalar1=noml.t[:, h:h + 1], scalar2=oml.t[:, h:h + 1],
                        op0=ALU.mult, op1=ALU.add))
                    V([mreset, lf_l[hl]], [A_l[hl]], lambda hl=hl: vec.tensor_tensor_scan(
                        out=A_l[hl].t[:, :], data0=mreset.t[:, :], data1=lf_l[hl].t[:, :], initial=0.0, op0=ALU.mult, op1=ALU.add))
                Avs = [A_l[hl].t[:, :].rearrange("p (j t) -> p j t", t=128) for hl in H4]
                for hl in H4:
                    sm, Av = sm_s[q][hl], Avs[hl]
                    V([A_l[hl]], [sm], lambda sm=sm, Av=Av: vec.tensor_scalar(out=sm.t[:, 0:4], in0=Av[:, :, 63], scalar1=-1.0,
                                                                              scalar2=None, op0=ALU.mult))
                    V([A_l[hl]], [sm], lambda sm=sm, Av=Av: vec.tensor_tensor(out=sm.t[:, 16:20], in0=Av[:, :, 127], in1=Av[:, :, 63],
                                                                              op=ALU.subtract))
                for hl in H4:
                    sm, Av = sm_s[q][hl], Avs[hl]
                    A([A_l[hl]], [sm], lambda sm=sm, Av=Av: act.activation(out=sm.t[:, 4:8], in_=Av[:, :, 127], func=AF.Exp))
                    A([sm], [sm], lambda sm=sm: act.activation(out=sm.t[:, 8:12], in_=sm.t[:, 16:20], func=AF.Exp))
                    if own:
                        A([A_l[hl]], [sm], lambda sm=sm, Av=Av: act.activation(out=sm.t[:, 12:16], in_=Av[:, :, 63], func=AF.Exp))
                for hl in H4:
                    for j in range(4):
                        A([A_l[hl]], [eNA_l[hl]], lambda j=j, hl=hl: act.activation(
                            out=eNA_l[hl].t[:, j * 128:(j + 1) * 128], in_=A_l[hl].t[:, j * 128:(j + 1) * 128], func=AF.Exp,
                            scale=-1.0, bias=A_l[hl].t[:, j * 128 + 63:j * 128 + 64]))
                    G([kk_l[hl], eNA_l[hl]], [ktT_s[q][hl]], lambda hl=hl: pool.tensor_tensor(
                        out=ktT_s[q][hl].t[:, :], in0=kk_l[hl].t[:, :], in1=eNA_l[hl].t[:, :], op=ALU.mult))
                if own:
                    for hl in H4:
                        sm = sm_s[q][hl]
                        for j in range(4):
                            A([A_l[hl], sm], [eA_l[hl]], lambda j=j, hl=hl, sm=sm: act.activation(
                                out=eA_l[hl].t[:, j * 128:(j + 1) * 128], in_=A_l[hl].t[:, j * 128:(j + 1) * 128], func=AF.Exp,
                                bias=sm.t[:, j:j + 1]))
                        V([sq_l[hl], eA_l[hl]], [qtT_s[q][hl]], lambda hl=hl: vec.scalar_tensor_tensor(
                            out=qtT_s[q][hl].t[:, :], in0=sq_l[hl].t[:, :], scalar=float(128 ** -0.5), in1=eA_l[hl].t[:, :],
                            op0=ALU.mult, op1=ALU.mult))

            def stageB(hh, blk):
                own = blk >= 6
                q = blk % 2
                xnT, vb = blkctx.pop((hh, blk))
                for hl in range(4):
                    ktT = ktT_s[q][hl]
                    tp = tpr.next()
                    for j in range(4):
                        PE([ktT, ident], [tp], lambda j=j, tp=tp, ktT=ktT: ten.transpose(
                            out=tp.t[:, j * 128:(j + 1) * 128], in_=ktT.t[:, j * 128:(j + 1) * 128], identity=ident[:]),
                           inc=(j == 3))
                    evac_copy(ktok_l[hl].t[:, :, :], tp.t[:, 0:512].rearrange("p (j q) -> p j q", j=4), [tp], [ktok_l[hl]])
                for j in range(4):
                    tsl = slice(j * 128, (j + 1) * 128)
                    for hl in range(4):
                        sm, ktT, qtT, ktok = sm_s[q][hl], ktT_s[q][hl], qtT_s[q][hl], ktok_l[hl]
                        vsl = slice(hl * 128, (hl + 1) * 128)
                        if own:
                            Smid, scm = Smid_r.next(), scm_r.next()
                            V([S.sub(hl), sm], [Smid], lambda j=j, Smid=Smid, sm=sm, hl=hl: vec.tensor_scalar(
                                out=Smid.t[:, :], in0=S.t[:, hl, :], scalar1=sm.t[:, 12 + j:13 + j], scalar2=None, op0=ALU.mult))
                            ps = psr4.next()
                            PE([ktT, qtT], [ps], lambda ps=ps, tsl=tsl, ktT=ktT, qtT=qtT: ten.matmul(
                                ps.t[:, 0:128], ktT.t[:, tsl], qtT.t[:, tsl], start=True, stop=True))
                            V([ps, triT], [scm], lambda ps=ps, scm=scm: vec.tensor_tensor(
                                out=scm.t[:, :], in0=ps.t[:, 0:128], in1=triT.t[:, :], op=ALU.mult))
                            po = por.next()
                            PE([scm, vb], [po], lambda po=po, scm=scm, j=j, vsl=vsl: ten.matmul(
                                po.t[:, 0:128], scm.t[:, :], vb.t[:, j, vsl], start=True, stop=False), inc=False)
                            PE([qtT, Smid], [po], lambda po=po, Smid=Smid, tsl=tsl, qtT=qtT: ten.matmul(
                                po.t[:, 0:128], qtT.t[:, tsl], Smid.t[:, :], start=False, stop=True))
                            A([po], [o_sb.sub(j)], lambda po=po, j=j, vsl=vsl: act.copy(out=o_sb.t[:, j, vsl], in_=po.t[:, 0:128]))
                            A([po], [junkh, ss_o], lambda po=po, j=j, hl=hl: act.activation(
                                out=junkh.t[:, :], in_=po.t[:, 0:128], func=AF.Square,
                                accum_out=ss_o.t[:, j * 4 + hl:j * 4 + hl + 1]))
                        pu = psr4.next()
                        PE([ktok, vb], [pu], lambda pu=pu, j=j, ktok=ktok, vsl=vsl: ten.matmul(
                            pu.t[:, 0:128], ktok.t[:, j, :], vb.t[:, j, vsl], start=True, stop=True))
                        V([S.sub(hl), sm], [S.sub(hl)], lambda j=j, sm=sm, hl=hl: vec.tensor_scalar(
                            out=S.t[:, hl, :], in0=S.t[:, hl, :], scalar1=sm.t[:, 4 + j:5 + j], scalar2=None, op0=ALU.mult))
                        V([pu, sm, S.sub(hl)], [S.sub(hl)], lambda pu=pu, j=j, sm=sm, hl=hl: vec.scalar_tensor_tensor(
                            out=S.t[:, hl, :], in0=pu.t[:, 0:128], scalar=sm.t[:, 8 + j:9 + j], in1=S.t[:, hl, :],
                            op0=ALU.mult, op1=ALU.add))
                if own:
                    V([ss_o], [tmp_o], lambda: vec.tensor_scalar(out=tmp_o.t[:, :], in0=ss_o.t[:, :], scalar1=1.0 / 128, scalar2=EPS,
                                                                 op0=ALU.mult, op1=ALU.add))
                    A([tmp_o], [tmp_o], lambda: act.activation(out=tmp_o.t[:, :], in_=tmp_o.t[:, :], func=AF.Sqrt))
                    V([tmp_o], [ss_o], lambda: vec.reciprocal(out=ss_o.t[:, :], in_=tmp_o.t[:, :]))
                    for j in range(4):
                        qt = (blk - 6) * 4 + j
                        sg, yhb = sg_r.next(), yhb_r.next()
                        ps = psr4.next()
                        for c in range(KC):
                            PE([xnT, W[3]], [ps], lambda c=c, ps=ps, j=j: ten.matmul(
                                ps.t[:, :], xnT.t[:, c, j * 128:(j + 1) * 128], W[3].t[:, c, :],
                                start=(c == 0), stop=(c == KC - 1)), inc=(c == KC - 1))
                        A([ps], [sg], lambda ps=ps, sg=sg: act.activation(out=sg.t[:, :], in_=ps.t[:, :], func=AF.Silu))
                        V([sg, on_b4], [sg], lambda sg=sg: vec.tensor_tensor(out=sg.t[:, :], in0=sg.t[:, :], in1=on_b4.t[:, :],
                                                                             op=ALU.mult))
                        for h2 in range(4):
                            v2 = slice(h2 * 128, (h2 + 1) * 128)
                            V([o_sb.sub(j), ss_o, sg], [yhb], lambda j=j, h2=h2, v2=v2, sg=sg, yhb=yhb: vec.scalar_tensor_tensor(
                                out=yhb.t[:, v2], in0=o_sb.t[:, j, v2], scalar=ss_o.t[:, j * 4 + h2:j * 4 + h2 + 1],
                                in1=sg.t[:, v2], op0=ALU.mult, op1=ALU.mult))
                        tp = tpr.next()
                        for h2 in range(4):
                            PE([yhb, ident], [tp], lambda h2=h2, tp=tp, yhb=yhb: ten.transpose(
                                out=tp.t[:, h2 * 128:(h2 + 1) * 128], in_=yhb.t[:, h2 * 128:(h2 + 1) * 128], identity=ident[:]),
                               inc=(h2 == 3))
                        evac_copy(yht.t[:, :, qt * 128:(qt + 1) * 128], tp.t[:, 0:512].rearrange("p (h t) -> p h t", h=4),
                                  [tp], [yht])

            seq = [(hh_, blk_) for hh_ in range(2) for blk_ in range(8)]
            stageA1(*seq[0])
            stageA2(*seq[0])
            for n_, (hh, blk) in enumerate(seq):
                nxt_ = seq[n_ + 1] if n_ + 1 < len(seq) else None
                if nxt_ is not None:
                    stageA1(*nxt_)
                    if nxt_[1] == 7 and nxt_[0] == 0:
                        load_first(1)
                stageB(hh, blk)
                if blk == 7:
                    k.dma(k.sp, yTs[:, hh * 4:(hh + 1) * 4, :], yht.t[:, :, :], [yht], [YTS.sub(f"h{hh}")], yht)
                    if nxt_ is not None:
                        load_rest(nxt_[0])
                        for hl in range(4):
                            V([], [S.sub(hl)], lambda hl=hl: vec.memset(S.t[:, hl, :], 0.0))
                if nxt_ is not None:
                    stageA2(*nxt_)
            k.barrier()

    if "ph" in dbg:
        d_yh = dbg_tensor("yhT", [128, 8, OWN], BF16)
        with contextlib.ExitStack() as es:
            t1 = k.sb("dbgyh", [128, 8, OWN], BF16, es)
            k.dma(k.sp, t1.t[:, :, :], yTs[:, 0:8, :], [YTS.sub("h0"), YTS.sub("h1")], [t1], t1)
            k.dma(k.sp, d_yh[:, :, :], t1.t[:, :, :], [t1], [OUT.sub("yh")], t1)
            k.barrier()
        return nc, dbg_out

    hres = k.sb("hres", [128, 8, D], F32)
    with contextlib.ExitStack() as es:
        W = walloc(es, "o")
        yT = k.sb("yT", [128, KC, OWN], BF16, es)
        xrr = Rot([k.sb(f"xres{i}", [128, D], F32, es) for i in range(2)])
        for n in range(4):
            load_w(W[n], w_out, n * 512)
        k.dma(k.sp, yT.t[:, :, :], yTs[:, :, :], [YTS.sub(x_) for x_ in ["h0", "h1"] + [8 + h for h in range(8)]], [yT], yT)
        for qt in range(8):
            xb = xrr.next()
            k.dma(k.sp, xb.t[:, :], xs[(NT - 8 + qt) * 128:(NT - 7 + qt) * 128, :], [], [xb], xb)
            for n in range(4):
                ps = psr.next()
                for c in range(KC):
                    PE([yT, W[n]], [ps], lambda c=c, ps=ps, n=n, qt=qt: ten.matmul(
                        ps.t[:, :], yT.t[:, c, qt * 128:(qt + 1) * 128], W[n].t[:, c, :],
                        start=(c == 0), stop=(c == KC - 1)), inc=(c == KC - 1))
                V([ps, xb], [hres.sub(qt)], lambda ps=ps, xb=xb, n=n, qt=qt: vec.tensor_tensor(
                    out=hres.t[:, qt, n * 512:(n + 1) * 512], in0=ps.t[:, :], in1=xb.t[:, n * 512:(n + 1) * 512], op=ALU.add))
        k.barrier()

    with contextlib.ExitStack() as es:
        W = walloc(es, "x")
        Wo4 = W[3].t[:, :, :].rearrange("p c n -> p (c n)").rearrange("p (c n) -> p c n", c=4)
        gc_b = k.sb("gc_b", [128, D], F32, es)
        gm_b = gc_b
        gxq_b = k.sb("gxq_b", [128, 128], F32, es)
        gxk_b = k.sb("gxk_b", [128, 128], F32, es)
        hnTr = Rot([k.sb(f"hnTx{i}", [128, KC, 128], BF16, es) for i in range(2)])
        memT = k.sb("memT", [128, KC, 256], BF16, es)
        kTx = k.sb("kTx", [128, 4, 256], BF16, es)
        vaug = k.sb("vaug", [128, 2, 4, 129], BF16, es)
        xnbr = Rot([k.sb(f"xnbx{i}", [128, D], BF16, es) for i in range(2)])
        mst = Rot([k.sb(f"mst{i}", [128, D], F32, es) for i in range(1)])
        junk = k.sb("junkx", [128, 128], BF16, es)
        ssr = Rot([k.sb(f"ssx{i}", [128, 8], F32, es) for i in range(4)])
        tmpr = Rot([k.sb(f"tmpx{i}", [128, 8], F32, es) for i in range(4)])
        knbr = Rot([k.sb(f"knbx{i}", [128, 4, 128], BF16, es) for i in range(2)])
        ptxr = Rot([k.sb(f"ptx{i}", [128, 256], BF16, es) for i in range(3)])
        obr = Rot([k.sb(f"obx{i}", [128, 512], BF16, es) for i in range(3)])
        rsr = Rot([k.sb(f"rsx{i}", [128, 1], F32, es) for i in range(4)])
        load_w(W[0], wq_x, 0)
        load_w(W[1], wk_x, 0)
        load_w(W[2], wv_x, 0)
        for c4 in range(4):
            k.dma(k.pool, Wo4[:, c4, :], wo_x[c4 * 128:(c4 + 1) * 128, :], [], [W[3]], W[3], max_dma_last_dim=4096)
        bcast_row(gm_b, mem_norm[0:1, :], D)
        bcast_row(gxq_b, xqn[0:1, :], 128)
        bcast_row(gxk_b, xkn[0:1, :], 128)
        V([], [vaug], lambda: vec.memset(vaug.t[:, :, :, 128:129], 1.0))
        for mt in range(2):
            mb = mst.next()
            k.dma(k.sp, mb.t[:, :], memb[mt * 128:(mt + 1) * 128, :], [], [mb], mb)
            norm_tile_to_T(mb.t[:, :], mb, gm_b, xnbr.next(), memT, mt * 128, ssr.next(), tmpr.next(), junk)
        for mt in range(2):
            headnorm_T(lambda c, mt=mt: memT.t[:, c, mt * 128:(mt + 1) * 128], memT, [W[1]], gxk_b, kTx, mt * 128,
                       ssr, tmpr, knbr, junk)
            ps = psr.next()
            for c in range(KC):
                PE([memT, W[2]], [ps], lambda c=c, ps=ps, mt=mt: ten.matmul(
                    ps.t[:, :], memT.t[:, c, mt * 128:(mt + 1) * 128], W[2].t[:, c, :],
                    start=(c == 0), stop=(c == KC - 1)), inc=(c == KC - 1))
            evac_copy(vaug.t[:, mt, :, 0:128], ps.t[:, :].rearrange("p (h d) -> p h d", h=4), [ps], [vaug])
        sc_att = float(128 ** -0.5)
        bcast_row(gc_b, norm_cross[0:1, :], D)
        qTxa = k.sb("qTxa", [128, 4, OWN], BF16, es)
        oTxa = k.sb("oTxa", [128, 4, OWN], BF16, es)
        xnbr = Rot(xnbr.bufs + [k.sb("xnbx2", [128, D], BF16, es)])
        knbr = Rot(knbr.bufs + [k.sb("knbx2", [128, 4, 128], BF16, es)])

        def x_part1(qt):
            xnb = xnbr.next()
            norm_part1(hres.t[:, qt, :], hres.sub(qt), gc_b, xnb, ssr.next(), tmpr.next(), junk)
            return xnb

        def x_T(qt, xnb):
            hnT = hnTr.next()
            norm_part2(xnb, hnT, 0)
            return hnT

        xnb_q = {0: x_part1(0), 1: x_part1(1)}
        hn_q = {0: x_T(0, xnb_q.pop(0))}
        qb_prev = None
        for qt in range(8):
            if qt + 2 < 8:
                xnb_q[qt + 2] = x_part1(qt + 2)
            if qt + 1 < 8:
                hn_q[qt + 1] = x_T(qt + 1, xnb_q.pop(qt + 1))
            hnT = hn_q.pop(qt)
            qb = headnorm_T(lambda c, hnT=hnT: hnT.t[:, c, :], hnT, [W[0]], gxq_b, qTxa, qt * 128,
                            ssr, tmpr, knbr, junk, defer=True)
            if qb_prev is not None:
                qb_prev()
            qb_prev = qb
        qb_prev()

        xitems = [(qt, h) for qt in range(8) for h in range(4)]
        ptxr = Rot(ptxr.bufs + [k.sb(f"ptxx{i}", [128, 256], BF16, es) for i in range(2)])
        pt_of, ob_of = {}, {}
        LAX = 2
        for idx in range(len(xitems) + LAX):
            if idx < len(xitems):
                qt, h = xitems[idx]
                ps = psr4.next()
                for mt in range(2):
                    PE([kTx, qTxa], [ps], lambda ps=ps, mt=mt, h=h, qt=qt: ten.matmul(
                        ps.t[:, mt * 128:(mt + 1) * 128], kTx.t[:, h, mt * 128:(mt + 1) * 128],
                        qTxa.t[:, h, qt * 128:(qt + 1) * 128], start=True, stop=True), inc=(mt == 1))
                pt = ptxr.next()
                A([ps], [pt], lambda ps=ps, pt=pt: act.activation(out=pt.t[:, :], in_=ps.t[:, 0:256], func=AF.Exp, scale=sc_att))
                pt_of[idx] = pt
            jx = idx - LAX
            if jx < 0:
                continue
            qt, h = xitems[jx]
            pt = pt_of.pop(jx)
            if h == 0:
                ob_of[qt] = obr.next()
            ob = ob_of[qt]
            po = por.next()
            for mt in range(2):
                PE([pt, vaug], [po], lambda po=po, pt=pt, mt=mt, h=h: ten.matmul(
                    po.t[:, 0:129], pt.t[:, mt * 128:(mt + 1) * 128], vaug.t[:, mt, h, :],
                    start=(mt == 0), stop=(mt == 1)), inc=(mt == 1))
            rs = rsr.next()
            V([po], [rs], lambda rs=rs, po=po: vec.reciprocal(out=rs.t[:, 0:1], in_=po.t[:, 128:129]))
            A([po, rs], [ob], lambda rs=rs, po=po, h=h, ob=ob: act.activation(
                out=ob.t[:, h * 128:(h + 1) * 128], in_=po.t[:, 0:128], func=AF.Copy, scale=rs.t[:, 0:1]))
            if h == 3:
                tp = tpr.next()
                for h2 in range(4):
                    PE([ob, ident], [tp], lambda h2=h2, tp=tp, ob=ob: ten.transpose(
                        out=tp.t[:, h2 * 128:(h2 + 1) * 128], in_=ob.t[:, h2 * 128:(h2 + 1) * 128], identity=ident[:]), inc=(h2 == 3))
                evac_copy(oTxa.t[:, :, qt * 128:(qt + 1) * 128], tp.t[:, 0:512].rearrange("p (h t) -> p h t", h=4), [tp],
                          [oTxa.sub(qt)])

        for qt in range(8):
            for n in range(4):
                ps = psr4.next()
                for c in range(4):
                    PE([oTxa.sub(qt), W[3]], [ps], lambda c=c, ps=ps, n=n, qt=qt: ten.matmul(
                        ps.t[:, :], oTxa.t[:, c, qt * 128:(qt + 1) * 128], Wo4[:, c, n * 512:(n + 1) * 512],
                        start=(c == 0), stop=(c == 3)), inc=(c == 3))
                V([ps, hres.sub(qt)], [hres.sub(qt)], lambda ps=ps, n=n, qt=qt: vec.tensor_tensor(
                    out=hres.t[:, qt, n * 512:(n + 1) * 512], in0=ps.t[:, :], in1=hres.t[:, qt, n * 512:(n + 1) * 512], op=ALU.add))
        k.barrier()

    with contextlib.ExitStack() as es:
        gm_b = k.sb("gmlp_b", [128, D], F32, es)
        hnT = k.sb("hnTm", [128, KC, OWN], BF16, es)
        xnbr = Rot([k.sb(f"xnbm{i}", [128, D], BF16, es) for i in range(2)])
        junk = k.sb("junkm", [128, 128], BF16, es)
        ssr = Rot([k.sb(f"ssm{i}", [128, 8], F32, es) for i in range(4)])
        tmpr = Rot([k.sb(f"tmpm{i}", [128, 8], F32, es) for i in range(4)])
        Wur = Rot([k.sb(f"Wu{i}", [128, KC, 512], BF16, es) for i in range(2)])
        Wdr = Rot([k.sb(f"Wd{i}", [128, 4, D], BF16, es) for i in range(2)])
        actr = Rot([k.sb(f"actT{i}", [128, 4, OWN], BF16, es) for i in range(2)])
        rlr = Rot([k.sb(f"rl{i}", [128, 512], F32, es) for i in range(3)])
        bcast_row(gm_b, norm_mlp[0:1, :], D)
        NG = 16

        def load_group(g):
            wu, wd = Wur.next(), Wdr.next()
            load_w(wu, w_up, g * 512)
            for c4 in range(4):
                k.dma(k.pool, wd.t[:, c4, :], w_down[g * 512 + c4 * 128:g * 512 + (c4 + 1) * 128, :],
                      [], [wd], wd, max_dma_last_dim=4096)
            return wu, wd

        nxt = load_group(0)
        for qt in range(8):
            norm_tile_to_T(hres.t[:, qt, :], hres.sub(qt), gm_b, xnbr.next(), hnT, qt * 128, ssr.next(), tmpr.next(), junk)
        sq_i = [0]
        for g in range(NG):
            wu, wd = nxt
            if g + 1 < NG:
                nxt = load_group(g + 1)
            actT = actr.next()
            for f in range(4):
                for th in range(2):
                    ps = psr.next()
                    for c in range(KC):
                        PE([hnT, wu], [ps], lambda c=c, ps=ps, f=f, th=th, wu=wu: ten.matmul(
                            ps.t[:, :], wu.t[:, c, f * 128:(f + 1) * 128], hnT.t[:, c, th * 512:(th + 1) * 512],
                            start=(c == 0), stop=(c == KC - 1)), inc=(c == KC - 1))
                    rl = rlr.next()
                    A([ps], [rl], lambda ps=ps, rl=rl: act.activation(out=rl.t[:, :], in_=ps.t[:, :], func=AF.Relu))
                    sq_i[0] += 1
                    if sq_i[0] % 2:
                        V([rl], [actT], lambda rl=rl, actT=actT, f=f, th=th: vec.tensor_tensor(
                            out=actT.t[:, f, th * 512:(th + 1) * 512], in0=rl.t[:, :], in1=rl.t[:, :], op=ALU.mult))
                    else:
                        G([rl], [actT], lambda rl=rl, actT=actT, f=f, th=th: pool.tensor_tensor(
                            out=actT.t[:, f, th * 512:(th + 1) * 512], in0=rl.t[:, :], in1=rl.t[:, :], op=ALU.mult))
            for qt in range(8):
                for n in range(4):
                    ps = psr.next()
                    for f in range(4):
                        PE([actT, wd], [ps], lambda f=f, ps=ps, n=n, qt=qt, wd=wd, actT=actT: ten.matmul(
                            ps.t[:, :], actT.t[:, f, qt * 128:(qt + 1) * 128], wd.t[:, f, n * 512:(n + 1) * 512],
                            start=(f == 0), stop=(f == 3)), inc=(f == 3))
                    V([ps, hres.sub(qt)], [hres.sub(qt)], lambda ps=ps, n=n, qt=qt: vec.tensor_tensor(
                        out=hres.t[:, qt, n * 512:(n + 1) * 512], in0=ps.t[:, :], in1=hres.t[:, qt, n * 512:(n + 1) * 512], op=ALU.add))
        for qt in range(8):
            k.dma(k.sp, out[qt * 128:(qt + 1) * 128, :], hres.t[:, qt, :], [hres.sub(qt)], [OUT.sub(qt)], hres.sub(qt))
        k.barrier()
    return nc, dbg_out


def make_in_maps(inputs):
    x = np.asarray(inputs["x"], dtype=np.float32)
    mem = np.asarray(inputs["mem"], dtype=np.float32)
    shared = {
        "norm_mix": inputs["norm_mix"][0:1], "w_in": inputs["w_in"][0], "lbl": inputs["hgrn_lb_logits"],
        "onorm": inputs["hgrn_onorm"][0:1], "qnorm": inputs["attn_qnorm"][0:1], "knorm": inputs["attn_knorm"][0:1],
        "w_out": inputs["w_out"][0], "norm_cross": inputs["norm_cross"][0:1], "mem_norm": inputs["mem_norm"][0:1],
        "wq_x": inputs["wq_x"][0], "wk_x": inputs["wk_x"][0], "wv_x": inputs["wv_x"][0], "wo_x": inputs["wo_x"][0],
        "xqn": inputs["xq_norm"][0:1], "xkn": inputs["xk_norm"][0:1], "norm_mlp": inputs["norm_mlp"][0:1],
        "w_up": inputs["w_up"][0], "w_down": inputs["w_down"][0],
    }
    shared = {n: np.ascontiguousarray(np.asarray(v, dtype=np.float32)) for n, v in shared.items()}
    maps = []
    for c in range(8):
        b, q = c // 4, c % 4
        npad = (3 - q) * OWN
        xw = np.zeros((T, D), np.float32)
        xw[npad:] = x[b, :(q + 1) * OWN]
        kb = np.zeros((1, T - OWN), np.float32)
        kb[0, :npad] = NEG
        m = dict(shared)
        m["xs"] = xw
        m["keybias"] = kb
        m["mem"] = np.ascontiguousarray(mem[b])
        maps.append(m)
    return maps


def kernel(**inputs):
    nc, _ = build()
    res = run_bass_kernel_spmd(nc, make_in_maps(inputs), core_ids=list(range(8)))
    outp = np.zeros((2, 4096, D), np.float32)
    for c in range(8):
        b, q = c // 4, c % 4
        outp[b, q * OWN:(q + 1) * OWN] = res.results[c]["out"]
    return outp
```
